# Optimizing a Trainium2 kernel written in Bass

```python
import jax, jax.numpy as jnp
from jax import lax
import numpy as np

D_MODEL = 1024
BATCH = 8
SEQ = 4096
DEPTH = 2

N_MEM = 256
MIX_WIDTH = 2 * D_MODEL
XA_HEADS = 4
XA_WIDTH = MIX_WIDTH // 4
XA_HEAD_DIM = XA_WIDTH // XA_HEADS
BRANCH_WIDTH = (MIX_WIDTH - XA_WIDTH) // 2
CHUNK = 128
A_HEADS = 4
A_HEAD_DIM = BRANCH_WIDTH // A_HEADS
SHORT_CONV = 3
POOL_WINDOWS = (2, 4, 8, 16)
C_GROUP = BRANCH_WIDTH // len(POOL_WINDOWS)
CONF_CONV = 31
EPS = 1e-6

N_EVEN = (DEPTH + 1) // 2
N_ODD = DEPTH // 2
EVEN_SPLITS = (BRANCH_WIDTH, BRANCH_WIDTH, BRANCH_WIDTH, BRANCH_WIDTH, BRANCH_WIDTH, XA_WIDTH, MIX_WIDTH)
ODD_SPLITS = (BRANCH_WIDTH, BRANCH_WIDTH, BRANCH_WIDTH, XA_WIDTH, MIX_WIDTH)
EVEN_IN = sum(EVEN_SPLITS)
ODD_IN = sum(ODD_SPLITS)

kernel_name = "hybrid_sgu_shortconv_pool_conformer_memxattn"


def _split(p, sizes):
    idx = [int(v) for v in np.cumsum(sizes)[:-1]]
    return jnp.split(p, idx, axis=-1)


def rms_norm(x, g):
    xf = x.astype(jnp.float32)
    y = xf * lax.rsqrt(jnp.mean(xf * xf, axis=-1, keepdims=True) + EPS)
    return (y * g.astype(jnp.float32)).astype(x.dtype)


def layer_norm(x, g, b):
    xf = x.astype(jnp.float32)
    mu = jnp.mean(xf, axis=-1, keepdims=True)
    var = jnp.mean(jnp.square(xf - mu), axis=-1, keepdims=True)
    y = (xf - mu) * lax.rsqrt(var + EPS)
    return (y * g.astype(jnp.float32) + b.astype(jnp.float32)).astype(x.dtype)


def causal_dwconv(x, w):
    k = w.shape[0]
    return lax.conv_general_dilated(
        x, w[:, None, :].astype(x.dtype), window_strides=(1,), padding=[(k - 1, 0)],
        dimension_numbers=("NWC", "WIO", "NWC"), feature_group_count=x.shape[-1])


def memory_cross_attention(q, mem_n, w_kv):
    bsz, s, _ = q.shape
    k, v = _split(jnp.einsum("bmd,de->bme", mem_n, w_kv), (XA_WIDTH, XA_WIDTH))
    q = q.reshape(bsz, s, XA_HEADS, XA_HEAD_DIM)
    k = k.reshape(bsz, -1, XA_HEADS, XA_HEAD_DIM)
    v = v.reshape(bsz, -1, XA_HEADS, XA_HEAD_DIM)
    scores = jnp.einsum("bshd,bmhd->bhsm", q, k).astype(jnp.float32) * (XA_HEAD_DIM ** -0.5)
    probs = jax.nn.softmax(scores, axis=-1).astype(v.dtype)
    return jnp.einsum("bhsm,bmhd->bshd", probs, v).reshape(bsz, s, XA_WIDTH)


def spatial_gating(u, v, ln_g, ln_b, w_s, b_s):
    bsz, s, _ = v.shape
    v = layer_norm(v, ln_g, ln_b).reshape(bsz, s // CHUNK, CHUNK, A_HEADS, A_HEAD_DIM)
    causal = jnp.tril(jnp.ones((CHUNK, CHUNK), dtype=bool))
    w = jnp.where(causal[None], w_s, 0.0).astype(v.dtype)
    sg = jnp.einsum("hts,bnshd->bnthd", w, v) + b_s.T[None, None, :, :, None]
    return u * sg.reshape(bsz, s, BRANCH_WIDTH)


def short_gated_conv(bg, cg, xin, w_conv):
    return bg * causal_dwconv(cg * xin, w_conv)


def multiscale_pool(z, w_grp, scale):
    s = z.shape[1]
    zf = z.astype(jnp.float32)
    csum = jnp.pad(jnp.cumsum(zf, axis=1), ((0, 0), (1, 0), (0, 0)))
    t = jnp.arange(1, s + 1)
    outs = []
    for g, win in enumerate(POOL_WINDOWS):
        sl = slice(g * C_GROUP, (g + 1) * C_GROUP)
        c = csum[..., sl]
        lo = jnp.pad(c, ((0, 0), (win, 0), (0, 0)))[:, : s + 1]
        cnt = jnp.minimum(t, win).astype(jnp.float32)[None, :, None]
        pooled = (c[:, 1:] - lo[:, 1:]) / cnt - zf[..., sl]
        outs.append(jnp.einsum("bsc,cd->bsd", pooled.astype(z.dtype), w_grp[g]))
    return jnp.concatenate(outs, axis=-1) * scale


def conformer_conv(a, b, w_dw, b_dw, ln_g, ln_b, w_pw, b_pw):
    z = a * jax.nn.sigmoid(b)
    z = causal_dwconv(z, w_dw) + b_dw
    z = jax.nn.silu(layer_norm(z, ln_g, ln_b))
    return jnp.einsum("bsc,cd->bsd", z, w_pw) + b_pw


def even_layer(x, mem, pre_g, w_in, a_ln_g, a_ln_b, a_ws, a_bs, b_conv, mem_g, w_kv, w_out, post_g):
    h = rms_norm(x, pre_g)
    p = jnp.einsum("bsd,de->bse", h, w_in)
    u, v, bg, cg, xin, q, gate = _split(p, EVEN_SPLITS)
    ya = spatial_gating(u, v, a_ln_g, a_ln_b, a_ws, a_bs)
    yb = short_gated_conv(bg, cg, xin, b_conv)
    yx = memory_cross_attention(q, rms_norm(mem, mem_g), w_kv)
    y = jnp.concatenate([ya, yb, yx], axis=-1) * jax.nn.silu(gate)
    return x + rms_norm(jnp.einsum("bse,ed->bsd", y, w_out), post_g)


def odd_layer(x, mem, pre_g, w_in, c_wgrp, c_scale, d_dw_w, d_dw_b, d_ln_g, d_ln_b, d_pw_w, d_pw_b,
              mem_g, w_kv, w_out, post_g):
    h = rms_norm(x, pre_g)
    p = jnp.einsum("bsd,de->bse", h, w_in)
    zc, ga, gb, q, gate = _split(p, ODD_SPLITS)
    yc = multiscale_pool(zc, c_wgrp, c_scale)
    yd = conformer_conv(ga, gb, d_dw_w, d_dw_b, d_ln_g, d_ln_b, d_pw_w, d_pw_b)
    yx = memory_cross_attention(q, rms_norm(mem, mem_g), w_kv)
    y = jnp.concatenate([yc, yd, yx], axis=-1) * jax.nn.silu(gate)
    return x + rms_norm(jnp.einsum("bse,ed->bsd", y, w_out), post_g)


def setup_inputs(seed: int = 0) -> dict:
    key = jax.random.key(seed)
    ks = iter(jax.random.split(key, 40))
    f32 = jnp.float32

    def nrm(shape, scale):
        return jax.random.normal(next(ks), shape, f32) * scale

    def gain(shape):
        return 1.0 + 0.05 * jax.random.normal(next(ks), shape, f32)

    ne, no = N_EVEN, N_ODD
    bw = BRANCH_WIDTH
    return {
        "x": jax.random.normal(next(ks), (BATCH, SEQ, D_MODEL), f32),
        "mem": jax.random.normal(next(ks), (BATCH, N_MEM, D_MODEL), f32),
        "even_pre_g": gain((ne, D_MODEL)),
        "even_w_in": nrm((ne, D_MODEL, EVEN_IN), D_MODEL ** -0.5),
        "even_a_ln_g": gain((ne, bw)),
        "even_a_ln_b": nrm((ne, bw), 0.02),
        "even_a_ws": nrm((ne, A_HEADS, CHUNK, CHUNK), CHUNK ** -0.5),
        "even_a_bs": nrm((ne, A_HEADS, CHUNK), 0.02),
        "even_b_conv": nrm((ne, SHORT_CONV, bw), SHORT_CONV ** -0.5),
        "even_mem_g": gain((ne, D_MODEL)),
        "even_w_kv": nrm((ne, D_MODEL, 2 * XA_WIDTH), D_MODEL ** -0.5),
        "even_w_out": nrm((ne, MIX_WIDTH, D_MODEL), MIX_WIDTH ** -0.5),
        "even_post_g": gain((ne, D_MODEL)),
        "odd_pre_g": gain((no, D_MODEL)),
        "odd_w_in": nrm((no, D_MODEL, ODD_IN), D_MODEL ** -0.5),
        "odd_c_wgrp": nrm((no, len(POOL_WINDOWS), C_GROUP, C_GROUP), C_GROUP ** -0.5),
        "odd_c_scale": gain((no, bw)),
        "odd_d_dw_w": nrm((no, CONF_CONV, bw), CONF_CONV ** -0.5),
        "odd_d_dw_b": nrm((no, bw), 0.02),
        "odd_d_ln_g": gain((no, bw)),
        "odd_d_ln_b": nrm((no, bw), 0.02),
        "odd_d_pw_w": nrm((no, bw, bw), bw ** -0.5),
        "odd_d_pw_b": nrm((no, bw), 0.02),
        "odd_mem_g": gain((no, D_MODEL)),
        "odd_w_kv": nrm((no, D_MODEL, 2 * XA_WIDTH), D_MODEL ** -0.5),
        "odd_w_out": nrm((no, MIX_WIDTH, D_MODEL), MIX_WIDTH ** -0.5),
        "odd_post_g": gain((no, D_MODEL)),
    }


def reference(x, mem,
              even_pre_g, even_w_in, even_a_ln_g, even_a_ln_b, even_a_ws, even_a_bs, even_b_conv,
              even_mem_g, even_w_kv, even_w_out, even_post_g,
              odd_pre_g, odd_w_in, odd_c_wgrp, odd_c_scale, odd_d_dw_w, odd_d_dw_b, odd_d_ln_g,
              odd_d_ln_b, odd_d_pw_w, odd_d_pw_b, odd_mem_g, odd_w_kv, odd_w_out, odd_post_g):
    for layer in range(DEPTH):
        i = layer // 2
        if layer % 2 == 0:
            x = even_layer(x, mem, even_pre_g[i], even_w_in[i], even_a_ln_g[i], even_a_ln_b[i],
                           even_a_ws[i], even_a_bs[i], even_b_conv[i], even_mem_g[i], even_w_kv[i],
                           even_w_out[i], even_post_g[i])
        else:
            x = odd_layer(x, mem, odd_pre_g[i], odd_w_in[i], odd_c_wgrp[i], odd_c_scale[i],
                          odd_d_dw_w[i], odd_d_dw_b[i], odd_d_ln_g[i], odd_d_ln_b[i], odd_d_pw_w[i],
                          odd_d_pw_b[i], odd_mem_g[i], odd_w_kv[i], odd_w_out[i], odd_post_g[i])
    return x
```

```python
import contextlib
import numpy as np
import concourse.bass as bass
import concourse.mybir as mybir
from concourse.bass_utils import run_bass_kernel_spmd

F32 = mybir.dt.float32
BF16 = mybir.dt.bfloat16
AF = mybir.ActivationFunctionType
ALU = mybir.AluOpType
AX = mybir.AxisListType

D = 1024
SEQ = 4096
NMEM = 256
T = 512
BW = 768
EVEN_IN = 6400
ODD_IN = 4864
EPS = 1e-6
RING = 13
SCALE = 128 ** -0.5
POOLW = (2, 4, 8, 16)


class Res:
    __slots__ = ("w", "r", "name")

    def __init__(self, name=""):
        self.w = None
        self.r = {}
        self.name = name


class Sched:
    ENG = ("pe", "act", "dve", "pool", "sp")

    def __init__(self, nc, es, n_dma_sems=16):
        self.nc = nc
        self.sem = {e: es.enter_context(nc.semaphore("s_" + e)) for e in self.ENG}
        self.cnt = {e: 0 for e in self.ENG}
        self.dsem = [es.enter_context(nc.semaphore("d%d" % i)) for i in range(n_dma_sems)]
        self.dcnt = [0] * n_dma_sems
        self.dnext = 0
        self.prog = {e: [] for e in self.ENG}
        self.seen = {e: {} for e in self.ENG}

    def _semobj(self, key):
        return self.sem[key] if isinstance(key, str) else self.dsem[key]

    def _need(self, eng, ev, waits):
        key, val = ev
        if self.seen[eng].get(key, 0) >= val:
            return
        if waits.get(key, 0) < val:
            waits[key] = val

    def _deps(self, eng, reads, writes):
        waits = {}
        for r in reads:
            if r.w is not None and (r.w[0] != eng or eng in ("act", "dve", "pool")):
                self._need(eng, r.w, waits)
        strict = eng in ("act", "dve", "pool")
        for w in writes:
            if w.w is not None and (w.w[0] != eng or strict):
                self._need(eng, w.w, waits)
            for k, v in w.r.items():
                if k != eng or strict:
                    self._need(eng, (k, v), waits)
        for k, v in waits.items():
            self.seen[eng][k] = v
        return waits

    def op(self, eng, fn, reads=(), writes=()):
        waits = self._deps(eng, reads, writes)
        self.cnt[eng] += 1
        val = self.cnt[eng]
        self.prog[eng].append((list(waits.items()), fn, self.sem[eng], 1))
        for r in reads:
            r.r[eng] = val
        for w in writes:
            w.w = (eng, val)
            w.r = {}

    def dma(self, out_ap, in_ap, reads=(), writes=(), q="sp", **kw):
        k = self.dnext
        self.dnext = (k + 1) % len(self.dsem)
        waits = self._deps(q, reads, writes)
        prev = self.dcnt[k] * 16
        if prev and self.seen[q].get(k, 0) < prev:
            if waits.get(k, 0) < prev:
                waits[k] = prev
            self.seen[q][k] = prev
        self.dcnt[k] += 1
        val = self.dcnt[k] * 16
        self.prog[q].append((list(waits.items()),
                             (lambda e: e.dma_start(out=out_ap, in_=in_ap, **kw)),
                             self.dsem[k], 16))
        for r in reads:
            r.r[k] = val
        for w in writes:
            w.w = (k, val)
            w.r = {}

    def dma_barrier(self, q="sp"):
        waits = []
        for k in range(len(self.dsem)):
            v = self.dcnt[k] * 16
            if v and self.seen[q].get(k, 0) < v:
                waits.append((k, v))
                self.seen[q][k] = v
        self.prog[q].append((waits, None, None, 0))

    def check(self):
        val = {}
        pc = {e: 0 for e in self.ENG}
        progress = True
        while progress:
            progress = False
            for e in self.ENG:
                while pc[e] < len(self.prog[e]):
                    waits, fn, sem, inc = self.prog[e][pc[e]]
                    if any(val.get(k, 0) < v for k, v in waits):
                        break
                    if fn is not None:
                        key = [k for k in list(self.sem) if self.sem[k] is sem]
                        key = key[0] if key else self.dsem.index(sem)
                        val[key] = val.get(key, 0) + inc
                    pc[e] += 1
                    progress = True
        stuck = {e: (pc[e], len(self.prog[e])) for e in self.ENG if pc[e] < len(self.prog[e])}
        if stuck:
            msg = []
            for e, (p, n) in stuck.items():
                waits = self.prog[e][p][0]
                msg.append("%s stuck at %d/%d waiting %s have %s" % (e, p, n, waits, [(k, val.get(k, 0)) for k, _ in waits]))
            raise RuntimeError("DEADLOCK: " + " | ".join(msg))

    def emit(self):
        nc = self.nc
        self.dma_barrier("sp")
        self.check()
        with nc.Block() as block:
            def mk(name):
                def body(e):
                    for waits, fn, sem, inc in self.prog[name]:
                        for key, val in waits:
                            e.wait_ge(self._semobj(key), val)
                        if fn is not None:
                            ins = fn(e)
                            ins.then_inc(sem, inc)
                return body
            block.tensor(mk("pe"))
            block.scalar(mk("act"))
            block.vector(mk("dve"))
            block.gpsimd(mk("pool"))
            block.sync(mk("sp"))


def hnd(t):
    return t.tensor if hasattr(t, "tensor") else t


def _head_segs(c):
    lo, hi = c * 128, c * 128 + 128
    out = []
    for h in range(4):
        a, b = max(lo, h * 192), min(hi, h * 192 + 192)
        if a < b:
            out.append((a - lo, b - lo, h))
    return out


def make_consts():
    c = np.zeros((128, 128 + 128 + 96), np.float32)
    c[:, 0:128] = np.eye(128, dtype=np.float32)
    t = np.arange(128)
    c[:, 128:256] = (t[None, :] <= t[:, None]).astype(np.float32)
    for ch in range(6):
        for p in range(128):
            win = POOLW[(ch * 128 + p) // 192]
            for tt in range(16):
                c[p, 256 + ch * 16 + tt] = win / min(tt + 1, win)
    return c


def build(nt=SEQ // T, nlayers=2):
    seq = nt * T
    nc = bass.Bass("TRN2", target_bir_lowering=False)

    def din(name, shape):
        return nc.dram_tensor(name, list(shape), F32, kind="ExternalInput").ap()

    x_d = din("x", [seq, D])
    mem_d = din("mem", [NMEM, D])
    cst_d = din("consts", [128, 352])
    e_pre_g = din("even_pre_g", [1, D]); e_w_in = din("even_w_in", [1, D, EVEN_IN])
    e_ln_g = din("even_a_ln_g", [1, BW]); e_ln_b = din("even_a_ln_b", [1, BW])
    e_ws = din("even_a_ws", [1, 4, 128, 128]); e_bs = din("even_a_bs", [1, 4, 128])
    e_bconv = din("even_b_conv", [1, 3, BW]); e_mem_g = din("even_mem_g", [1, D])
    e_w_kv = din("even_w_kv", [1, D, D]); e_w_out = din("even_w_out", [1, 2 * D, D]); e_post_g = din("even_post_g", [1, D])
    o_pre_g = din("odd_pre_g", [1, D]); o_w_in = din("odd_w_in", [1, D, ODD_IN])
    o_wgrp = din("odd_c_wgrp", [1, 4, 192, 192]); o_cscale = din("odd_c_scale", [1, BW])
    o_dw_w = din("odd_d_dw_w", [1, 31, BW]); o_dw_b = din("odd_d_dw_b", [1, BW])
    o_ln_g = din("odd_d_ln_g", [1, BW]); o_ln_b = din("odd_d_ln_b", [1, BW])
    o_pw_w = din("odd_d_pw_w", [1, BW, BW]); o_pw_b = din("odd_d_pw_b", [1, BW])
    o_mem_g = din("odd_mem_g", [1, D]); o_w_kv = din("odd_w_kv", [1, D, D])
    o_w_out = din("odd_w_out", [1, 2 * D, D]); o_post_g = din("odd_post_g", [1, D])
    out_d = nc.dram_tensor("out", [seq, D], F32, kind="ExternalOutput").ap()

    blocks = []

    def add_block(spec):
        blocks.append(spec)
        return len(blocks) - 1

    def wblk(w2d, ld, r0_list_stride, A, B, c0):
        return dict(kind="w", w=w2d, A=A, B=B, c0=c0, rs=r0_list_stride)

    w_in0 = e_w_in[0]; w_in1 = o_w_in[0]
    L0 = {}
    L0["VA"] = [add_block(dict(kind="w", w=w_in0, A=2, B=512, c0=768, r0=2 * j * 128)) for j in range(4)]
    L0["VB"] = [add_block(dict(kind="w", w=w_in0, A=4, B=256, c0=1280, r0=4 * j * 128)) for j in range(2)]
    def win_blk(w, col):
        return add_block(dict(kind="w", w=w, A=8, B=128, c0=col, r0=0))
    L0["U"] = [win_blk(w_in0, c * 128) for c in range(6)]
    L0["BG"] = [win_blk(w_in0, 1536 + c * 128) for c in range(6)]
    L0["CG"] = [win_blk(w_in0, 2304 + c * 128) for c in range(6)]
    L0["XI"] = [win_blk(w_in0, 3072 + c * 128) for c in range(6)]
    L0["Q"] = [win_blk(w_in0, 3840 + c * 128) for c in range(4)]
    L0["G"] = [win_blk(w_in0, 4352 + c * 128) for c in range(16)]
    L0["WO"] = [add_block(dict(kind="w", w=e_w_out[0], A=2, B=512, c0=half * 512, r0=2 * ep * 128))
                for half in range(2) for ep in range(8)]
    L0["KV"] = [win_blk(e_w_kv[0], j * 128) for j in range(8)]
    L1 = {}
    L1["ZC"] = [win_blk(w_in1, c * 128) for c in range(6)]
    L1["GA"] = [win_blk(w_in1, 768 + c * 128) for c in range(6)]
    L1["GB"] = [win_blk(w_in1, 1536 + c * 128) for c in range(6)]
    L1["Q"] = [win_blk(w_in1, 2304 + c * 128) for c in range(4)]
    L1["G"] = [win_blk(w_in1, 2816 + c * 128) for c in range(16)]
    L1["BD"] = [add_block(dict(kind="bd", cp=c)) for c in range(6)]
    L1["DG"] = [[add_block(dict(kind="dg", c=c, j=j)) for j in range(4)] for c in range(6)]
    L1["PW"] = [add_block(dict(kind="w", w=o_pw_w[0], A=6, B=128, c0=c * 128, r0=0)) for c in range(6)]
    L1["WO"] = [add_block(dict(kind="w", w=o_w_out[0], A=2, B=512, c0=half * 512, r0=2 * ep * 128))
                for half in range(2) for ep in range(8)]
    L1["KV"] = [win_blk(o_w_kv[0], j * 128) for j in range(8)]
    NBLK = len(blocks)
    wsc = nc.dram_tensor("wsc", [NBLK, 128, 1024], BF16).ap()

    def bd_kcs(cp):
        return sorted({kc for kc in range(6)
                       if any(max(kc * 128, g * 192) < min(kc * 128 + 128, g * 192 + 192) and
                              max(cp * 128, g * 192) < min(cp * 128 + 128, g * 192 + 192) for g in range(4))})

    order = []
    order += L0["KV"]
    for ti_ in range(nt):
        order += L0["VA"] + L0["VB"]
        for c in range(6):
            order += [L0["U"][c], L0["G"][c]]
        for c in range(6):
            order += [L0["CG"][c], L0["XI"][c], L0["BG"][c], L0["G"][6 + c]]
        order += L0["Q"] + L0["G"][12:16] + L0["WO"]
        if nlayers > 1:
            if ti_ == 0:
                order += L1["KV"]
            order += L1["ZC"]
            for c in range(6):
                order += [L1["BD"][c], L1["G"][c]]
            for c in range(6):
                order += [L1["GA"][c], L1["GB"][c]] + L1["DG"][c]
            for c in range(6):
                order += [L1["PW"][c], L1["G"][6 + c]]
            order += L1["Q"] + L1["G"][12:16] + L1["WO"]

    with contextlib.ExitStack() as es:
        S = Sched(nc, es)

        def sb(name, shape, dt=F32):
            return es.enter_context(nc.sbuf_tensor(name, list(shape), dt))

        ring = sb("ring", [128, RING, 1024], BF16); ring_res = [Res("ring%d" % i) for i in range(RING)]
        xts = [sb("xt%d" % j, [128, 4, D]) for j in range(2)]
        xtrs = [[Res("xt%d_%d" % (j, i)) for i in range(4)] for j in range(2)]
        xt = xts[0]; xt_res = xtrs[0]
        hT = sb("hT", [128, 8, T], BF16); hT_res = [Res("hT%d" % i) for i in range(4)]
        H = dict(ap=hT, res=[[r_] for r_ in hT_res])
        hb = [sb("hb%d" % i, [128, D], BF16) for i in range(2)]; hb_res = [Res("hb0"), Res("hb1")]
        yg = sb("yg", [128, 16, T], BF16); yg_res = [Res("yg%d" % i) for i in range(16)]
        NTMP = 6
        tmp = [sb("tmp%d" % i, [128, 528]) for i in range(NTMP)]; tmp_res = [Res("tmp%d" % i) for i in range(NTMP)]
        tmp_i = [0]

        def gettmp():
            i = tmp_i[0]; tmp_i[0] = (i + 1) % NTMP
            return tmp[i], tmp_res[i]

        cst = sb("cst", [128, 352]); cst_res = Res("cst")
        ident = sb("ident", [128, 128], BF16); ident_res = Res("ident")
        onesf = sb("onesf", [128, 128]); onesf_res = Res("onesf")
        ones1 = sb("ones1", [1, 128], BF16); ones1_res = Res("ones1")
        gb = [sb("gb%d" % l, [128, 8, 128]) for l in range(2)]; gb_res = [Res("gb0"), Res("gb1")]
        gcol = sb("gcol", [128, 4, 8]); gcol_res = Res("gcol")
        pgb = [sb("pgb%d" % l, [128, D]) for l in range(2)]; pgb_res = [Res("pgb0"), Res("pgb1")]
        lngb = sb("lngb", [128, BW]); lnbb = sb("lnbb", [128, BW]); lnp_res = Res("lnp")
        wmT = sb("wmT", [128, 4, 128], BF16); wmT_res = Res("wmT")
        bsrow = sb("bsrow", [1, 512], BF16); bsrow_res = Res("bsrow")
        chp = sb("chp", [128, 40, 6]); chp_res = Res("chp")
        kT = [sb("kT%d" % l, [128, 4, NMEM], BF16) for l in range(2)]; kT_res = [Res("kT0"), Res("kT1")]
        vv = [sb("vv%d" % l, [128, 2, 512], BF16) for l in range(2)]; vv_res = [Res("vv0"), Res("vv1")]
        ssq = sb("ssq", [128, 8]); ssq_res = Res("ssq")
        ms4 = sb("ms4", [128, 4]); ms4_res = Res("ms4")
        rstd4 = sb("rstd4", [128, 4]); rstd4_res = Res("rstd4")
        vln = sb("vln", [128, 4, BW], BF16); vln_res = [Res("vln%d" % i) for i in range(4)]
        vn = [sb("vn%d" % i, [128, BW]) for i in range(2)]; vn_res = [Res("vn0"), Res("vn1")]
        st6 = [sb("st6_%d" % i, [128, 3, 6]) for i in range(4)]; st6_res = [Res("st6%d" % i) for i in range(4)]
        mv = [sb("mv%d" % i, [128, 4]) for i in range(4)]; mv_res = [Res("mv%d" % i) for i in range(4)]
        pbh = sb("pbh", [128, 6, 2]); pbh_res = [Res("pbh%d" % i) for i in range(6)]
        qT = sb("qT", [128, 4, T], BF16); qT_res = [Res("qT%d" % i) for i in range(4)]
        sgx = sb("sgx", [128, 4, T]); sgx_res = [Res("sgx%d" % i) for i in range(4)]
        ebuf = sb("ebuf", [128, 4, NMEM]); ebuf_res = Res("ebuf")
        pbuf = sb("pbuf", [128, 4, NMEM], BF16); pbuf_res = Res("pbuf")
        pts = sb("pts", [128, 1024], BF16); pts_res = Res("pts")
        at_s = [sb("at_s%d" % i, [128, 16]) for i in range(2)]; at_res = [Res("at0"), Res("at1")]
        zbh = sb("zbh", [128, 6, 16]); zbh_res = [Res("zbh%d" % i) for i in range(6)]
        pooled = sb("pooled", [128, 6, T], BF16); pooled_res = [Res("pooled%d" % i) for i in range(6)]
        zg = sb("zg", [128, 6, 30 + T], BF16); zg_res = [Res("zg%d" % i) for i in range(6)]
        cvs = sb("cvs", [128, 6, T]); cvs_res = [Res("cvs%d" % i) for i in range(6)]
        zs = sb("zs", [128, 6, T], BF16); zs_res = [Res("zs%d" % i) for i in range(6)]
        stA = sb("stA", [128, T]); stA_res = Res("stA")
        stB = sb("stB", [128, T]); stB_res = Res("stB")
        junk = sb("junk", [128, D], BF16); junk_res = Res("junk")
        ps = es.enter_context(nc.psum_tensor("ps", [128, 8, 512], F32))
        psb = ps.bitcast(BF16)
        ps_res = [Res("ps%d" % i) for i in range(8)]
        bank_ptr = [0]
        reserved = set()

        def alloc(n=1):
            while True:
                b = bank_ptr[0]
                if n == 2 and b % 2:
                    b += 1
                if b + n > 8:
                    b = 0
                bank_ptr[0] = (b + n) % 8
                if all((b + i) not in reserved for i in range(n)):
                    return b

        scr_res = Res("scr")
        out_res = Res("out")

        rstate = dict(loaded=0, taken=0, released=0)
        defer = dict(on=False)

        def blk_cols(bid):
            sp_ = blocks[bid]
            if sp_["kind"] == "w":
                return sp_["A"] * sp_["B"]
            return 768 if sp_["kind"] == "bd" else 1024

        prepped = set()
        stage_pool = dict(slots=[], i=0)
        pend_store = []

        def flush_stores(keep):
            while len(pend_store) > keep:
                bid, slot, nb_ = pend_store.pop(0)
                S.dma(wsc[bid][:, 0:nb_], ring[:, slot, 0:nb_], reads=[ring_res[slot]], writes=[scr_res_b[bid]])

        def ring_fill():
            while rstate["loaded"] < len(order) and rstate["loaded"] < rstate["released"] + RING:
                n = rstate["loaded"]
                slot = n % RING
                bid = order[n]
                nb_ = blk_cols(bid)
                if bid not in prepped:
                    sl = stage_pool["slots"]
                    st32, r32 = sl[stage_pool["i"] % len(sl)]
                    stage_pool["i"] += 1
                    prep_a(bid, st32, r32, ring[:, slot, :], [ring_res[slot]])
                    prepped.add(bid)
                    pend_store.append((bid, slot, nb_))
                    flush_stores(2)
                else:
                    flush_stores(0)
                    S.dma(ring[:, slot, 0:nb_], wsc[bid][:, 0:nb_], reads=[scr_res_b[bid]], writes=[ring_res[slot]])
                rstate["loaded"] += 1

        mode = dict(dry=False)

        def take(expect):
            if mode["dry"]:
                order.append(expect)
                return ring[:, 0, :], ring_res[0]
            n = rstate["taken"]
            assert order[n] == expect, (n, order[n], expect)
            assert n < rstate["loaded"], "ring too shallow"
            rstate["taken"] += 1
            slot = n % RING
            return ring[:, slot, :], ring_res[slot]

        def done():
            if mode["dry"]:
                return
            rstate["released"] = rstate["taken"]
            ring_fill()

        def bcast_rows(dram_row, n):
            return bass.AP(dram_row.tensor, dram_row.offset, [[0, 128], [1, n]])

        S.dma(cst[:], cst_d, writes=[cst_res])
        S.op("dve", lambda e: e.tensor_copy(out=ident[:], in_=cst[:, 0:128]), reads=[cst_res], writes=[ident_res])
        nh1 = sb("nh1", [128, 4]); nh1_res = Res("nh1")
        S.op("pool", lambda e: e.memset(nh1[:], -0.5), writes=[nh1_res])
        S.op("pool", lambda e: e.memset(onesf[:], 1.0), writes=[onesf_res])
        S.op("pool", lambda e: e.memset(ones1[:], 1.0), writes=[ones1_res])
        S.op("pool", lambda e: e.memset(zg[:], 0.0), writes=zg_res)
        S.op("pool", lambda e: e.memset(pbh[:], 0.0), writes=pbh_res)
        S.op("pool", lambda e: e.memset(zbh[:], 0.0), writes=zbh_res)
        jdummy = sb("jdummy", [128, 4])
        _g_rows = []
        for i, g in enumerate((e_pre_g, o_pre_g, e_mem_g, o_mem_g)):
            rr = Res("gcolrow%d" % i); _g_rows.append(rr)
            S.dma(gcol[:, i, :], g[0].rearrange("(c p) -> p c", p=128), writes=[rr],
                  allow_slow_non_contiguous=True)
        S.op("pool", lambda e: e.memset(jdummy[:, 0:1], 0.0), reads=_g_rows, writes=[gcol_res])
        for l in range(2):
            src = bass.AP(hnd(gcol), l * 8, [[32, 128], [1, 8], [0, 128]])
            S.op("dve", (lambda src, l: lambda e: e.tensor_copy(out=gb[l][:], in_=src))(src, l),
                 reads=[gcol_res], writes=[gb_res[l]])
        for l, g in enumerate((e_post_g, o_post_g)):
            S.dma(pgb[l][:], bcast_rows(g, D), writes=[pgb_res[l]])
        _lr = [Res("lnrow0"), Res("lnrow1")]
        S.dma(lngb[:], bcast_rows(e_ln_g, BW), writes=[_lr[0]])
        S.dma(lnbb[:], bcast_rows(e_ln_b, BW), writes=[_lr[1]])
        S.op("pool", lambda e: e.memset(jdummy[:, 1:2], 0.0), reads=_lr, writes=[lnp_res])
        prow = [e_bconv[0, k] for k in range(3)] + [o_cscale[0]] + [o_dw_w[0, k] for k in range(31)] + \
               [o_dw_b[0], o_ln_g[0], o_ln_b[0], o_pw_b[0]]
        _c_rows = []
        for r, src in enumerate(prow):
            rr = Res("chprow%d" % r); _c_rows.append(rr)
            S.dma(chp[:, r, :], src.rearrange("(c p) -> p c", p=128), writes=[rr],
                  allow_slow_non_contiguous=True)
        S.op("pool", lambda e: e.memset(jdummy[:, 2:3], 0.0), reads=_c_rows, writes=[chp_res])
        R_BC, R_CS, R_DW, R_DWB, R_LNG, R_LNB, R_PWB = 0, 3, 4, 35, 36, 37, 38
        wnat, wnat_res = tmp[0], tmp_res[0]
        S.dma(wnat[:, 0:512].rearrange("p (h s) -> p h s", h=4), e_ws[0].rearrange("h t s -> t h s"), writes=[wnat_res])
        for h in range(4):
            S.op("dve", (lambda h: lambda e: e.tensor_tensor(out=hb[0][:, h * 128:(h + 1) * 128], in0=wnat[:, h * 128:(h + 1) * 128],
                                                            in1=cst[:, 128:256], op=ALU.mult))(h),
                 reads=[wnat_res, cst_res], writes=[hb_res[0]])
        b = alloc()
        def _wtr(e, b=b):
            for h in range(4):
                ins = e.transpose(out=psb[:, b, h * 128:(h + 1) * 128], in_=hb[0][:, h * 128:(h + 1) * 128], identity=ident[:])
            return ins
        S.op("pe", _wtr, reads=[hb_res[0], ident_res], writes=[ps_res[b]])
        S.op("act", (lambda b: lambda e: e.activation(out=wmT[:].rearrange("p h t -> p (h t)"), in_=psb[:, b, 0:512], func=AF.Copy))(b),
             reads=[ps_res[b]], writes=[wmT_res])
        S.dma(tmp[1][0:1, 0:512], e_bs[0].rearrange("h t -> (h t)").rearrange("(o n) -> o n", o=1), writes=[tmp_res[1]])
        S.op("dve", lambda e: e.tensor_copy(out=bsrow[:], in_=tmp[1][0:1, 0:512]), reads=[tmp_res[1]], writes=[bsrow_res])

        cast_ctr = [0]

        def prep_a(bi, st32, st32_res, st16, st16_res):
            spec = blocks[bi]
            ce = ("act", "dve")[cast_ctr[0] % 2]
            cast_ctr[0] += 1
            if spec["kind"] == "w":
                w = spec["w"]; A = spec["A"]; B = spec["B"]; c0 = spec["c0"]; r0 = spec["r0"]
                src = w[r0:r0 + A * 128, c0:c0 + B].rearrange("(a p) n -> p a n", p=128)
                S.dma(st32[:, 0:A * B].rearrange("p (a n) -> p a n", a=A), src, writes=st32_res)
                n = A * B
                if ce == "act":
                    S.op("act", lambda e: e.activation(out=st16[:, 0:n], in_=st32[:, 0:n], func=AF.Copy), reads=st32_res, writes=st16_res)
                else:
                    S.op("dve", lambda e: e.tensor_copy(out=st16[:, 0:n], in_=st32[:, 0:n]), reads=st32_res, writes=st16_res)
                return n
            elif spec["kind"] == "bd":
                cp = spec["cp"]
                S.op("pool", lambda e: e.memset(st32, 0.0), writes=st32_res)
                for kc in range(6):
                    for g in range(4):
                        ra, rb = max(kc * 128, g * 192), min(kc * 128 + 128, g * 192 + 192)
                        ca, cb = max(cp * 128, g * 192), min(cp * 128 + 128, g * 192 + 192)
                        if ra < rb and ca < cb:
                            dst = st32[ra - kc * 128:rb - kc * 128, kc * 128 + (ca - cp * 128):kc * 128 + (cb - cp * 128)]
                            S.dma(dst, o_wgrp[0, g, ra - g * 192:rb - g * 192, ca - g * 192:cb - g * 192],
                                  reads=st32_res, writes=st32_res)
                S.op("dve", lambda e: e.tensor_copy(out=st16[:, 0:768], in_=st32[:, 0:768]), reads=st32_res, writes=st16_res)
                return 768
            else:
                c = spec["c"]; jj = spec["j"]
                def _dg(e):
                    ins = None
                    for kk in range(8):
                        k_ = jj * 8 + kk
                        if k_ < 31:
                            ins = e.tensor_scalar(out=st16[:, kk * 128:(kk + 1) * 128], in0=cst[:, 0:128],
                                                  scalar1=chp[:, R_DW + k_, c:c + 1], scalar2=0.5, op0=ALU.mult, op1=ALU.mult)
                        else:
                            ins = e.memset(st16[:, kk * 128:(kk + 1) * 128], 0.0)
                    return ins
                S.op("dve", _dg, reads=[cst_res, chp_res], writes=st16_res)
                return 1024

        scr_res_b = [Res("scr%d" % i) for i in range(NBLK)]

        class Prepper:
            def __init__(self, ids, s32, s16, la):
                self.ids = list(ids); self.s32 = s32; self.s16 = s16; self.la = la
                self.na = 0; self.nb = 0; self.n = {}; self.base = 0
                self.stored = set()

            def _a(self):
                i = self.na
                bi = self.ids[i]
                st32, r32 = self.s32[(i - self.base) % len(self.s32)]
                st16, r16 = self.s16[(i - self.base) % len(self.s16)]
                self.n[i] = prep_a(bi, st32, r32, st16, r16)
                self.na += 1

            def step(self):
                if self.nb >= len(self.ids):
                    return False
                while self.na < len(self.ids) and self.na < self.nb + max(self.la, 1):
                    self._a()
                i = self.nb
                bi = self.ids[i]
                st16, r16 = self.s16[(i - self.base) % len(self.s16)]
                n = self.n[i]
                S.dma(wsc[bi][:, 0:n], st16[:, 0:n], reads=r16, writes=[scr_res_b[bi]])
                self.stored.add(bi)
                self.nb += 1
                return True

            def ensure(self, bi):
                while bi not in self.stored:
                    assert self.step()

            def switch(self, s32, s16, la):
                old_la = self.la
                self.la = 0
                while self.nb < self.na:
                    self.step()
                self.s32 = s32; self.s16 = s16; self.la = la
                self.base = self.nb

        def pair16(buf, res, j):
            return (buf[:, 2 * j:2 * j + 2, :].rearrange("p a n -> p (a n)"), [res[2 * j], res[2 * j + 1]])

        stsm = [sb("stsm%d" % i, [128, 4]) for i in range(4)]; stsm_res = [Res("stsm%d" % i) for i in range(4)]

        def rms_pre(l, nsub, g_tile, g_res, xt, xt_res, as_waves=False, dst=None, hidden=False):
            dst_ap, dst_res = (hT, [[r_] for r_ in hT_res]) if dst is None else dst
            def w_sq(sub):
                m = stsm[sub]
                S.op("act", lambda e: e.activation(out=junk[:], in_=xt[:, sub, :], func=AF.Square, accum_out=m[:, 0:1]),
                     reads=[xt_res[sub]], writes=[junk_res, stsm_res[sub]])

            def w_stat(sub):
                m = stsm[sub]; mr = stsm_res[sub]
                S.op("dve", lambda e: e.tensor_scalar(out=m[:, 1:2], in0=m[:, 0:1], scalar1=1.0 / D, scalar2=EPS, op0=ALU.mult, op1=ALU.add),
                     reads=[mr], writes=[mr])
                if hidden:
                    S.op("pool", lambda e: e.tensor_tensor(out=m[:, 2:3], in0=m[:, 1:2], in1=nh1[:, 0:1], op=ALU.pow),
                         reads=[mr, nh1_res], writes=[mr])
                else:
                    S.op("act", lambda e: e.activation(out=m[:, 1:2], in_=m[:, 1:2], func=AF.Sqrt), reads=[mr], writes=[mr])
                    S.op("dve", lambda e: e.reciprocal(out=m[:, 2:3], in_=m[:, 1:2]), reads=[mr], writes=[mr])

            def w_scale(sub):
                m = stsm[sub]; mr = stsm_res[sub]
                hbt, hbr = hb[sub % 2], hb_res[sub % 2]
                if hidden:
                    S.op("dve", lambda e: e.tensor_scalar(out=hbt[:], in0=xt[:, sub, :], scalar1=m[:, 2:3], scalar2=None, op0=ALU.mult),
                         reads=[xt_res[sub], mr], writes=[hbr])
                else:
                    S.op("act", lambda e: e.activation(out=hbt[:], in_=xt[:, sub, :], func=AF.Copy, scale=m[:, 2:3]),
                         reads=[xt_res[sub], mr], writes=[hbr])

            def w_tr(sub):
                hbt, hbr = hb[sub % 2], hb_res[sub % 2]
                b = alloc()
                def _tr(e):
                    for kc in range(8):
                        ins = e.transpose(out=psb[:, b, kc * 128:(kc + 1) * 128], in_=hbt[:, kc * 128:(kc + 1) * 128], identity=ident[:])
                    return ins
                S.op("pe", _tr, reads=[hbr, ident_res], writes=[ps_res[b]])
                S.op("dve", lambda e: e.tensor_tensor(
                    out=dst_ap[:, :, sub * 128:(sub + 1) * 128], in0=psb[:, b, :].rearrange("p (k t) -> p k t", k=8),
                    in1=g_tile, op=ALU.mult), reads=[ps_res[b]] + g_res, writes=list(dst_res[sub]))

            def wave(k):
                if 0 <= k - 3 < nsub:
                    w_tr(k - 3)
                if 0 <= k - 2 < nsub:
                    w_scale(k - 2)
                if 0 <= k - 1 < nsub:
                    w_stat(k - 1)
                if k < nsub:
                    w_sq(k)
            waves = [(lambda k=k: wave(k)) for k in range(nsub + 3)]
            if as_waves:
                return waves
            for w in waves:
                w()

        def mm_fm(blk, blk_res, b, nk=8):
            hcur = H["ap"]
            hres = []
            for rl in H["res"]:
                for r_ in rl:
                    if r_ not in hres:
                        hres.append(r_)
            def f(e):
                for kc in range(nk):
                    ins = e.matmul(out=ps[:, b, :], lhsT=blk[:, kc * 128:(kc + 1) * 128], rhs=hcur[:, kc, :],
                                   start=(kc == 0), stop=(kc == nk - 1))
                return ins
            S.op("pe", f, reads=[blk_res] + hres, writes=[ps_res[b]])

        def proj(bid):
            blk, br = take(bid)
            b = alloc()
            mm_fm(blk, br, b)
            return b

        def silu_to_tmp(b):
            t, tr = gettmp()
            S.op("act", lambda e: e.activation(out=t[:, 0:T], in_=ps[:, b, :], func=AF.Silu), reads=[ps_res[b]], writes=[tr])
            return t, tr

        mg = sgx[:, 0:2, :].rearrange("p a (k t) -> p (a k) t", k=4); mg_res = [sgx_res[0], sgx_res[1]]

        def kv_prologue(l, LB, memg_idx, xt, xt_res):
            S.dma(xt[:, 0:2, :], mem_d.rearrange("(s p) d -> p s d", p=128), writes=[xt_res[0], xt_res[1]])
            src = bass.AP(hnd(gcol), memg_idx * 8, [[32, 128], [1, 8], [0, 128]])
            S.op("dve", lambda e: e.tensor_copy(out=mg, in_=src), reads=[gcol_res], writes=mg_res)
            rms_pre(l, 2, mg, mg_res, xt, xt_res)
            kvb = [take(bid) for bid in LB["KV"]]
            for h in range(4):
                b = alloc()
                def f(e, h=h, b=b):
                    for kc in range(8):
                        ins = e.matmul(out=ps[:, b, 0:NMEM], lhsT=kvb[h][0][:, kc * 128:(kc + 1) * 128], rhs=hT[:, kc, 0:NMEM],
                                       start=(kc == 0), stop=(kc == 7))
                    return ins
                S.op("pe", f, reads=[kvb[h][1], hT_res[0], hT_res[1]], writes=[ps_res[b]])
                S.op("act", (lambda h, b: lambda e: e.activation(out=kT[l][:, h, :], in_=ps[:, b, 0:NMEM], func=AF.Copy))(h, b),
                     reads=[ps_res[b]], writes=[kT_res[l]])
            for mc in range(2):
                b = alloc()
                def f(e, mc=mc, b=b):
                    for j in range(4):
                        for kc in range(8):
                            ins = e.matmul(out=ps[:, b, j * 128:(j + 1) * 128], lhsT=hT[:, kc, mc * 128:(mc + 1) * 128],
                                           rhs=kvb[4 + j][0][:, kc * 128:(kc + 1) * 128], start=(kc == 0), stop=(kc == 7))
                    return ins
                S.op("pe", f, reads=[kvb[4 + j][1] for j in range(4)] + [hT_res[mc]], writes=[ps_res[b]])
                S.op("act", (lambda mc, b: lambda e: e.activation(out=vv[l][:, mc, :], in_=ps[:, b, :], func=AF.Copy))(mc, b),
                     reads=[ps_res[b]], writes=[vv_res[l]])
            done()


        pbufs = [pbuf, sb("pbufB", [128, 4, NMEM], BF16)]; pbufs_res = [pbuf_res, Res("pbufB")]

        def att_proj(l, LB):
            for h in range(4):
                b = proj(LB["Q"][h])
                S.op("act", (lambda h, b: lambda e: e.activation(out=qT[:, h, :], in_=ps[:, b, :], func=AF.Copy))(h, b),
                     reads=[ps_res[b]], writes=[qT_res[h]])
            done()
            for h in range(4):
                b = proj(LB["G"][12 + h])
                S.op("act", (lambda h, b: lambda e: e.activation(out=sgx[:, h, :], in_=ps[:, b, :], func=AF.Silu))(h, b),
                     reads=[ps_res[b]], writes=[sgx_res[h]])
            done()

        def att_stages(l):
            def s1(sub):
                ts = slice(sub * 128, (sub + 1) * 128)
                b2 = alloc(2)
                def fq(e):
                    for h in range(4):
                        ins = e.matmul(out=ps[:, b2 + h // 2, (h % 2) * 256:(h % 2) * 256 + 256], lhsT=qT[:, h, ts], rhs=kT[l][:, h, :],
                                       start=True, stop=True)
                    return ins
                S.op("pe", fq, reads=qT_res + [kT_res[l]], writes=[ps_res[b2], ps_res[b2 + 1]])
                a, ar = at_s[sub % 2], at_res[sub % 2]
                pb, pbr = pbufs[sub % 2], pbufs_res[sub % 2]
                sc4 = ps[:, b2:b2 + 2, :].rearrange("p b (h m) -> p (b h) m", h=2)
                S.op("dve", lambda e: e.tensor_reduce(out=a[:, 0:4], in_=sc4, axis=AX.X, op=ALU.max),
                     reads=[ps_res[b2], ps_res[b2 + 1]], writes=[ar])
                S.op("dve", lambda e: e.tensor_scalar(out=a[:, 4:8], in0=a[:, 0:4], scalar1=-SCALE, scalar2=None, op0=ALU.mult),
                     reads=[ar], writes=[ar])
                def fe(e):
                    for h in range(4):
                        ins = e.activation(out=ebuf[:, h, :], in_=ps[:, b2 + h // 2, (h % 2) * 256:(h % 2) * 256 + 256], func=AF.Exp,
                                           bias=a[:, 4 + h:5 + h], scale=SCALE, accum_out=a[:, 8 + h:9 + h])
                    return ins
                S.op("act", fe, reads=[ps_res[b2], ps_res[b2 + 1], ar], writes=[ebuf_res, ar])
                S.op("dve", lambda e: e.reciprocal(out=a[:, 12:16], in_=a[:, 8:12]), reads=[ar], writes=[ar])
                def fn_(e):
                    for h in range(4):
                        ins = e.tensor_scalar(out=pb[:, h, :], in0=ebuf[:, h, :], scalar1=a[:, 12 + h:13 + h], scalar2=None, op0=ALU.mult)
                    return ins
                S.op("dve", fn_, reads=[ebuf_res, ar], writes=[pbr])

            def s2a(sub):
                pb, pbr = pbufs[sub % 2], pbufs_res[sub % 2]
                bT = alloc()
                def ft(e):
                    for mc in range(2):
                        for h in range(4):
                            ins = e.transpose(out=psb[:, bT, (mc * 4 + h) * 128:(mc * 4 + h + 1) * 128],
                                              in_=pb[:, h, mc * 128:(mc + 1) * 128], identity=ident[:])
                    return ins
                S.op("pe", ft, reads=[pbr, ident_res], writes=[ps_res[bT]])
                S.op("dve", lambda e: e.tensor_copy(out=pts[:], in_=psb[:, bT, :]), reads=[ps_res[bT]], writes=[pts_res])

            def s2b(sub):
                ts = slice(sub * 128, (sub + 1) * 128)
                bO = alloc()
                def fo(e):
                    for h in range(4):
                        for mc in range(2):
                            ins = e.matmul(out=ps[:, bO, h * 128:(h + 1) * 128], lhsT=vv[l][:, mc, h * 128:(h + 1) * 128],
                                           rhs=pts[:, (mc * 4 + h) * 128:(mc * 4 + h + 1) * 128], start=(mc == 0), stop=(mc == 1))
                    return ins
                S.op("pe", fo, reads=[pts_res, vv_res[l]], writes=[ps_res[bO]])
                S.op("dve", lambda e: e.tensor_tensor(out=yg[:, 12:16, ts], in0=ps[:, bO, :].rearrange("p (h t) -> p h t", h=4),
                                                      in1=sgx[:, :, ts], op=ALU.mult),
                     reads=[ps_res[bO]] + sgx_res, writes=yg_res[12:16])

            fmap = {"s1": s1, "s2a": s2a, "s2b": s2b}
            seq = [("s1", 0), ("s1", 1), ("s2a", 0), ("s1", 2), ("s2b", 0), ("s2a", 1), ("s1", 3), ("s2b", 1),
                   ("s2a", 2), ("s2b", 2), ("s2a", 3), ("s2b", 3)]
            return [(lambda k=k, sub=sub: fmap[k](sub)) for k, sub in seq]

        def interleave(mains, fillers, slots):
            fi = 0
            for i, m in enumerate(mains):
                m()
                for _ in range(slots.count(i)):
                    if fi < len(fillers):
                        fillers[fi](); fi += 1
            while fi < len(fillers):
                fillers[fi](); fi += 1

        warm = sb("warm", [128, 4]); warm_res = Res("warm")
        S.op("pool", lambda e: e.memset(warm[:], 1.0), writes=[warm_res])

        def w_out_phase(l, LB, last_layer, tile, xt, xt_res, extra=(), after_sub=()):
            bank_ptr[0] = 0
            after_sub = list(after_sub)
            for half in range(2):
                for ep in range(8):
                    blk, br = take(LB["WO"][half * 8 + ep])
                    def f(e, blk=blk, half=half, ep=ep):
                        for el in range(2):
                            ee = 2 * ep + el
                            for sub in range(4):
                                ins = e.matmul(out=ps[:, 2 * sub + half, :], lhsT=yg[:, ee, sub * 128:(sub + 1) * 128],
                                               rhs=blk[:, el * 512:(el + 1) * 512], start=(ee == 0), stop=(ee == 15))
                        return ins
                    S.op("pe", f, reads=[br, yg_res[2 * ep], yg_res[2 * ep + 1]], writes=[ps_res[2 * s_ + half] for s_ in range(4)])
                    done()
                def fs(e, half=half):
                    for sub in range(4):
                        dmy = (junk[:, 0:512], junk[:, 512:1024], hb[0][:, 0:512], hb[1][:, 0:512])[sub]
                        ins = e.activation(out=dmy, in_=ps[:, 2 * sub + half, :], func=AF.Square,
                                           accum_out=ssq[:, half * 4 + sub:half * 4 + sub + 1])
                    return ins
                S.op("act", fs, reads=[ps_res[2 * s_ + half] for s_ in range(4)], writes=[junk_res, ssq_res, hb_res[0], hb_res[1]])
                if half == 0:
                    S.op("act", lambda e: e.activation(out=warm[:, 0:1], in_=warm[:, 1:2], func=AF.Sqrt), reads=[warm_res], writes=[warm_res])
            S.op("dve", lambda e: e.tensor_tensor(out=ms4[:], in0=ssq[:, 0:4], in1=ssq[:, 4:8], op=ALU.add), reads=[ssq_res], writes=[ms4_res])
            S.op("act", lambda e: e.activation(out=ms4[:], in_=ms4[:], func=AF.Sqrt, bias=EPS, scale=1.0 / D), reads=[ms4_res], writes=[ms4_res])
            S.op("dve", lambda e: e.reciprocal(out=rstd4[:], in_=ms4[:]), reads=[ms4_res], writes=[rstd4_res])
            for sub in range(4):
                for half in range(2):
                    t, tr = gettmp()
                    hs = slice(half * 512, (half + 1) * 512)
                    S.op("dve", (lambda t, sub, half, hs: lambda e: e.scalar_tensor_tensor(
                        out=t[:, 0:512], in0=ps[:, 2 * sub + half, :], scalar=rstd4[:, sub:sub + 1], in1=pgb[l][:, hs],
                        op0=ALU.mult, op1=ALU.mult))(t, sub, half, hs),
                        reads=[ps_res[2 * sub + half], rstd4_res, pgb_res[l]], writes=[tr])
                    S.op("pool", (lambda t, sub, hs: lambda e: e.tensor_tensor(out=xt[:, sub, hs], in0=xt[:, sub, hs], in1=t[:, 0:512], op=ALU.add))(t, sub, hs),
                         reads=[tr, xt_res[sub]], writes=[xt_res[sub]])
                if after_sub:
                    after_sub.pop(0)()
            while after_sub:
                after_sub.pop(0)()
            if last_layer:
                S.dma(out_d[tile * T:(tile + 1) * T, :].rearrange("(s p) d -> p s d", p=128), xt[:, :, :], reads=xt_res, writes=[out_res])

        hTa = cvs[:, 0:4, :].rearrange("p a n -> p (a n)").bitcast(BF16).rearrange("p (k t) -> p k t", k=8)
        pre_done = {}

        def load_x(tile):
            S.dma(xts[tile % 2][:, :, :], x_d[tile * T:(tile + 1) * T, :].rearrange("(s p) d -> p s d", p=128), writes=xtrs[tile % 2])

        def do_tile(tile):
            xt = xts[tile % 2]; xt_res = xtrs[tile % 2]
            real = not mode["dry"]
            if tile == 0:
                load_x(0)
            if pre_done.get(tile):
                H["ap"] = hTa; H["res"] = [list(cvs_res[0:4]) for _ in range(4)]
            else:
                H["ap"] = hT; H["res"] = [[r_] for r_ in hT_res]
                rms_pre(0, 4, gb[0][:], [gb_res[0]], xt, xt_res)
            vab = [take(bid) for bid in L0["VA"]]
            vbb = [take(bid) for bid in L0["VB"]]
            for sub in range(4):
                ts = slice(sub * 128, (sub + 1) * 128)
                b2 = alloc(2)
                def fv(e, ts=ts, b2=b2, hcur=H["ap"]):
                    for kc in range(8):
                        e.matmul(out=ps[:, b2, :], lhsT=hcur[:, kc, ts], rhs=vab[kc // 2][0][:, (kc % 2) * 512:(kc % 2) * 512 + 512],
                                 start=(kc == 0), stop=(kc == 7))
                        ins = e.matmul(out=ps[:, b2 + 1, 0:256], lhsT=hcur[:, kc, ts], rhs=vbb[kc // 4][0][:, (kc % 4) * 256:(kc % 4) * 256 + 256],
                                       start=(kc == 0), stop=(kc == 7))
                    return ins
                S.op("pe", fv, reads=[x_[1] for x_ in vab + vbb] + list(H["res"][sub]), writes=[ps_res[b2], ps_res[b2 + 1]])
                def fbn(e, sub=sub, b2=b2):
                    e.bn_stats(out=st6[sub][:, 0, :], in_=ps[:, b2, 0:256])
                    e.bn_stats(out=st6[sub][:, 1, :], in_=ps[:, b2, 256:512])
                    return e.bn_stats(out=st6[sub][:, 2, :], in_=ps[:, b2 + 1, 0:256])
                S.op("dve", fbn, reads=[ps_res[b2], ps_res[b2 + 1]], writes=[st6_res[sub]])
                m = mv[sub]; mr = mv_res[sub]
                S.op("dve", (lambda sub, m: lambda e: e.bn_aggr(out=m[:, 0:2], in_=st6[sub][:]))(sub, m), reads=[st6_res[sub]], writes=[mr])
                S.op("dve", (lambda m: lambda e: e.tensor_scalar(out=m[:, 2:3], in0=m[:, 1:2], scalar1=EPS, scalar2=None, op0=ALU.add))(m),
                     reads=[mr], writes=[mr])
                S.op("act", (lambda m: lambda e: e.activation(out=m[:, 2:3], in_=m[:, 2:3], func=AF.Sqrt))(m), reads=[mr], writes=[mr])
                S.op("dve", (lambda m: lambda e: e.reciprocal(out=m[:, 3:4], in_=m[:, 2:3]))(m), reads=[mr], writes=[mr])
                S.op("dve", (lambda m: lambda e: e.tensor_scalar(out=m[:, 2:3], in0=m[:, 0:1], scalar1=m[:, 3:4], scalar2=-1.0,
                                                                 op0=ALU.mult, op1=ALU.mult))(m), reads=[mr], writes=[mr])
                vt, vr = vn[sub % 2], vn_res[sub % 2]
                def fva(e, m=m, vt=vt, b2=b2):
                    e.activation(out=vt[:, 0:512], in_=ps[:, b2, :], func=AF.Identity, bias=m[:, 2:3], scale=m[:, 3:4])
                    return e.activation(out=vt[:, 512:768], in_=ps[:, b2 + 1, 0:256], func=AF.Identity, bias=m[:, 2:3], scale=m[:, 3:4])
                S.op("act", fva, reads=[ps_res[b2], ps_res[b2 + 1], mr], writes=[vr])
                S.op("pool", (lambda vt: lambda e: e.tensor_tensor(out=vt[:], in0=vt[:], in1=lngb[:], op=ALU.mult))(vt),
                     reads=[vr, lnp_res], writes=[vr])
                S.op("pool", (lambda vt, sub: lambda e: e.tensor_tensor(out=vln[:, sub, :], in0=vt[:], in1=lnbb[:], op=ALU.add))(vt, sub),
                     reads=[vr, lnp_res], writes=[vln_res[sub]])
            done()
            att_proj(0, L0)

            def a_chunk(c):
                bu = proj(L0["U"][c])
                bg = proj(L0["G"][c])
                done()
                bs = alloc()
                segs = _head_segs(c)
                def fsg(e):
                    for sub in range(4):
                        ts = slice(sub * 128, (sub + 1) * 128)
                        for (p0, p1, h) in segs:
                            e.matmul(out=ps[p0:p1, bs, ts], lhsT=vln[:, sub, c * 128 + p0:c * 128 + p1], rhs=wmT[:, h, :],
                                     start=True, stop=False)
                            ins = e.matmul(out=ps[p0:p1, bs, ts], lhsT=ones1[0:1, 0:p1 - p0], rhs=bsrow[0:1, h * 128:(h + 1) * 128],
                                           start=False, stop=True)
                    return ins
                S.op("pe", fsg, reads=vln_res + [wmT_res, ones1_res, bsrow_res], writes=[ps_res[bs]])
                sgt, sgr = silu_to_tmp(bg)
                t, tr = gettmp()
                S.op("dve", lambda e: e.tensor_tensor(out=t[:, 0:T], in0=ps[:, bu, :], in1=sgt[:, 0:T], op=ALU.mult),
                     reads=[ps_res[bu], sgr], writes=[tr])
                S.op("dve", lambda e: e.tensor_tensor(out=yg[:, c, :], in0=ps[:, bs, :], in1=t[:, 0:T], op=ALU.mult),
                     reads=[ps_res[bs], tr], writes=[yg_res[c]])

            def b_chunk(c):
                bcg = proj(L0["CG"][c]); bxi = proj(L0["XI"][c]); bbg = proj(L0["BG"][c]); bg = proj(L0["G"][6 + c])
                done()
                xi, xir = gettmp()
                S.op("act", lambda e: e.activation(out=xi[:, 0:T], in_=ps[:, bxi, :], func=AF.Copy), reads=[ps_res[bxi]], writes=[xir])
                sgt, sgr = silu_to_tmp(bg)
                pr, prr = gettmp()
                S.op("pool", lambda e: e.tensor_copy(out=pr[:, 0:2], in_=pbh[:, c, :]), reads=[pbh_res[c]], writes=[prr])
                S.op("dve", lambda e: e.tensor_tensor(out=pr[:, 2:2 + T], in0=ps[:, bcg, :], in1=xi[:, 0:T], op=ALU.mult),
                     reads=[ps_res[bcg], xir, prr], writes=[prr])
                S.op("pool", lambda e: e.tensor_copy(out=pbh[:, c, :], in_=pr[:, T:T + 2]), reads=[prr], writes=[pbh_res[c]])
                acc, accr = gettmp()
                S.op("dve", lambda e: e.tensor_scalar(out=acc[:, 0:T], in0=pr[:, 0:T], scalar1=chp[:, R_BC + 0, c:c + 1],
                                                      scalar2=None, op0=ALU.mult), reads=[prr, chp_res], writes=[accr])
                for k in (1, 2):
                    S.op("dve", (lambda k: lambda e: e.scalar_tensor_tensor(
                        out=acc[:, 0:T], in0=pr[:, k:k + T], scalar=chp[:, R_BC + k, c:c + 1], in1=acc[:, 0:T],
                        op0=ALU.mult, op1=ALU.add))(k), reads=[prr, chp_res, accr], writes=[accr])
                S.op("dve", lambda e: e.tensor_tensor(out=acc[:, 0:T], in0=ps[:, bbg, :], in1=acc[:, 0:T], op=ALU.mult),
                     reads=[ps_res[bbg], accr], writes=[accr])
                S.op("pool", lambda e: e.tensor_tensor(out=yg[:, 6 + c, :], in0=acc[:, 0:T], in1=sgt[:, 0:T], op=ALU.mult),
                     reads=[accr, sgr], writes=[yg_res[6 + c]])

            mains = [(lambda c=c: a_chunk(c)) for c in range(6)] + [(lambda c=c: b_chunk(c)) for c in range(6)]
            interleave(mains, att_stages(0), list(range(12)))
            pre1 = ()
            if tile > 0 and nlayers > 1:
                pre1 = rms_pre(1, 4, gb[1][:], [gb_res[1]], xt, xt_res, as_waves=True)
            w_out_phase(0, L0, nlayers == 1, tile, xt, xt_res, after_sub=pre1)
            if tile == 0 and nlayers > 1:
                if real:
                    stage_pool["slots"] = stage_pool["l1"]
                kv_prologue(1, L1, 3, xts[1], xtrs[1])
            if tile + 1 < nt and (tile > 0 or nlayers == 1):
                if real:
                    flush_stores(0)
                load_x(tile + 1)
            if nlayers == 1:
                return
            H["ap"] = hT; H["res"] = [[r_] for r_ in hT_res]
            if tile == 0:
                rms_pre(1, 4, gb[1][:], [gb_res[1]], xt, xt_res)
            for c in range(6):
                bz = proj(L1["ZC"][c])
                z, zr = gettmp()
                S.op("pool", (lambda z, c: lambda e: e.tensor_copy(out=z[:, 0:16], in_=zbh[:, c, :]))(z, c), reads=[zbh_res[c]], writes=[zr])
                S.op("act", (lambda z, bz: lambda e: e.activation(out=z[:, 16:528], in_=ps[:, bz, :], func=AF.Copy))(z, bz),
                     reads=[ps_res[bz], zr], writes=[zr])
                S.op("pool", (lambda z, c: lambda e: e.tensor_copy(out=zbh[:, c, :], in_=z[:, 512:528]))(z, c), reads=[zr], writes=[zbh_res[c]])
                wins = sorted({POOLW[(c * 128 + p) // 192] for p in (0, 127)})
                sA, sAr = gettmp(); sB, sBr = gettmp()
                have = {}
                S.op("dve", (lambda sA, z: lambda e: e.tensor_tensor(out=sA[:, 1:528], in0=z[:, 1:528], in1=z[:, 0:527], op=ALU.add))(sA, z),
                     reads=[zr], writes=[sAr])
                have[2] = (sA, sAr)
                if max(wins) >= 4:
                    S.op("dve", (lambda sA, sB: lambda e: e.tensor_tensor(out=sB[:, 3:528], in0=sA[:, 3:528], in1=sA[:, 1:526], op=ALU.add))(sA, sB),
                         reads=[sAr], writes=[sBr])
                    have[4] = (sB, sBr)
                if max(wins) >= 8:
                    S.op("dve", (lambda sA, sB: lambda e: e.tensor_tensor(out=sA[:, 7:528], in0=sB[:, 7:528], in1=sB[:, 3:524], op=ALU.add))(sA, sB),
                         reads=[sBr, sAr], writes=[sAr])
                    have[8] = (sA, sAr)
                if max(wins) >= 16:
                    S.op("dve", (lambda sA, sB: lambda e: e.tensor_tensor(out=sB[:, 15:528], in0=sA[:, 15:528], in1=sA[:, 7:520], op=ALU.add))(sA, sB),
                         reads=[sAr, sBr], writes=[sBr])
                    have[16] = (sB, sBr)
                for (p0, p1) in ((0, 64), (64, 128)):
                    win = POOLW[(c * 128 + p0) // 192]
                    sw, swr = have[win]
                    if tile == 0:
                        S.op("pool", (lambda sw, p0, p1, c: lambda e: e.tensor_tensor(
                            out=sw[p0:p1, 16:32], in0=sw[p0:p1, 16:32], in1=cst[p0:p1, 256 + c * 16:256 + c * 16 + 16], op=ALU.mult))(sw, p0, p1, c),
                            reads=[swr, cst_res], writes=[swr])
                    S.op("dve", (lambda sw, z, p0, p1, c, win: lambda e: e.scalar_tensor_tensor(
                        out=pooled[p0:p1, c, :], in0=sw[p0:p1, 16:528], scalar=1.0 / win, in1=z[p0:p1, 16:528],
                        op0=ALU.mult, op1=ALU.subtract))(sw, z, p0, p1, c, win), reads=[swr, zr], writes=[pooled_res[c]])
            done()
            att_proj(1, L1)

            def bd_chunk(cp):
                blk, br = take(L1["BD"][cp])
                b = alloc()
                kcs = bd_kcs(cp)
                def fbd(e):
                    for i, kc in enumerate(kcs):
                        ins = e.matmul(out=ps[:, b, :], lhsT=blk[:, kc * 128:(kc + 1) * 128], rhs=pooled[:, kc, :],
                                       start=(i == 0), stop=(i == len(kcs) - 1))
                    return ins
                S.op("pe", fbd, reads=[br] + [pooled_res[kc] for kc in kcs], writes=[ps_res[b]])
                bg = proj(L1["G"][cp])
                done()
                sgt, sgr = silu_to_tmp(bg)
                S.op("dve", lambda e: e.scalar_tensor_tensor(
                    out=yg[:, cp, :], in0=ps[:, b, :], scalar=chp[:, R_CS, cp:cp + 1], in1=sgt[:, 0:T], op0=ALU.mult, op1=ALU.mult),
                    reads=[ps_res[b], chp_res, sgr], writes=[yg_res[cp]])

            dst = {}

            def d_proj(c):
                bga = proj(L1["GA"][c]); bgb = proj(L1["GB"][c])
                done()
                th, thr = gettmp()
                S.op("act", lambda e: e.activation(out=th[:, 0:T], in_=ps[:, bgb, :], func=AF.Tanh, scale=0.5), reads=[ps_res[bgb]], writes=[thr])
                S.op("pool", lambda e: e.tensor_copy(out=zg[:, c, 0:30], in_=zg[:, c, T:T + 30]), reads=[zg_res[c]], writes=[zg_res[c]])
                S.op("dve", lambda e: e.scalar_tensor_tensor(
                    out=zg[:, c, 30:30 + T], in0=th[:, 0:T], scalar=1.0, in1=ps[:, bga, :], op0=ALU.add, op1=ALU.mult),
                    reads=[thr, ps_res[bga], zg_res[c]], writes=[zg_res[c]])

            def d_conv(c):
                dgb = [take(bid) for bid in L1["DG"][c]]
                bc = alloc()
                def fcv(e):
                    for k in range(31):
                        ins = e.matmul(out=ps[:, bc, :], lhsT=dgb[k // 8][0][:, (k % 8) * 128:(k % 8 + 1) * 128], rhs=zg[:, c, k:k + T],
                                       start=(k == 0), stop=(k == 30))
                    return ins
                S.op("pe", fcv, reads=[x_[1] for x_ in dgb] + [zg_res[c]], writes=[ps_res[bc]])
                done()
                S.op("act", lambda e: e.activation(out=cvs[:, c, :], in_=ps[:, bc, :], func=AF.Identity, bias=chp[:, R_DWB, c:c + 1]),
                     reads=[ps_res[bc], chp_res], writes=[cvs_res[c]])
                sq, sqr = gettmp()
                S.op("pool", lambda e: e.tensor_tensor(out=sq[:, 0:T], in0=cvs[:, c, :], in1=cvs[:, c, :], op=ALU.mult),
                     reads=[cvs_res[c]], writes=[sqr])
                dst[c] = (sq, sqr)

            def d_stat(c):
                if c == 0:
                    dst["bsum"] = alloc(); reserved.add(dst["bsum"])
                    dst["bsq"] = alloc(); reserved.add(dst["bsq"])
                bsum, bsq = dst["bsum"], dst["bsq"]
                sq, sqr = dst[c]
                def fst(e):
                    e.matmul(out=ps[:, bsum, :], lhsT=onesf[:], rhs=cvs[:, c, :], start=(c == 0), stop=(c == 5))
                    return e.matmul(out=ps[:, bsq, :], lhsT=onesf[:], rhs=sq[:, 0:T], start=(c == 0), stop=(c == 5))
                S.op("pe", fst, reads=[onesf_res, cvs_res[c], sqr], writes=[ps_res[bsum], ps_res[bsq]])
                if c == 5:
                    d_f0()

            def d_f0():
                bsum, bsq = dst["bsum"], dst["bsq"]
                mean, meanr = gettmp(); msq, msqr = gettmp()
                dst["mean"] = (mean, meanr); dst["msq"] = (msq, msqr)
                S.op("dve", lambda e: e.tensor_scalar(out=mean[:, 0:T], in0=ps[:, bsum, :], scalar1=1.0 / BW, scalar2=None, op0=ALU.mult),
                     reads=[ps_res[bsum]], writes=[meanr])
                S.op("dve", lambda e: e.tensor_tensor(out=msq[:, 0:T], in0=mean[:, 0:T], in1=mean[:, 0:T], op=ALU.mult), reads=[meanr], writes=[msqr])
                S.op("dve", lambda e: e.scalar_tensor_tensor(out=msq[:, 0:T], in0=ps[:, bsq, :], scalar=1.0 / BW, in1=msq[:, 0:T],
                                                             op0=ALU.mult, op1=ALU.subtract), reads=[ps_res[bsq], msqr], writes=[msqr])
                S.op("dve", lambda e: e.tensor_scalar(out=msq[:, 0:T], in0=msq[:, 0:T], scalar1=EPS, scalar2=None, op0=ALU.add),
                     reads=[msqr], writes=[msqr])
                reserved.discard(bsum); reserved.discard(bsq)

            def d_f1():
                msq, msqr = dst["msq"]
                S.op("act", lambda e: e.activation(out=msq[:, 0:T], in_=msq[:, 0:T], func=AF.Sqrt), reads=[msqr], writes=[msqr])

            def d_f2():
                msq, msqr = dst["msq"]; mean, meanr = dst["mean"]
                S.op("dve", lambda e: e.reciprocal(out=stA[:], in_=msq[:, 0:T]), reads=[msqr], writes=[stA_res])
                S.op("dve", lambda e: e.scalar_tensor_tensor(out=stB[:], in0=mean[:, 0:T], scalar=-1.0, in1=stA[:], op0=ALU.mult, op1=ALU.mult),
                     reads=[meanr, stA_res], writes=[stB_res])

            def d_fn(c):
                t, tr = gettmp()
                S.op("dve", lambda e: e.tensor_tensor(out=t[:, 0:T], in0=cvs[:, c, :], in1=stA[:], op=ALU.mult),
                     reads=[cvs_res[c], stA_res], writes=[tr])
                S.op("pool" if c % 2 == 0 else "dve", lambda e: e.tensor_tensor(out=t[:, 0:T], in0=t[:, 0:T], in1=stB[:], op=ALU.add),
                     reads=[tr, stB_res], writes=[tr])
                S.op("act", lambda e: e.activation(out=zs[:, c, :], in_=t[:, 0:T], func=AF.Silu,
                                                   bias=chp[:, R_LNB, c:c + 1], scale=chp[:, R_LNG, c:c + 1]),
                     reads=[tr, chp_res], writes=[zs_res[c]])

            def pw_chunk(cp):
                blk, br = take(L1["PW"][cp])
                b = alloc()
                def fpw(e):
                    for kc in range(6):
                        ins = e.matmul(out=ps[:, b, :], lhsT=blk[:, kc * 128:(kc + 1) * 128], rhs=zs[:, kc, :], start=(kc == 0), stop=(kc == 5))
                    return ins
                S.op("pe", fpw, reads=[br] + zs_res, writes=[ps_res[b]])
                bg = proj(L1["G"][6 + cp])
                done()
                sgt, sgr = silu_to_tmp(bg)
                S.op("dve", lambda e: e.scalar_tensor_tensor(
                    out=yg[:, 6 + cp, :], in0=ps[:, b, :], scalar=chp[:, R_PWB, cp:cp + 1], in1=sgt[:, 0:T], op0=ALU.add, op1=ALU.mult),
                    reads=[ps_res[b], chp_res, sgr], writes=[yg_res[6 + cp]])

            P = lambda c: (lambda: d_proj(c))
            Cv = lambda c: (lambda: d_conv(c))
            St = lambda c: (lambda: d_stat(c))
            mains = [P(0), P(1), Cv(0), P(2), Cv(1), St(0), P(3), Cv(2), St(1), P(4), Cv(3), St(2), P(5), Cv(4), St(3), Cv(5), St(4), St(5)]
            bd = [(lambda cp=cp: bd_chunk(cp)) for cp in range(6)]
            fn = [(lambda c=c: d_fn(c)) for c in range(6)]
            mains += [d_f1, bd[0], d_f2, bd[1], fn[0], fn[1], bd[2], fn[2], fn[3], bd[3], fn[4], fn[5], bd[4], bd[5]]
            mains += [(lambda cp=cp: pw_chunk(cp)) for cp in range(6)]
            slots = [1, 3, 5, 7, 9, 11, 13, 15, 17, 18, 20, 22]
            if 0 < tile and tile + 1 < nt:
                nw = rms_pre(0, 4, gb[0][:], [gb_res[0]], xts[(tile + 1) % 2], xtrs[(tile + 1) % 2], as_waves=True, hidden=True,
                             dst=(hTa, [list(cvs_res[0:4]) for _ in range(4)]))
                for i_, anchor in enumerate((bd[1], bd[2], bd[3])):
                    mains.insert(mains.index(anchor) + 1, nw[i_])
                p0 = mains.index(bd[5])
                for i_, w_ in enumerate(nw[3:]):
                    mains.insert(min(p0 + 1 + 2 * i_, len(mains)), w_)
                pre_done[tile + 1] = True
            interleave(mains, att_stages(1), slots)
            nxt = ()
            if tile + 1 < nt:
                nxt = ()
            w_out_phase(1, L1, True, tile, xt, xt_res, extra=nxt)
            if tile == 0 and nt > 1:
                if real:
                    flush_stores(0)
                load_x(1)

        class _Null:
            def op(self, *a, **k): pass
            def dma(self, *a, **k): pass
        S_real = S
        S = _Null()
        mode["dry"] = True
        del order[:]
        kv_prologue(0, L0, 2, xts[0], xtrs[0])
        n0 = len(order)
        do_tile(0)
        n1 = len(order)
        if nt > 1:
            do_tile(1)
            per = order[n1:]
            for _ in range(nt - 2):
                order.extend(per)
        mode["dry"] = False
        S = S_real
        bank_ptr[0] = 0; tmp_i[0] = 0; reserved.clear(); pre_done.clear()
        stage_pool["l1"] = [(xts[1][:, k, :], [xtrs[1][k]]) for k in (2, 3, 0, 1)]
        stage_pool["slots"] = stage_pool["l1"] + [
            (cvs[:, 2 * j:2 * j + 2, :].rearrange("p a n -> p (a n)"), [cvs_res[2 * j], cvs_res[2 * j + 1]]) for j in range(3)]
        ring_fill()

        kv_prologue(0, L0, 2, xts[0], xtrs[0])
        for tile in range(nt):
            do_tile(tile)
        flush_stores(0)
        assert rstate["taken"] == len(order), (rstate, len(order))
        S.emit()
    print("sbuf bytes remaining", nc.sbuf_bytes_remaining, flush=True)
    return nc


_NC_CACHE = {}


def kernel(**inputs):
    nt = SEQ // T
    if "nc" not in _NC_CACHE:
        _NC_CACHE["nc"] = build(nt, 2)
    nc = _NC_CACHE["nc"]
    consts = make_consts()
    x = np.ascontiguousarray(inputs["x"], dtype=np.float32)
    mem = np.ascontiguousarray(inputs["mem"], dtype=np.float32)
    shared = {k: np.ascontiguousarray(v, dtype=np.float32) for k, v in inputs.items() if k not in ("x", "mem")}
    shared["consts"] = consts
    in_maps = []
    for b in range(8):
        m = dict(shared)
        m["x"] = x[b]
        m["mem"] = mem[b]
        in_maps.append(m)
    res = run_bass_kernel_spmd(nc, in_maps, core_ids=list(range(8)))
    return np.stack([np.asarray(r["out"], dtype=np.float32) for r in res.results], axis=0)
```

```python
import contextlib
import numpy as np
import concourse.bass as bass
import concourse.mybir as mybir
from concourse.bass_utils import run_bass_kernel_spmd

F32 = mybir.dt.float32
BF16 = mybir.dt.bfloat16
AF = mybir.ActivationFunctionType
ALU = mybir.AluOpType
AX = mybir.AxisListType

D = 1024
SEQ = 4096
NMEM = 256
T = 512
BW = 768
EVEN_IN = 6400
ODD_IN = 4864
EPS = 1e-6
RING = 13
SCALE = 128 ** -0.5
POOLW = (2, 4, 8, 16)


class Res:
    __slots__ = ("w", "r", "name")

    def __init__(self, name=""):
        self.w = None
        self.r = {}
        self.name = name


class Sched:
    ENG = ("pe", "act", "dve", "pool", "sp")

    def __init__(self, nc, es, n_dma_sems=16):
        self.nc = nc
        self.sem = {e: es.enter_context(nc.semaphore("s_" + e)) for e in self.ENG}
        self.cnt = {e: 0 for e in self.ENG}
        self.dsem = [es.enter_context(nc.semaphore("d%d" % i)) for i in range(n_dma_sems)]
        self.dcnt = [0] * n_dma_sems
        self.dnext = 0
        self.prog = {e: [] for e in self.ENG}
        self.seen = {e: {} for e in self.ENG}

    def _semobj(self, key):
        return self.sem[key] if isinstance(key, str) else self.dsem[key]

    def _need(self, eng, ev, waits):
        key, val = ev
        if self.seen[eng].get(key, 0) >= val:
            return
        if waits.get(key, 0) < val:
            waits[key] = val

    def _deps(self, eng, reads, writes):
        waits = {}
        for r in reads:
            if r.w is not None and (r.w[0] != eng or eng in ("act", "dve", "pool")):
                self._need(eng, r.w, waits)
        strict = eng in ("act", "dve", "pool")
        for w in writes:
            if w.w is not None and (w.w[0] != eng or strict):
                self._need(eng, w.w, waits)
            for k, v in w.r.items():
                if k != eng or strict:
                    self._need(eng, (k, v), waits)
        for k, v in waits.items():
            self.seen[eng][k] = v
        return waits

    def op(self, eng, fn, reads=(), writes=()):
        waits = self._deps(eng, reads, writes)
        self.cnt[eng] += 1
        val = self.cnt[eng]
        self.prog[eng].append((list(waits.items()), fn, self.sem[eng], 1))
        for r in reads:
            r.r[eng] = val
        for w in writes:
            w.w = (eng, val)
            w.r = {}

    def dma(self, out_ap, in_ap, reads=(), writes=(), q="sp", **kw):
        k = self.dnext
        self.dnext = (k + 1) % len(self.dsem)
        waits = self._deps(q, reads, writes)
        prev = self.dcnt[k] * 16
        if prev and self.seen[q].get(k, 0) < prev:
            if waits.get(k, 0) < prev:
                waits[k] = prev
            self.seen[q][k] = prev
        self.dcnt[k] += 1
        val = self.dcnt[k] * 16
        self.prog[q].append((list(waits.items()),
                             (lambda e: e.dma_start(out=out_ap, in_=in_ap, **kw)),
                             self.dsem[k], 16))
        for r in reads:
            r.r[k] = val
        for w in writes:
            w.w = (k, val)
            w.r = {}

    def dma_barrier(self, q="sp"):
        waits = []
        for k in range(len(self.dsem)):
            v = self.dcnt[k] * 16
            if v and self.seen[q].get(k, 0) < v:
                waits.append((k, v))
                self.seen[q][k] = v
        self.prog[q].append((waits, None, None, 0))

    def check(self):
        val = {}
        pc = {e: 0 for e in self.ENG}
        progress = True
        while progress:
            progress = False
            for e in self.ENG:
                while pc[e] < len(self.prog[e]):
                    waits, fn, sem, inc = self.prog[e][pc[e]]
                    if any(val.get(k, 0) < v for k, v in waits):
                        break
                    if fn is not None:
                        key = [k for k in list(self.sem) if self.sem[k] is sem]
                        key = key[0] if key else self.dsem.index(sem)
                        val[key] = val.get(key, 0) + inc
                    pc[e] += 1
                    progress = True
        stuck = {e: (pc[e], len(self.prog[e])) for e in self.ENG if pc[e] < len(self.prog[e])}
        if stuck:
            msg = []
            for e, (p, n) in stuck.items():
                waits = self.prog[e][p][0]
                msg.append("%s stuck at %d/%d waiting %s have %s" % (e, p, n, waits, [(k, val.get(k, 0)) for k, _ in waits]))
            raise RuntimeError("DEADLOCK: " + " | ".join(msg))

    def emit(self):
        nc = self.nc
        self.dma_barrier("sp")
        self.check()
        with nc.Block() as block:
            def mk(name):
                def body(e):
                    for waits, fn, sem, inc in self.prog[name]:
                        for key, val in waits:
                            e.wait_ge(self._semobj(key), val)
                        if fn is not None:
                            ins = fn(e)
                            ins.then_inc(sem, inc)
                return body
            block.tensor(mk("pe"))
            block.scalar(mk("act"))
            block.vector(mk("dve"))
            block.gpsimd(mk("pool"))
            block.sync(mk("sp"))


def hnd(t):
    return t.tensor if hasattr(t, "tensor") else t


def _head_segs(c):
    lo, hi = c * 128, c * 128 + 128
    out = []
    for h in range(4):
        a, b = max(lo, h * 192), min(hi, h * 192 + 192)
        if a < b:
            out.append((a - lo, b - lo, h))
    return out


def make_consts():
    c = np.zeros((128, 128 + 128 + 96), np.float32)
    c[:, 0:128] = np.eye(128, dtype=np.float32)
    t = np.arange(128)
    c[:, 128:256] = (t[None, :] <= t[:, None]).astype(np.float32)
    for ch in range(6):
        for p in range(128):
            win = POOLW[(ch * 128 + p) // 192]
            for tt in range(16):
                c[p, 256 + ch * 16 + tt] = win / min(tt + 1, win)
    return c


def build(nt=SEQ // T, nlayers=2):
    seq = nt * T
    nc = bass.Bass("TRN2", target_bir_lowering=False)

    def din(name, shape):
        return nc.dram_tensor(name, list(shape), F32, kind="ExternalInput").ap()

    x_d = din("x", [seq, D])
    mem_d = din("mem", [NMEM, D])
    cst_d = din("consts", [128, 352])
    e_pre_g = din("even_pre_g", [1, D]); e_w_in = din("even_w_in", [1, D, EVEN_IN])
    e_ln_g = din("even_a_ln_g", [1, BW]); e_ln_b = din("even_a_ln_b", [1, BW])
    e_ws = din("even_a_ws", [1, 4, 128, 128]); e_bs = din("even_a_bs", [1, 4, 128])
    e_bconv = din("even_b_conv", [1, 3, BW]); e_mem_g = din("even_mem_g", [1, D])
    e_w_kv = din("even_w_kv", [1, D, D]); e_w_out = din("even_w_out", [1, 2 * D, D]); e_post_g = din("even_post_g", [1, D])
    o_pre_g = din("odd_pre_g", [1, D]); o_w_in = din("odd_w_in", [1, D, ODD_IN])
    o_wgrp = din("odd_c_wgrp", [1, 4, 192, 192]); o_cscale = din("odd_c_scale", [1, BW])
    o_dw_w = din("odd_d_dw_w", [1, 31, BW]); o_dw_b = din("odd_d_dw_b", [1, BW])
    o_ln_g = din("odd_d_ln_g", [1, BW]); o_ln_b = din("odd_d_ln_b", [1, BW])
    o_pw_w = din("odd_d_pw_w", [1, BW, BW]); o_pw_b = din("odd_d_pw_b", [1, BW])
    o_mem_g = din("odd_mem_g", [1, D]); o_w_kv = din("odd_w_kv", [1, D, D])
    o_w_out = din("odd_w_out", [1, 2 * D, D]); o_post_g = din("odd_post_g", [1, D])
    out_d = nc.dram_tensor("out", [seq, D], F32, kind="ExternalOutput").ap()

    blocks = []

    def add_block(spec):
        blocks.append(spec)
        return len(blocks) - 1

    def wblk(w2d, ld, r0_list_stride, A, B, c0):
        return dict(kind="w", w=w2d, A=A, B=B, c0=c0, rs=r0_list_stride)

    w_in0 = e_w_in[0]; w_in1 = o_w_in[0]
    L0 = {}
    L0["VA"] = [add_block(dict(kind="w", w=w_in0, A=2, B=512, c0=768, r0=2 * j * 128)) for j in range(4)]
    L0["VB"] = [add_block(dict(kind="w", w=w_in0, A=4, B=256, c0=1280, r0=4 * j * 128)) for j in range(2)]
    def win_blk(w, col):
        return add_block(dict(kind="w", w=w, A=8, B=128, c0=col, r0=0))
    L0["U"] = [win_blk(w_in0, c * 128) for c in range(6)]
    L0["BG"] = [win_blk(w_in0, 1536 + c * 128) for c in range(6)]
    L0["CG"] = [win_blk(w_in0, 2304 + c * 128) for c in range(6)]
    L0["XI"] = [win_blk(w_in0, 3072 + c * 128) for c in range(6)]
    L0["Q"] = [win_blk(w_in0, 3840 + c * 128) for c in range(4)]
    L0["G"] = [win_blk(w_in0, 4352 + c * 128) for c in range(16)]
    L0["WO"] = [add_block(dict(kind="w", w=e_w_out[0], A=2, B=512, c0=half * 512, r0=2 * ep * 128))
                for half in range(2) for ep in range(8)]
    L0["KV"] = [win_blk(e_w_kv[0], j * 128) for j in range(8)]
    L1 = {}
    L1["ZC"] = [win_blk(w_in1, c * 128) for c in range(6)]
    L1["GA"] = [win_blk(w_in1, 768 + c * 128) for c in range(6)]
    L1["GB"] = [win_blk(w_in1, 1536 + c * 128) for c in range(6)]
    L1["Q"] = [win_blk(w_in1, 2304 + c * 128) for c in range(4)]
    L1["G"] = [win_blk(w_in1, 2816 + c * 128) for c in range(16)]
    L1["BD"] = [add_block(dict(kind="bd", cp=c)) for c in range(6)]
    L1["DG"] = [[add_block(dict(kind="dg", c=c, j=j)) for j in range(4)] for c in range(6)]
    L1["PW"] = [add_block(dict(kind="w", w=o_pw_w[0], A=6, B=128, c0=c * 128, r0=0)) for c in range(6)]
    L1["WO"] = [add_block(dict(kind="w", w=o_w_out[0], A=2, B=512, c0=half * 512, r0=2 * ep * 128))
                for half in range(2) for ep in range(8)]
    L1["KV"] = [win_blk(o_w_kv[0], j * 128) for j in range(8)]
    NBLK = len(blocks)
    wsc = nc.dram_tensor("wsc", [NBLK, 128, 1024], BF16).ap()

    def bd_kcs(cp):
        return sorted({kc for kc in range(6)
                       if any(max(kc * 128, g * 192) < min(kc * 128 + 128, g * 192 + 192) and
                              max(cp * 128, g * 192) < min(cp * 128 + 128, g * 192 + 192) for g in range(4))})

    order = []
    order += L0["KV"]
    for ti_ in range(nt):
        order += L0["VA"] + L0["VB"]
        for c in range(6):
            order += [L0["U"][c], L0["G"][c]]
        for c in range(6):
            order += [L0["CG"][c], L0["XI"][c], L0["BG"][c], L0["G"][6 + c]]
        order += L0["Q"] + L0["G"][12:16] + L0["WO"]
        if nlayers > 1:
            if ti_ == 0:
                order += L1["KV"]
            order += L1["ZC"]
            for c in range(6):
                order += [L1["BD"][c], L1["G"][c]]
            for c in range(6):
                order += [L1["GA"][c], L1["GB"][c]] + L1["DG"][c]
            for c in range(6):
                order += [L1["PW"][c], L1["G"][6 + c]]
            order += L1["Q"] + L1["G"][12:16] + L1["WO"]

    with contextlib.ExitStack() as es:
        S = Sched(nc, es)

        def sb(name, shape, dt=F32):
            return es.enter_context(nc.sbuf_tensor(name, list(shape), dt))

        ring = sb("ring", [128, RING, 1024], BF16); ring_res = [Res("ring%d" % i) for i in range(RING)]
        xts = [sb("xt%d" % j, [128, 4, D]) for j in range(2)]
        xtrs = [[Res("xt%d_%d" % (j, i)) for i in range(4)] for j in range(2)]
        xt = xts[0]; xt_res = xtrs[0]
        hT = sb("hT", [128, 8, T], BF16); hT_res = [Res("hT%d" % i) for i in range(4)]
        H = dict(ap=hT, res=[[r_] for r_ in hT_res])
        hb = [sb("hb%d" % i, [128, D], BF16) for i in range(2)]; hb_res = [Res("hb0"), Res("hb1")]
        yg = sb("yg", [128, 16, T], BF16); yg_res = [Res("yg%d" % i) for i in range(16)]
        NTMP = 6
        tmp = [sb("tmp%d" % i, [128, 528]) for i in range(NTMP)]; tmp_res = [Res("tmp%d" % i) for i in range(NTMP)]
        tmp_i = [0]

        def gettmp():
            i = tmp_i[0]; tmp_i[0] = (i + 1) % NTMP
            return tmp[i], tmp_res[i]

        cst = sb("cst", [128, 352]); cst_res = Res("cst")
        ident = sb("ident", [128, 128], BF16); ident_res = Res("ident")
        onesf = sb("onesf", [128, 128]); onesf_res = Res("onesf")
        ones1 = sb("ones1", [1, 128], BF16); ones1_res = Res("ones1")
        gb = [sb("gb%d" % l, [128, 8, 128]) for l in range(2)]; gb_res = [Res("gb0"), Res("gb1")]
        gcol = sb("gcol", [128, 4, 8]); gcol_res = Res("gcol")
        pgb = [sb("pgb%d" % l, [128, D]) for l in range(2)]; pgb_res = [Res("pgb0"), Res("pgb1")]
        lngb = sb("lngb", [128, BW]); lnbb = sb("lnbb", [128, BW]); lnp_res = Res("lnp")
        wmT = sb("wmT", [128, 4, 128], BF16); wmT_res = Res("wmT")
        bsrow = sb("bsrow", [1, 512], BF16); bsrow_res = Res("bsrow")
        chp = sb("chp", [128, 40, 6]); chp_res = Res("chp")
        kT = [sb("kT%d" % l, [128, 4, NMEM], BF16) for l in range(2)]; kT_res = [Res("kT0"), Res("kT1")]
        vv = [sb("vv%d" % l, [128, 2, 512], BF16) for l in range(2)]; vv_res = [Res("vv0"), Res("vv1")]
        ssq = sb("ssq", [128, 8]); ssq_res = Res("ssq")
        ms4 = sb("ms4", [128, 4]); ms4_res = Res("ms4")
        rstd4 = sb("rstd4", [128, 4]); rstd4_res = Res("rstd4")
        vln = sb("vln", [128, 4, BW], BF16); vln_res = [Res("vln%d" % i) for i in range(4)]
        vn = [sb("vn%d" % i, [128, BW]) for i in range(2)]; vn_res = [Res("vn0"), Res("vn1")]
        st6 = [sb("st6_%d" % i, [128, 3, 6]) for i in range(4)]; st6_res = [Res("st6%d" % i) for i in range(4)]
        mv = [sb("mv%d" % i, [128, 4]) for i in range(4)]; mv_res = [Res("mv%d" % i) for i in range(4)]
        pbh = sb("pbh", [128, 6, 2]); pbh_res = [Res("pbh%d" % i) for i in range(6)]
        qT = sb("qT", [128, 4, T], BF16); qT_res = [Res("qT%d" % i) for i in range(4)]
        sgx = sb("sgx", [128, 4, T]); sgx_res = [Res("sgx%d" % i) for i in range(4)]
        ebuf = sb("ebuf", [128, 4, NMEM]); ebuf_res = Res("ebuf")
        pbuf = sb("pbuf", [128, 4, NMEM], BF16); pbuf_res = Res("pbuf")
        pts = sb("pts", [128, 1024], BF16); pts_res = Res("pts")
        at_s = [sb("at_s%d" % i, [128, 16]) for i in range(2)]; at_res = [Res("at0"), Res("at1")]
        zbh = sb("zbh", [128, 6, 16]); zbh_res = [Res("zbh%d" % i) for i in range(6)]
        pooled = sb("pooled", [128, 6, T], BF16); pooled_res = [Res("pooled%d" % i) for i in range(6)]
        zg = sb("zg", [128, 6, 30 + T], BF16); zg_res = [Res("zg%d" % i) for i in range(6)]
        cvs = sb("cvs", [128, 6, T]); cvs_res = [Res("cvs%d" % i) for i in range(6)]
        zs = sb("zs", [128, 6, T], BF16); zs_res = [Res("zs%d" % i) for i in range(6)]
        stA = sb("stA", [128, T]); stA_res = Res("stA")
        stB = sb("stB", [128, T]); stB_res = Res("stB")
        junk = sb("junk", [128, D], BF16); junk_res = Res("junk")
        ps = es.enter_context(nc.psum_tensor("ps", [128, 8, 512], F32))
        psb = ps.bitcast(BF16)
        ps_res = [Res("ps%d" % i) for i in range(8)]
        bank_ptr = [0]
        reserved = set()

        def alloc(n=1):
            while True:
                b = bank_ptr[0]
                if n == 2 and b % 2:
                    b += 1
                if b + n > 8:
                    b = 0
                bank_ptr[0] = (b + n) % 8
                if all((b + i) not in reserved for i in range(n)):
                    return b

        scr_res = Res("scr")
        out_res = Res("out")

        rstate = dict(loaded=0, taken=0, released=0)
        defer = dict(on=False)

        def blk_cols(bid):
            sp_ = blocks[bid]
            if sp_["kind"] == "w":
                return sp_["A"] * sp_["B"]
            return 768 if sp_["kind"] == "bd" else 1024

        prepped = set()
        stage_pool = dict(slots=[], i=0)
        pend_store = []

        def flush_stores(keep):
            while len(pend_store) > keep:
                bid, slot, nb_ = pend_store.pop(0)
                S.dma(wsc[bid][:, 0:nb_], ring[:, slot, 0:nb_], reads=[ring_res[slot]], writes=[scr_res_b[bid]])

        def ring_fill():
            while rstate["loaded"] < len(order) and rstate["loaded"] < rstate["released"] + RING:
                n = rstate["loaded"]
                slot = n % RING
                bid = order[n]
                nb_ = blk_cols(bid)
                if bid not in prepped:
                    sl = stage_pool["slots"]
                    st32, r32 = sl[stage_pool["i"] % len(sl)]
                    stage_pool["i"] += 1
                    prep_a(bid, st32, r32, ring[:, slot, :], [ring_res[slot]])
                    prepped.add(bid)
                    pend_store.append((bid, slot, nb_))
                    flush_stores(2)
                else:
                    flush_stores(0)
                    S.dma(ring[:, slot, 0:nb_], wsc[bid][:, 0:nb_], reads=[scr_res_b[bid]], writes=[ring_res[slot]])
                rstate["loaded"] += 1

        mode = dict(dry=False)

        def take(expect):
            if mode["dry"]:
                order.append(expect)
                return ring[:, 0, :], ring_res[0]
            n = rstate["taken"]
            assert order[n] == expect, (n, order[n], expect)
            assert n < rstate["loaded"], "ring too shallow"
            rstate["taken"] += 1
            slot = n % RING
            return ring[:, slot, :], ring_res[slot]

        def done():
            if mode["dry"]:
                return
            rstate["released"] = rstate["taken"]
            ring_fill()

        def bcast_rows(dram_row, n):
            return bass.AP(dram_row.tensor, dram_row.offset, [[0, 128], [1, n]])

        S.dma(cst[:], cst_d, writes=[cst_res])
        S.op("dve", lambda e: e.tensor_copy(out=ident[:], in_=cst[:, 0:128]), reads=[cst_res], writes=[ident_res])
        nh1 = sb("nh1", [128, 4]); nh1_res = Res("nh1")
        S.op("pool", lambda e: e.memset(nh1[:], -0.5), writes=[nh1_res])
        S.op("pool", lambda e: e.memset(onesf[:], 1.0), writes=[onesf_res])
        S.op("pool", lambda e: e.memset(ones1[:], 1.0), writes=[ones1_res])
        S.op("pool", lambda e: e.memset(zg[:], 0.0), writes=zg_res)
        S.op("pool", lambda e: e.memset(pbh[:], 0.0), writes=pbh_res)
        S.op("pool", lambda e: e.memset(zbh[:], 0.0), writes=zbh_res)
        jdummy = sb("jdummy", [128, 4])
        _g_rows = []
        for i, g in enumerate((e_pre_g, o_pre_g, e_mem_g, o_mem_g)):
            rr = Res("gcolrow%d" % i); _g_rows.append(rr)
            S.dma(gcol[:, i, :], g[0].rearrange("(c p) -> p c", p=128), writes=[rr],
                  allow_slow_non_contiguous=True)
        S.op("pool", lambda e: e.memset(jdummy[:, 0:1], 0.0), reads=_g_rows, writes=[gcol_res])
        for l in range(2):
            src = bass.AP(hnd(gcol), l * 8, [[32, 128], [1, 8], [0, 128]])
            S.op("dve", (lambda src, l: lambda e: e.tensor_copy(out=gb[l][:], in_=src))(src, l),
                 reads=[gcol_res], writes=[gb_res[l]])
        for l, g in enumerate((e_post_g, o_post_g)):
            S.dma(pgb[l][:], bcast_rows(g, D), writes=[pgb_res[l]])
        _lr = [Res("lnrow0"), Res("lnrow1")]
        S.dma(lngb[:], bcast_rows(e_ln_g, BW), writes=[_lr[0]])
        S.dma(lnbb[:], bcast_rows(e_ln_b, BW), writes=[_lr[1]])
        S.op("pool", lambda e: e.memset(jdummy[:, 1:2], 0.0), reads=_lr, writes=[lnp_res])
        prow = [e_bconv[0, k] for k in range(3)] + [o_cscale[0]] + [o_dw_w[0, k] for k in range(31)] + \
               [o_dw_b[0], o_ln_g[0], o_ln_b[0], o_pw_b[0]]
        _c_rows = []
        for r, src in enumerate(prow):
            rr = Res("chprow%d" % r); _c_rows.append(rr)
            S.dma(chp[:, r, :], src.rearrange("(c p) -> p c", p=128), writes=[rr],
                  allow_slow_non_contiguous=True)
        S.op("pool", lambda e: e.memset(jdummy[:, 2:3], 0.0), reads=_c_rows, writes=[chp_res])
        R_BC, R_CS, R_DW, R_DWB, R_LNG, R_LNB, R_PWB = 0, 3, 4, 35, 36, 37, 38
        wnat, wnat_res = tmp[0], tmp_res[0]
        S.dma(wnat[:, 0:512].rearrange("p (h s) -> p h s", h=4), e_ws[0].rearrange("h t s -> t h s"), writes=[wnat_res])
        for h in range(4):
            S.op("dve", (lambda h: lambda e: e.tensor_tensor(out=hb[0][:, h * 128:(h + 1) * 128], in0=wnat[:, h * 128:(h + 1) * 128],
                                                            in1=cst[:, 128:256], op=ALU.mult))(h),
                 reads=[wnat_res, cst_res], writes=[hb_res[0]])
        b = alloc()
        def _wtr(e, b=b):
            for h in range(4):
                ins = e.transpose(out=psb[:, b, h * 128:(h + 1) * 128], in_=hb[0][:, h * 128:(h + 1) * 128], identity=ident[:])
            return ins
        S.op("pe", _wtr, reads=[hb_res[0], ident_res], writes=[ps_res[b]])
        S.op("act", (lambda b: lambda e: e.activation(out=wmT[:].rearrange("p h t -> p (h t)"), in_=psb[:, b, 0:512], func=AF.Copy))(b),
             reads=[ps_res[b]], writes=[wmT_res])
        S.dma(tmp[1][0:1, 0:512], e_bs[0].rearrange("h t -> (h t)").rearrange("(o n) -> o n", o=1), writes=[tmp_res[1]])
        S.op("dve", lambda e: e.tensor_copy(out=bsrow[:], in_=tmp[1][0:1, 0:512]), reads=[tmp_res[1]], writes=[bsrow_res])

        cast_ctr = [0]

        def prep_a(bi, st32, st32_res, st16, st16_res):
            spec = blocks[bi]
            ce = ("act", "dve")[cast_ctr[0] % 2]
            cast_ctr[0] += 1
            if spec["kind"] == "w":
                w = spec["w"]; A = spec["A"]; B = spec["B"]; c0 = spec["c0"]; r0 = spec["r0"]
                src = w[r0:r0 + A * 128, c0:c0 + B].rearrange("(a p) n -> p a n", p=128)
                S.dma(st32[:, 0:A * B].rearrange("p (a n) -> p a n", a=A), src, writes=st32_res)
                n = A * B
                if ce == "act":
                    S.op("act", lambda e: e.activation(out=st16[:, 0:n], in_=st32[:, 0:n], func=AF.Copy), reads=st32_res, writes=st16_res)
                else:
                    S.op("dve", lambda e: e.tensor_copy(out=st16[:, 0:n], in_=st32[:, 0:n]), reads=st32_res, writes=st16_res)
                return n
            elif spec["kind"] == "bd":
                cp = spec["cp"]
                S.op("pool", lambda e: e.memset(st32, 0.0), writes=st32_res)
                for kc in range(6):
                    for g in range(4):
                        ra, rb = max(kc * 128, g * 192), min(kc * 128 + 128, g * 192 + 192)
                        ca, cb = max(cp * 128, g * 192), min(cp * 128 + 128, g * 192 + 192)
                        if ra < rb and ca < cb:
                            dst = st32[ra - kc * 128:rb - kc * 128, kc * 128 + (ca - cp * 128):kc * 128 + (cb - cp * 128)]
                            S.dma(dst, o_wgrp[0, g, ra - g * 192:rb - g * 192, ca - g * 192:cb - g * 192],
                                  reads=st32_res, writes=st32_res)
                S.op("dve", lambda e: e.tensor_copy(out=st16[:, 0:768], in_=st32[:, 0:768]), reads=st32_res, writes=st16_res)
                return 768
            else:
                c = spec["c"]; jj = spec["j"]
                def _dg(e):
                    ins = None
                    for kk in range(8):
                        k_ = jj * 8 + kk
                        if k_ < 31:
                            ins = e.tensor_scalar(out=st16[:, kk * 128:(kk + 1) * 128], in0=cst[:, 0:128],
                                                  scalar1=chp[:, R_DW + k_, c:c + 1], scalar2=0.5, op0=ALU.mult, op1=ALU.mult)
                        else:
                            ins = e.memset(st16[:, kk * 128:(kk + 1) * 128], 0.0)
                    return ins
                S.op("dve", _dg, reads=[cst_res, chp_res], writes=st16_res)
                return 1024

        scr_res_b = [Res("scr%d" % i) for i in range(NBLK)]

        class Prepper:
            def __init__(self, ids, s32, s16, la):
                self.ids = list(ids); self.s32 = s32; self.s16 = s16; self.la = la
                self.na = 0; self.nb = 0; self.n = {}; self.base = 0
                self.stored = set()

            def _a(self):
                i = self.na
                bi = self.ids[i]
                st32, r32 = self.s32[(i - self.base) % len(self.s32)]
                st16, r16 = self.s16[(i - self.base) % len(self.s16)]
                self.n[i] = prep_a(bi, st32, r32, st16, r16)
                self.na += 1

            def step(self):
                if self.nb >= len(self.ids):
                    return False
                while self.na < len(self.ids) and self.na < self.nb + max(self.la, 1):
                    self._a()
                i = self.nb
                bi = self.ids[i]
                st16, r16 = self.s16[(i - self.base) % len(self.s16)]
                n = self.n[i]
                S.dma(wsc[bi][:, 0:n], st16[:, 0:n], reads=r16, writes=[scr_res_b[bi]])
                self.stored.add(bi)
                self.nb += 1
                return True

            def ensure(self, bi):
                while bi not in self.stored:
                    assert self.step()

            def switch(self, s32, s16, la):
                old_la = self.la
                self.la = 0
                while self.nb < self.na:
                    self.step()
                self.s32 = s32; self.s16 = s16; self.la = la
                self.base = self.nb

        def pair16(buf, res, j):
            return (buf[:, 2 * j:2 * j + 2, :].rearrange("p a n -> p (a n)"), [res[2 * j], res[2 * j + 1]])

        stsm = [sb("stsm%d" % i, [128, 4]) for i in range(4)]; stsm_res = [Res("stsm%d" % i) for i in range(4)]

        def rms_pre(l, nsub, g_tile, g_res, xt, xt_res, as_waves=False, dst=None, hidden=False):
            dst_ap, dst_res = (hT, [[r_] for r_ in hT_res]) if dst is None else dst
            def w_sq(sub):
                m = stsm[sub]
                S.op("act", lambda e: e.activation(out=junk[:], in_=xt[:, sub, :], func=AF.Square, accum_out=m[:, 0:1]),
                     reads=[xt_res[sub]], writes=[junk_res, stsm_res[sub]])

            def w_stat(sub):
                m = stsm[sub]; mr = stsm_res[sub]
                if hidden:
                    S.op("dve", lambda e: e.tensor_scalar(out=m[:, 1:2], in0=m[:, 0:1], scalar1=1.0 / D, scalar2=EPS, op0=ALU.mult, op1=ALU.add),
                         reads=[mr], writes=[mr])
                    S.op("pool", lambda e: e.tensor_tensor(out=m[:, 2:3], in0=m[:, 1:2], in1=nh1[:, 0:1], op=ALU.pow),
                         reads=[mr, nh1_res], writes=[mr])
                else:
                    S.op("act", lambda e: e.activation(out=m[:, 1:2], in_=m[:, 0:1], func=AF.Sqrt, bias=EPS, scale=1.0 / D), reads=[mr], writes=[mr])
                    S.op("dve", lambda e: e.reciprocal(out=m[:, 2:3], in_=m[:, 1:2]), reads=[mr], writes=[mr])

            def w_scale(sub):
                m = stsm[sub]; mr = stsm_res[sub]
                hbt, hbr = hb[sub % 2], hb_res[sub % 2]
                if hidden:
                    S.op("dve", lambda e: e.tensor_scalar(out=hbt[:], in0=xt[:, sub, :], scalar1=m[:, 2:3], scalar2=None, op0=ALU.mult),
                         reads=[xt_res[sub], mr], writes=[hbr])
                else:
                    S.op("act", lambda e: e.activation(out=hbt[:], in_=xt[:, sub, :], func=AF.Copy, scale=m[:, 2:3]),
                         reads=[xt_res[sub], mr], writes=[hbr])

            def w_tr(sub):
                hbt, hbr = hb[sub % 2], hb_res[sub % 2]
                b = alloc()
                def _tr(e):
                    for kc in range(8):
                        ins = e.transpose(out=psb[:, b, kc * 128:(kc + 1) * 128], in_=hbt[:, kc * 128:(kc + 1) * 128], identity=ident[:])
                    return ins
                S.op("pe", _tr, reads=[hbr, ident_res], writes=[ps_res[b]])
                S.op("dve", lambda e: e.tensor_tensor(
                    out=dst_ap[:, :, sub * 128:(sub + 1) * 128], in0=psb[:, b, :].rearrange("p (k t) -> p k t", k=8),
                    in1=g_tile, op=ALU.mult), reads=[ps_res[b]] + g_res, writes=list(dst_res[sub]))

            def wave(k):
                if 0 <= k - 3 < nsub:
                    w_tr(k - 3)
                if 0 <= k - 2 < nsub:
                    w_scale(k - 2)
                if 0 <= k - 1 < nsub:
                    w_stat(k - 1)
                if k < nsub:
                    w_sq(k)
            waves = [(lambda k=k: wave(k)) for k in range(nsub + 3)]
            if as_waves:
                return waves
            for w in waves:
                w()

        def mm_fm(blk, blk_res, b, nk=8):
            hcur = H["ap"]
            hres = []
            for rl in H["res"]:
                for r_ in rl:
                    if r_ not in hres:
                        hres.append(r_)
            def f(e):
                for kc in range(nk):
                    ins = e.matmul(out=ps[:, b, :], lhsT=blk[:, kc * 128:(kc + 1) * 128], rhs=hcur[:, kc, :],
                                   start=(kc == 0), stop=(kc == nk - 1))
                return ins
            S.op("pe", f, reads=[blk_res] + hres, writes=[ps_res[b]])

        def proj(bid):
            blk, br = take(bid)
            b = alloc()
            mm_fm(blk, br, b)
            return b

        def silu_to_tmp(b):
            t, tr = gettmp()
            S.op("act", lambda e: e.activation(out=t[:, 0:T], in_=ps[:, b, :], func=AF.Silu), reads=[ps_res[b]], writes=[tr])
            return t, tr

        mg = sgx[:, 0:2, :].rearrange("p a (k t) -> p (a k) t", k=4); mg_res = [sgx_res[0], sgx_res[1]]

        def kv_prologue(l, LB, memg_idx, xt, xt_res):
            S.dma(xt[:, 0:2, :], mem_d.rearrange("(s p) d -> p s d", p=128), writes=[xt_res[0], xt_res[1]])
            src = bass.AP(hnd(gcol), memg_idx * 8, [[32, 128], [1, 8], [0, 128]])
            S.op("dve", lambda e: e.tensor_copy(out=mg, in_=src), reads=[gcol_res], writes=mg_res)
            rms_pre(l, 2, mg, mg_res, xt, xt_res)
            kvb = [take(bid) for bid in LB["KV"]]
            for h in range(4):
                b = alloc()
                def f(e, h=h, b=b):
                    for kc in range(8):
                        ins = e.matmul(out=ps[:, b, 0:NMEM], lhsT=kvb[h][0][:, kc * 128:(kc + 1) * 128], rhs=hT[:, kc, 0:NMEM],
                                       start=(kc == 0), stop=(kc == 7))
                    return ins
                S.op("pe", f, reads=[kvb[h][1], hT_res[0], hT_res[1]], writes=[ps_res[b]])
                S.op("act", (lambda h, b: lambda e: e.activation(out=kT[l][:, h, :], in_=ps[:, b, 0:NMEM], func=AF.Copy))(h, b),
                     reads=[ps_res[b]], writes=[kT_res[l]])
            for mc in range(2):
                b = alloc()
                def f(e, mc=mc, b=b):
                    for j in range(4):
                        for kc in range(8):
                            ins = e.matmul(out=ps[:, b, j * 128:(j + 1) * 128], lhsT=hT[:, kc, mc * 128:(mc + 1) * 128],
                                           rhs=kvb[4 + j][0][:, kc * 128:(kc + 1) * 128], start=(kc == 0), stop=(kc == 7))
                    return ins
                S.op("pe", f, reads=[kvb[4 + j][1] for j in range(4)] + [hT_res[mc]], writes=[ps_res[b]])
                S.op("act", (lambda mc, b: lambda e: e.activation(out=vv[l][:, mc, :], in_=ps[:, b, :], func=AF.Copy))(mc, b),
                     reads=[ps_res[b]], writes=[vv_res[l]])
            done()


        pbufs = [pbuf, sb("pbufB", [128, 4, NMEM], BF16)]; pbufs_res = [pbuf_res, Res("pbufB")]

        def att_proj(l, LB):
            for h in range(4):
                b = proj(LB["Q"][h])
                S.op("act", (lambda h, b: lambda e: e.activation(out=qT[:, h, :], in_=ps[:, b, :], func=AF.Copy))(h, b),
                     reads=[ps_res[b]], writes=[qT_res[h]])
            done()
            for h in range(4):
                b = proj(LB["G"][12 + h])
                S.op("act", (lambda h, b: lambda e: e.activation(out=sgx[:, h, :], in_=ps[:, b, :], func=AF.Silu))(h, b),
                     reads=[ps_res[b]], writes=[sgx_res[h]])
            done()

        def att_stages(l):
            def s1(sub):
                ts = slice(sub * 128, (sub + 1) * 128)
                b2 = alloc(2)
                def fq(e):
                    for h in range(4):
                        ins = e.matmul(out=ps[:, b2 + h // 2, (h % 2) * 256:(h % 2) * 256 + 256], lhsT=qT[:, h, ts], rhs=kT[l][:, h, :],
                                       start=True, stop=True)
                    return ins
                S.op("pe", fq, reads=qT_res + [kT_res[l]], writes=[ps_res[b2], ps_res[b2 + 1]])
                a, ar = at_s[sub % 2], at_res[sub % 2]
                pb, pbr = pbufs[sub % 2], pbufs_res[sub % 2]
                sc4 = ps[:, b2:b2 + 2, :].rearrange("p b (h m) -> p (b h) m", h=2)
                S.op("dve", lambda e: e.tensor_reduce(out=a[:, 0:4], in_=sc4, axis=AX.X, op=ALU.max),
                     reads=[ps_res[b2], ps_res[b2 + 1]], writes=[ar])
                S.op("dve", lambda e: e.tensor_scalar(out=a[:, 4:8], in0=a[:, 0:4], scalar1=-SCALE, scalar2=None, op0=ALU.mult),
                     reads=[ar], writes=[ar])
                def fe(e):
                    for h in range(4):
                        ins = e.activation(out=ebuf[:, h, :], in_=ps[:, b2 + h // 2, (h % 2) * 256:(h % 2) * 256 + 256], func=AF.Exp,
                                           bias=a[:, 4 + h:5 + h], scale=SCALE, accum_out=a[:, 8 + h:9 + h])
                    return ins
                S.op("act", fe, reads=[ps_res[b2], ps_res[b2 + 1], ar], writes=[ebuf_res, ar])
                S.op("dve", lambda e: e.reciprocal(out=a[:, 12:16], in_=a[:, 8:12]), reads=[ar], writes=[ar])
                def fn_(e):
                    for h in range(4):
                        ins = e.tensor_scalar(out=pb[:, h, :], in0=ebuf[:, h, :], scalar1=a[:, 12 + h:13 + h], scalar2=None, op0=ALU.mult)
                    return ins
                S.op("dve", fn_, reads=[ebuf_res, ar], writes=[pbr])

            def s2a(sub):
                pb, pbr = pbufs[sub % 2], pbufs_res[sub % 2]
                bT = alloc()
                def ft(e):
                    for mc in range(2):
                        for h in range(4):
                            ins = e.transpose(out=psb[:, bT, (mc * 4 + h) * 128:(mc * 4 + h + 1) * 128],
                                              in_=pb[:, h, mc * 128:(mc + 1) * 128], identity=ident[:])
                    return ins
                S.op("pe", ft, reads=[pbr, ident_res], writes=[ps_res[bT]])
                S.op("dve", lambda e: e.tensor_copy(out=pts[:], in_=psb[:, bT, :]), reads=[ps_res[bT]], writes=[pts_res])

            def s2b(sub):
                ts = slice(sub * 128, (sub + 1) * 128)
                bO = alloc()
                def fo(e):
                    for h in range(4):
                        for mc in range(2):
                            ins = e.matmul(out=ps[:, bO, h * 128:(h + 1) * 128], lhsT=vv[l][:, mc, h * 128:(h + 1) * 128],
                                           rhs=pts[:, (mc * 4 + h) * 128:(mc * 4 + h + 1) * 128], start=(mc == 0), stop=(mc == 1))
                    return ins
                S.op("pe", fo, reads=[pts_res, vv_res[l]], writes=[ps_res[bO]])
                S.op("dve", lambda e: e.tensor_tensor(out=yg[:, 12:16, ts], in0=ps[:, bO, :].rearrange("p (h t) -> p h t", h=4),
                                                      in1=sgx[:, :, ts], op=ALU.mult),
                     reads=[ps_res[bO]] + sgx_res, writes=yg_res[12:16])

            fmap = {"s1": s1, "s2a": s2a, "s2b": s2b}
            seq = [("s1", 0), ("s1", 1), ("s2a", 0), ("s1", 2), ("s2b", 0), ("s2a", 1), ("s1", 3), ("s2b", 1),
                   ("s2a", 2), ("s2b", 2), ("s2a", 3), ("s2b", 3)]
            return [(lambda k=k, sub=sub: fmap[k](sub)) for k, sub in seq]

        def interleave(mains, fillers, slots):
            fi = 0
            for i, m in enumerate(mains):
                m()
                for _ in range(slots.count(i)):
                    if fi < len(fillers):
                        fillers[fi](); fi += 1
            while fi < len(fillers):
                fillers[fi](); fi += 1

        warm = sb("warm", [128, 4]); warm_res = Res("warm")
        S.op("pool", lambda e: e.memset(warm[:], 1.0), writes=[warm_res])

        def w_out_phase(l, LB, last_layer, tile, xt, xt_res, extra=(), after_sub=()):
            bank_ptr[0] = 0
            after_sub = list(after_sub)
            for half in range(2):
                for ep in range(8):
                    blk, br = take(LB["WO"][half * 8 + ep])
                    def f(e, blk=blk, half=half, ep=ep):
                        for el in range(2):
                            ee = 2 * ep + el
                            for sub in range(4):
                                ins = e.matmul(out=ps[:, 2 * sub + half, :], lhsT=yg[:, ee, sub * 128:(sub + 1) * 128],
                                               rhs=blk[:, el * 512:(el + 1) * 512], start=(ee == 0), stop=(ee == 15))
                        return ins
                    S.op("pe", f, reads=[br, yg_res[2 * ep], yg_res[2 * ep + 1]], writes=[ps_res[2 * s_ + half] for s_ in range(4)])
                    done()
                def fs(e, half=half):
                    for sub in range(4):
                        dmy = (junk[:, 0:512], junk[:, 512:1024], hb[0][:, 0:512], hb[1][:, 0:512])[sub]
                        ins = e.activation(out=dmy, in_=ps[:, 2 * sub + half, :], func=AF.Square,
                                           accum_out=ssq[:, half * 4 + sub:half * 4 + sub + 1])
                    return ins
                S.op("act", fs, reads=[ps_res[2 * s_ + half] for s_ in range(4)], writes=[junk_res, ssq_res, hb_res[0], hb_res[1]])
                if half == 0:
                    S.op("act", lambda e: e.activation(out=warm[:, 0:1], in_=warm[:, 1:2], func=AF.Sqrt), reads=[warm_res], writes=[warm_res])
            S.op("dve", lambda e: e.tensor_tensor(out=ms4[:], in0=ssq[:, 0:4], in1=ssq[:, 4:8], op=ALU.add), reads=[ssq_res], writes=[ms4_res])
            S.op("act", lambda e: e.activation(out=ms4[:], in_=ms4[:], func=AF.Sqrt, bias=EPS, scale=1.0 / D), reads=[ms4_res], writes=[ms4_res])
            S.op("dve", lambda e: e.reciprocal(out=rstd4[:], in_=ms4[:]), reads=[ms4_res], writes=[rstd4_res])
            for sub in range(4):
                for half in range(2):
                    t, tr = gettmp()
                    hs = slice(half * 512, (half + 1) * 512)
                    S.op("dve", (lambda t, sub, half, hs: lambda e: e.scalar_tensor_tensor(
                        out=t[:, 0:512], in0=ps[:, 2 * sub + half, :], scalar=rstd4[:, sub:sub + 1], in1=pgb[l][:, hs],
                        op0=ALU.mult, op1=ALU.mult))(t, sub, half, hs),
                        reads=[ps_res[2 * sub + half], rstd4_res, pgb_res[l]], writes=[tr])
                    S.op("pool", (lambda t, sub, hs: lambda e: e.tensor_tensor(out=xt[:, sub, hs], in0=xt[:, sub, hs], in1=t[:, 0:512], op=ALU.add))(t, sub, hs),
                         reads=[tr, xt_res[sub]], writes=[xt_res[sub]])
                if after_sub:
                    after_sub.pop(0)()
            while after_sub:
                after_sub.pop(0)()
            if last_layer:
                S.dma(out_d[tile * T:(tile + 1) * T, :].rearrange("(s p) d -> p s d", p=128), xt[:, :, :], reads=xt_res, writes=[out_res])

        hTa = cvs[:, 0:4, :].rearrange("p a n -> p (a n)").bitcast(BF16).rearrange("p (k t) -> p k t", k=8)
        pre_done = {}

        def load_x(tile):
            S.dma(xts[tile % 2][:, :, :], x_d[tile * T:(tile + 1) * T, :].rearrange("(s p) d -> p s d", p=128), writes=xtrs[tile % 2])

        def do_tile(tile):
            xt = xts[tile % 2]; xt_res = xtrs[tile % 2]
            real = not mode["dry"]
            if tile == 0:
                load_x(0)
            if pre_done.get(tile):
                H["ap"] = hTa; H["res"] = [list(cvs_res[0:4]) for _ in range(4)]
            else:
                H["ap"] = hT; H["res"] = [[r_] for r_ in hT_res]
                rms_pre(0, 4, gb[0][:], [gb_res[0]], xt, xt_res)
            vab = [take(bid) for bid in L0["VA"]]
            vbb = [take(bid) for bid in L0["VB"]]
            for sub in range(4):
                ts = slice(sub * 128, (sub + 1) * 128)
                b2 = alloc(2)
                def fv(e, ts=ts, b2=b2, hcur=H["ap"]):
                    for kc in range(8):
                        e.matmul(out=ps[:, b2, :], lhsT=hcur[:, kc, ts], rhs=vab[kc // 2][0][:, (kc % 2) * 512:(kc % 2) * 512 + 512],
                                 start=(kc == 0), stop=(kc == 7))
                        ins = e.matmul(out=ps[:, b2 + 1, 0:256], lhsT=hcur[:, kc, ts], rhs=vbb[kc // 4][0][:, (kc % 4) * 256:(kc % 4) * 256 + 256],
                                       start=(kc == 0), stop=(kc == 7))
                    return ins
                S.op("pe", fv, reads=[x_[1] for x_ in vab + vbb] + list(H["res"][sub]), writes=[ps_res[b2], ps_res[b2 + 1]])
                def fbn(e, sub=sub, b2=b2):
                    e.bn_stats(out=st6[sub][:, 0, :], in_=ps[:, b2, 0:256])
                    e.bn_stats(out=st6[sub][:, 1, :], in_=ps[:, b2, 256:512])
                    return e.bn_stats(out=st6[sub][:, 2, :], in_=ps[:, b2 + 1, 0:256])
                S.op("dve", fbn, reads=[ps_res[b2], ps_res[b2 + 1]], writes=[st6_res[sub]])
                m = mv[sub]; mr = mv_res[sub]
                S.op("dve", (lambda sub, m: lambda e: e.bn_aggr(out=m[:, 0:2], in_=st6[sub][:]))(sub, m), reads=[st6_res[sub]], writes=[mr])
                S.op("dve", (lambda m: lambda e: e.tensor_scalar(out=m[:, 2:3], in0=m[:, 1:2], scalar1=EPS, scalar2=None, op0=ALU.add))(m),
                     reads=[mr], writes=[mr])
                S.op("act", (lambda m: lambda e: e.activation(out=m[:, 2:3], in_=m[:, 2:3], func=AF.Sqrt))(m), reads=[mr], writes=[mr])
                S.op("dve", (lambda m: lambda e: e.reciprocal(out=m[:, 3:4], in_=m[:, 2:3]))(m), reads=[mr], writes=[mr])
                S.op("dve", (lambda m: lambda e: e.tensor_scalar(out=m[:, 2:3], in0=m[:, 0:1], scalar1=m[:, 3:4], scalar2=-1.0,
                                                                 op0=ALU.mult, op1=ALU.mult))(m), reads=[mr], writes=[mr])
                vt, vr = vn[sub % 2], vn_res[sub % 2]
                def fva(e, m=m, vt=vt, b2=b2):
                    e.activation(out=vt[:, 0:512], in_=ps[:, b2, :], func=AF.Identity, bias=m[:, 2:3], scale=m[:, 3:4])
                    return e.activation(out=vt[:, 512:768], in_=ps[:, b2 + 1, 0:256], func=AF.Identity, bias=m[:, 2:3], scale=m[:, 3:4])
                S.op("act", fva, reads=[ps_res[b2], ps_res[b2 + 1], mr], writes=[vr])
                S.op("pool", (lambda vt: lambda e: e.tensor_tensor(out=vt[:], in0=vt[:], in1=lngb[:], op=ALU.mult))(vt),
                     reads=[vr, lnp_res], writes=[vr])
                S.op("pool", (lambda vt, sub: lambda e: e.tensor_tensor(out=vln[:, sub, :], in0=vt[:], in1=lnbb[:], op=ALU.add))(vt, sub),
                     reads=[vr, lnp_res], writes=[vln_res[sub]])
            done()
            att_proj(0, L0)

            def a_chunk(c):
                bu = proj(L0["U"][c])
                bg = proj(L0["G"][c])
                done()
                bs = alloc()
                segs = _head_segs(c)
                def fsg(e):
                    for sub in range(4):
                        ts = slice(sub * 128, (sub + 1) * 128)
                        for (p0, p1, h) in segs:
                            e.matmul(out=ps[p0:p1, bs, ts], lhsT=vln[:, sub, c * 128 + p0:c * 128 + p1], rhs=wmT[:, h, :],
                                     start=True, stop=False)
                            ins = e.matmul(out=ps[p0:p1, bs, ts], lhsT=ones1[0:1, 0:p1 - p0], rhs=bsrow[0:1, h * 128:(h + 1) * 128],
                                           start=False, stop=True)
                    return ins
                S.op("pe", fsg, reads=vln_res + [wmT_res, ones1_res, bsrow_res], writes=[ps_res[bs]])
                sgt, sgr = silu_to_tmp(bg)
                t, tr = gettmp()
                S.op("dve", lambda e: e.tensor_tensor(out=t[:, 0:T], in0=ps[:, bu, :], in1=sgt[:, 0:T], op=ALU.mult),
                     reads=[ps_res[bu], sgr], writes=[tr])
                S.op("dve", lambda e: e.tensor_tensor(out=yg[:, c, :], in0=ps[:, bs, :], in1=t[:, 0:T], op=ALU.mult),
                     reads=[ps_res[bs], tr], writes=[yg_res[c]])

            def b_chunk(c):
                bcg = proj(L0["CG"][c]); bxi = proj(L0["XI"][c]); bbg = proj(L0["BG"][c]); bg = proj(L0["G"][6 + c])
                done()
                xi, xir = gettmp()
                S.op("act", lambda e: e.activation(out=xi[:, 0:T], in_=ps[:, bxi, :], func=AF.Copy), reads=[ps_res[bxi]], writes=[xir])
                sgt, sgr = silu_to_tmp(bg)
                pr, prr = gettmp()
                S.op("pool", lambda e: e.tensor_copy(out=pr[:, 0:2], in_=pbh[:, c, :]), reads=[pbh_res[c]], writes=[prr])
                S.op("dve", lambda e: e.tensor_tensor(out=pr[:, 2:2 + T], in0=ps[:, bcg, :], in1=xi[:, 0:T], op=ALU.mult),
                     reads=[ps_res[bcg], xir, prr], writes=[prr])
                S.op("pool", lambda e: e.tensor_copy(out=pbh[:, c, :], in_=pr[:, T:T + 2]), reads=[prr], writes=[pbh_res[c]])
                acc, accr = gettmp()
                S.op("dve", lambda e: e.tensor_scalar(out=acc[:, 0:T], in0=pr[:, 0:T], scalar1=chp[:, R_BC + 0, c:c + 1],
                                                      scalar2=None, op0=ALU.mult), reads=[prr, chp_res], writes=[accr])
                for k in (1, 2):
                    S.op("dve", (lambda k: lambda e: e.scalar_tensor_tensor(
                        out=acc[:, 0:T], in0=pr[:, k:k + T], scalar=chp[:, R_BC + k, c:c + 1], in1=acc[:, 0:T],
                        op0=ALU.mult, op1=ALU.add))(k), reads=[prr, chp_res, accr], writes=[accr])
                S.op("dve", lambda e: e.tensor_tensor(out=acc[:, 0:T], in0=ps[:, bbg, :], in1=acc[:, 0:T], op=ALU.mult),
                     reads=[ps_res[bbg], accr], writes=[accr])
                S.op("pool", lambda e: e.tensor_tensor(out=yg[:, 6 + c, :], in0=acc[:, 0:T], in1=sgt[:, 0:T], op=ALU.mult),
                     reads=[accr, sgr], writes=[yg_res[6 + c]])

            mains = [(lambda c=c: a_chunk(c)) for c in range(6)] + [(lambda c=c: b_chunk(c)) for c in range(6)]
            interleave(mains, att_stages(0), list(range(12)))
            pre1 = ()
            if tile > 0 and nlayers > 1:
                pre1 = rms_pre(1, 4, gb[1][:], [gb_res[1]], xt, xt_res, as_waves=True)
            w_out_phase(0, L0, nlayers == 1, tile, xt, xt_res, after_sub=pre1)
            if tile == 0 and nlayers > 1:
                if real:
                    stage_pool["slots"] = stage_pool["l1"]
                kv_prologue(1, L1, 3, xts[1], xtrs[1])
            if tile + 1 < nt and (tile > 0 or nlayers == 1):
                if real:
                    flush_stores(0)
                load_x(tile + 1)
            if nlayers == 1:
                return
            H["ap"] = hT; H["res"] = [[r_] for r_ in hT_res]
            if tile == 0:
                rms_pre(1, 4, gb[1][:], [gb_res[1]], xt, xt_res)
            for c in range(6):
                bz = proj(L1["ZC"][c])
                z, zr = gettmp()
                S.op("pool", (lambda z, c: lambda e: e.tensor_copy(out=z[:, 0:16], in_=zbh[:, c, :]))(z, c), reads=[zbh_res[c]], writes=[zr])
                S.op("act", (lambda z, bz: lambda e: e.activation(out=z[:, 16:528], in_=ps[:, bz, :], func=AF.Copy))(z, bz),
                     reads=[ps_res[bz], zr], writes=[zr])
                S.op("pool", (lambda z, c: lambda e: e.tensor_copy(out=zbh[:, c, :], in_=z[:, 512:528]))(z, c), reads=[zr], writes=[zbh_res[c]])
                wins = sorted({POOLW[(c * 128 + p) // 192] for p in (0, 127)})
                sA, sAr = gettmp(); sB, sBr = gettmp()
                have = {}
                S.op("dve", (lambda sA, z: lambda e: e.tensor_tensor(out=sA[:, 1:528], in0=z[:, 1:528], in1=z[:, 0:527], op=ALU.add))(sA, z),
                     reads=[zr], writes=[sAr])
                have[2] = (sA, sAr)
                if max(wins) >= 4:
                    S.op("dve", (lambda sA, sB: lambda e: e.tensor_tensor(out=sB[:, 3:528], in0=sA[:, 3:528], in1=sA[:, 1:526], op=ALU.add))(sA, sB),
                         reads=[sAr], writes=[sBr])
                    have[4] = (sB, sBr)
                if max(wins) >= 8:
                    S.op("dve", (lambda sA, sB: lambda e: e.tensor_tensor(out=sA[:, 7:528], in0=sB[:, 7:528], in1=sB[:, 3:524], op=ALU.add))(sA, sB),
                         reads=[sBr, sAr], writes=[sAr])
                    have[8] = (sA, sAr)
                if max(wins) >= 16:
                    S.op("dve", (lambda sA, sB: lambda e: e.tensor_tensor(out=sB[:, 15:528], in0=sA[:, 15:528], in1=sA[:, 7:520], op=ALU.add))(sA, sB),
                         reads=[sAr, sBr], writes=[sBr])
                    have[16] = (sB, sBr)
                for (p0, p1) in ((0, 64), (64, 128)):
                    win = POOLW[(c * 128 + p0) // 192]
                    sw, swr = have[win]
                    if tile == 0:
                        S.op("pool", (lambda sw, p0, p1, c: lambda e: e.tensor_tensor(
                            out=sw[p0:p1, 16:32], in0=sw[p0:p1, 16:32], in1=cst[p0:p1, 256 + c * 16:256 + c * 16 + 16], op=ALU.mult))(sw, p0, p1, c),
                            reads=[swr, cst_res], writes=[swr])
                    S.op("dve", (lambda sw, z, p0, p1, c, win: lambda e: e.scalar_tensor_tensor(
                        out=pooled[p0:p1, c, :], in0=sw[p0:p1, 16:528], scalar=1.0 / win, in1=z[p0:p1, 16:528],
                        op0=ALU.mult, op1=ALU.subtract))(sw, z, p0, p1, c, win), reads=[swr, zr], writes=[pooled_res[c]])
            done()
            att_proj(1, L1)

            def bd_chunk(cp):
                blk, br = take(L1["BD"][cp])
                b = alloc()
                kcs = bd_kcs(cp)
                def fbd(e):
                    for i, kc in enumerate(kcs):
                        ins = e.matmul(out=ps[:, b, :], lhsT=blk[:, kc * 128:(kc + 1) * 128], rhs=pooled[:, kc, :],
                                       start=(i == 0), stop=(i == len(kcs) - 1))
                    return ins
                S.op("pe", fbd, reads=[br] + [pooled_res[kc] for kc in kcs], writes=[ps_res[b]])
                bg = proj(L1["G"][cp])
                done()
                sgt, sgr = silu_to_tmp(bg)
                S.op("dve", lambda e: e.scalar_tensor_tensor(
                    out=yg[:, cp, :], in0=ps[:, b, :], scalar=chp[:, R_CS, cp:cp + 1], in1=sgt[:, 0:T], op0=ALU.mult, op1=ALU.mult),
                    reads=[ps_res[b], chp_res, sgr], writes=[yg_res[cp]])

            dst = {}

            def d_proj(c):
                bga = proj(L1["GA"][c]); bgb = proj(L1["GB"][c])
                done()
                th, thr = gettmp()
                S.op("act", lambda e: e.activation(out=th[:, 0:T], in_=ps[:, bgb, :], func=AF.Tanh, scale=0.5), reads=[ps_res[bgb]], writes=[thr])
                S.op("pool", lambda e: e.tensor_copy(out=zg[:, c, 0:30], in_=zg[:, c, T:T + 30]), reads=[zg_res[c]], writes=[zg_res[c]])
                S.op("dve", lambda e: e.scalar_tensor_tensor(
                    out=zg[:, c, 30:30 + T], in0=th[:, 0:T], scalar=1.0, in1=ps[:, bga, :], op0=ALU.add, op1=ALU.mult),
                    reads=[thr, ps_res[bga], zg_res[c]], writes=[zg_res[c]])

            def d_conv(c):
                dgb = [take(bid) for bid in L1["DG"][c]]
                bc = alloc()
                def fcv(e):
                    for k in range(31):
                        ins = e.matmul(out=ps[:, bc, :], lhsT=dgb[k // 8][0][:, (k % 8) * 128:(k % 8 + 1) * 128], rhs=zg[:, c, k:k + T],
                                       start=(k == 0), stop=(k == 30))
                    return ins
                S.op("pe", fcv, reads=[x_[1] for x_ in dgb] + [zg_res[c]], writes=[ps_res[bc]])
                done()
                S.op("act", lambda e: e.activation(out=cvs[:, c, :], in_=ps[:, bc, :], func=AF.Identity, bias=chp[:, R_DWB, c:c + 1]),
                     reads=[ps_res[bc], chp_res], writes=[cvs_res[c]])
                sq, sqr = gettmp()
                S.op("pool", lambda e: e.tensor_tensor(out=sq[:, 0:T], in0=cvs[:, c, :], in1=cvs[:, c, :], op=ALU.mult),
                     reads=[cvs_res[c]], writes=[sqr])
                dst[c] = (sq, sqr)

            def d_stat(c):
                if c == 0:
                    dst["bsum"] = alloc(); reserved.add(dst["bsum"])
                    dst["bsq"] = alloc(); reserved.add(dst["bsq"])
                bsum, bsq = dst["bsum"], dst["bsq"]
                sq, sqr = dst[c]
                def fst(e):
                    e.matmul(out=ps[:, bsum, :], lhsT=onesf[:], rhs=cvs[:, c, :], start=(c == 0), stop=(c == 5))
                    return e.matmul(out=ps[:, bsq, :], lhsT=onesf[:], rhs=sq[:, 0:T], start=(c == 0), stop=(c == 5))
                S.op("pe", fst, reads=[onesf_res, cvs_res[c], sqr], writes=[ps_res[bsum], ps_res[bsq]])
                if c == 5:
                    d_f0()

            def d_f0():
                bsum, bsq = dst["bsum"], dst["bsq"]
                mean, meanr = gettmp(); msq, msqr = gettmp()
                dst["mean"] = (mean, meanr); dst["msq"] = (msq, msqr)
                S.op("dve", lambda e: e.tensor_scalar(out=mean[:, 0:T], in0=ps[:, bsum, :], scalar1=1.0 / BW, scalar2=None, op0=ALU.mult),
                     reads=[ps_res[bsum]], writes=[meanr])
                S.op("dve", lambda e: e.tensor_tensor(out=msq[:, 0:T], in0=mean[:, 0:T], in1=mean[:, 0:T], op=ALU.mult), reads=[meanr], writes=[msqr])
                S.op("dve", lambda e: e.scalar_tensor_tensor(out=msq[:, 0:T], in0=ps[:, bsq, :], scalar=1.0 / BW, in1=msq[:, 0:T],
                                                             op0=ALU.mult, op1=ALU.subtract), reads=[ps_res[bsq], msqr], writes=[msqr])
                S.op("dve", lambda e: e.tensor_scalar(out=msq[:, 0:T], in0=msq[:, 0:T], scalar1=EPS, scalar2=None, op0=ALU.add),
                     reads=[msqr], writes=[msqr])
                reserved.discard(bsum); reserved.discard(bsq)

            def d_f1():
                msq, msqr = dst["msq"]
                S.op("act", lambda e: e.activation(out=msq[:, 0:T], in_=msq[:, 0:T], func=AF.Sqrt), reads=[msqr], writes=[msqr])

            def d_f2():
                msq, msqr = dst["msq"]; mean, meanr = dst["mean"]
                S.op("dve", lambda e: e.reciprocal(out=stA[:], in_=msq[:, 0:T]), reads=[msqr], writes=[stA_res])
                S.op("dve", lambda e: e.scalar_tensor_tensor(out=stB[:], in0=mean[:, 0:T], scalar=-1.0, in1=stA[:], op0=ALU.mult, op1=ALU.mult),
                     reads=[meanr, stA_res], writes=[stB_res])

            def d_fn(c):
                t, tr = gettmp()
                S.op("dve", lambda e: e.tensor_tensor(out=t[:, 0:T], in0=cvs[:, c, :], in1=stA[:], op=ALU.mult),
                     reads=[cvs_res[c], stA_res], writes=[tr])
                S.op("pool" if c % 2 == 0 else "dve", lambda e: e.tensor_tensor(out=t[:, 0:T], in0=t[:, 0:T], in1=stB[:], op=ALU.add),
                     reads=[tr, stB_res], writes=[tr])
                S.op("act", lambda e: e.activation(out=zs[:, c, :], in_=t[:, 0:T], func=AF.Silu,
                                                   bias=chp[:, R_LNB, c:c + 1], scale=chp[:, R_LNG, c:c + 1]),
                     reads=[tr, chp_res], writes=[zs_res[c]])

            def pw_chunk(cp):
                blk, br = take(L1["PW"][cp])
                b = alloc()
                def fpw(e):
                    for kc in range(6):
                        ins = e.matmul(out=ps[:, b, :], lhsT=blk[:, kc * 128:(kc + 1) * 128], rhs=zs[:, kc, :], start=(kc == 0), stop=(kc == 5))
                    return ins
                S.op("pe", fpw, reads=[br] + zs_res, writes=[ps_res[b]])
                bg = proj(L1["G"][6 + cp])
                done()
                sgt, sgr = silu_to_tmp(bg)
                S.op("dve", lambda e: e.scalar_tensor_tensor(
                    out=yg[:, 6 + cp, :], in0=ps[:, b, :], scalar=chp[:, R_PWB, cp:cp + 1], in1=sgt[:, 0:T], op0=ALU.add, op1=ALU.mult),
                    reads=[ps_res[b], chp_res, sgr], writes=[yg_res[6 + cp]])

            P = lambda c: (lambda: d_proj(c))
            Cv = lambda c: (lambda: d_conv(c))
            St = lambda c: (lambda: d_stat(c))
            mains = [P(0), P(1), Cv(0), P(2), Cv(1), St(0), P(3), Cv(2), St(1), P(4), Cv(3), St(2), P(5), Cv(4), St(3), Cv(5), St(4), St(5)]
            bd = [(lambda cp=cp: bd_chunk(cp)) for cp in range(6)]
            fn = [(lambda c=c: d_fn(c)) for c in range(6)]
            mains += [d_f1, bd[0], d_f2, bd[1], fn[0], fn[1], bd[2], fn[2], fn[3], bd[3], fn[4], fn[5], bd[4], bd[5]]
            mains += [(lambda cp=cp: pw_chunk(cp)) for cp in range(6)]
            slots = [1, 3, 5, 7, 9, 11, 13, 15, 17, 18, 20, 22]
            if 0 < tile and tile + 1 < nt:
                nw = rms_pre(0, 4, gb[0][:], [gb_res[0]], xts[(tile + 1) % 2], xtrs[(tile + 1) % 2], as_waves=True, hidden=True,
                             dst=(hTa, [list(cvs_res[0:4]) for _ in range(4)]))
                for i_, anchor in enumerate((bd[1], bd[2], bd[3])):
                    mains.insert(mains.index(anchor) + 1, nw[i_])
                p0 = mains.index(bd[5])
                for i_, w_ in enumerate(nw[3:]):
                    mains.insert(min(p0 + 1 + 2 * i_, len(mains)), w_)
                pre_done[tile + 1] = True
            interleave(mains, att_stages(1), slots)
            nxt = ()
            if tile + 1 < nt:
                nxt = ()
            w_out_phase(1, L1, True, tile, xt, xt_res, extra=nxt)
            if tile == 0 and nt > 1:
                if real:
                    flush_stores(0)
                load_x(1)

        class _Null:
            def op(self, *a, **k): pass
            def dma(self, *a, **k): pass
        S_real = S
        S = _Null()
        mode["dry"] = True
        del order[:]
        kv_prologue(0, L0, 2, xts[0], xtrs[0])
        n0 = len(order)
        do_tile(0)
        n1 = len(order)
        if nt > 1:
            do_tile(1)
            per = order[n1:]
            for _ in range(nt - 2):
                order.extend(per)
        mode["dry"] = False
        S = S_real
        bank_ptr[0] = 0; tmp_i[0] = 0; reserved.clear(); pre_done.clear()
        stage_pool["l1"] = [(xts[1][:, k, :], [xtrs[1][k]]) for k in (2, 3, 0, 1)]
        stage_pool["slots"] = stage_pool["l1"] + [
            (cvs[:, 2 * j:2 * j + 2, :].rearrange("p a n -> p (a n)"), [cvs_res[2 * j], cvs_res[2 * j + 1]]) for j in range(3)]
        ring_fill()

        kv_prologue(0, L0, 2, xts[0], xtrs[0])
        for tile in range(nt):
            do_tile(tile)
        flush_stores(0)
        assert rstate["taken"] == len(order), (rstate, len(order))
        S.emit()
    print("sbuf bytes remaining", nc.sbuf_bytes_remaining, flush=True)
    return nc


_NC_CACHE = {}


def kernel(**inputs):
    nt = SEQ // T
    if "nc" not in _NC_CACHE:
        _NC_CACHE["nc"] = build(nt, 2)
    nc = _NC_CACHE["nc"]
    consts = make_consts()
    x = np.ascontiguousarray(inputs["x"], dtype=np.float32)
    mem = np.ascontiguousarray(inputs["mem"], dtype=np.float32)
    shared = {k: np.ascontiguousarray(v, dtype=np.float32) for k, v in inputs.items() if k not in ("x", "mem")}
    shared["consts"] = consts
    in_maps = []
    for b in range(8):
        m = dict(shared)
        m["x"] = x[b]
        m["mem"] = mem[b]
        in_maps.append(m)
    res = run_bass_kernel_spmd(nc, in_maps, core_ids=list(range(8)))
    return np.stack([np.asarray(r["out"], dtype=np.float32) for r in res.results], axis=0)
```

```python
import contextlib
import numpy as np
import concourse.bass as bass
import concourse.mybir as mybir
from concourse.bass_utils import run_bass_kernel_spmd

F32 = mybir.dt.float32
BF16 = mybir.dt.bfloat16
AF = mybir.ActivationFunctionType
ALU = mybir.AluOpType
AX = mybir.AxisListType

D = 1024
SEQ = 4096
NMEM = 256
T = 512
BW = 768
EVEN_IN = 6400
ODD_IN = 4864
EPS = 1e-6
RING = 13
SCALE = 128 ** -0.5
POOLW = (2, 4, 8, 16)


class Res:
    __slots__ = ("w", "r", "name")

    def __init__(self, name=""):
        self.w = None
        self.r = {}
        self.name = name


class Sched:
    ENG = ("pe", "act", "dve", "pool", "sp")

    def __init__(self, nc, es, n_dma_sems=16):
        self.nc = nc
        self.sem = {e: es.enter_context(nc.semaphore("s_" + e)) for e in self.ENG}
        self.cnt = {e: 0 for e in self.ENG}
        self.dsem = [es.enter_context(nc.semaphore("d%d" % i)) for i in range(n_dma_sems)]
        self.dcnt = [0] * n_dma_sems
        self.dnext = 0
        self.prog = {e: [] for e in self.ENG}
        self.seen = {e: {} for e in self.ENG}

    def _semobj(self, key):
        return self.sem[key] if isinstance(key, str) else self.dsem[key]

    def _need(self, eng, ev, waits):
        key, val = ev
        if self.seen[eng].get(key, 0) >= val:
            return
        if waits.get(key, 0) < val:
            waits[key] = val

    def _deps(self, eng, reads, writes):
        waits = {}
        for r in reads:
            if r.w is not None and (r.w[0] != eng or eng in ("act", "dve", "pool")):
                self._need(eng, r.w, waits)
        strict = eng in ("act", "dve", "pool")
        for w in writes:
            if w.w is not None and (w.w[0] != eng or strict):
                self._need(eng, w.w, waits)
            for k, v in w.r.items():
                if k != eng or strict:
                    self._need(eng, (k, v), waits)
        for k, v in waits.items():
            self.seen[eng][k] = v
        return waits

    def op(self, eng, fn, reads=(), writes=()):
        waits = self._deps(eng, reads, writes)
        self.cnt[eng] += 1
        val = self.cnt[eng]
        self.prog[eng].append((list(waits.items()), fn, self.sem[eng], 1))
        for r in reads:
            r.r[eng] = val
        for w in writes:
            w.w = (eng, val)
            w.r = {}

    def dma(self, out_ap, in_ap, reads=(), writes=(), q="sp", **kw):
        k = self.dnext
        self.dnext = (k + 1) % len(self.dsem)
        waits = self._deps(q, reads, writes)
        prev = self.dcnt[k] * 16
        if prev and self.seen[q].get(k, 0) < prev:
            if waits.get(k, 0) < prev:
                waits[k] = prev
            self.seen[q][k] = prev
        self.dcnt[k] += 1
        val = self.dcnt[k] * 16
        self.prog[q].append((list(waits.items()),
                             (lambda e: e.dma_start(out=out_ap, in_=in_ap, **kw)),
                             self.dsem[k], 16))
        for r in reads:
            r.r[k] = val
        for w in writes:
            w.w = (k, val)
            w.r = {}

    def dma_barrier(self, q="sp"):
        waits = []
        for k in range(len(self.dsem)):
            v = self.dcnt[k] * 16
            if v and self.seen[q].get(k, 0) < v:
                waits.append((k, v))
                self.seen[q][k] = v
        self.prog[q].append((waits, None, None, 0))

    def check(self):
        val = {}
        pc = {e: 0 for e in self.ENG}
        progress = True
        while progress:
            progress = False
            for e in self.ENG:
                while pc[e] < len(self.prog[e]):
                    waits, fn, sem, inc = self.prog[e][pc[e]]
                    if any(val.get(k, 0) < v for k, v in waits):
                        break
                    if fn is not None:
                        key = [k for k in list(self.sem) if self.sem[k] is sem]
                        key = key[0] if key else self.dsem.index(sem)
                        val[key] = val.get(key, 0) + inc
                    pc[e] += 1
                    progress = True
        stuck = {e: (pc[e], len(self.prog[e])) for e in self.ENG if pc[e] < len(self.prog[e])}
        if stuck:
            msg = []
            for e, (p, n) in stuck.items():
                waits = self.prog[e][p][0]
                msg.append("%s stuck at %d/%d waiting %s have %s" % (e, p, n, waits, [(k, val.get(k, 0)) for k, _ in waits]))
            raise RuntimeError("DEADLOCK: " + " | ".join(msg))

    def emit(self):
        nc = self.nc
        self.dma_barrier("sp")
        self.check()
        with nc.Block() as block:
            def mk(name):
                def body(e):
                    for waits, fn, sem, inc in self.prog[name]:
                        for key, val in waits:
                            e.wait_ge(self._semobj(key), val)
                        if fn is not None:
                            ins = fn(e)
                            ins.then_inc(sem, inc)
                return body
            block.tensor(mk("pe"))
            block.scalar(mk("act"))
            block.vector(mk("dve"))
            block.gpsimd(mk("pool"))
            block.sync(mk("sp"))


def hnd(t):
    return t.tensor if hasattr(t, "tensor") else t


def _head_segs(c):
    lo, hi = c * 128, c * 128 + 128
    out = []
    for h in range(4):
        a, b = max(lo, h * 192), min(hi, h * 192 + 192)
        if a < b:
            out.append((a - lo, b - lo, h))
    return out


def make_consts():
    c = np.zeros((128, 128 + 128 + 96), np.float32)
    c[:, 0:128] = np.eye(128, dtype=np.float32)
    t = np.arange(128)
    c[:, 128:256] = (t[None, :] <= t[:, None]).astype(np.float32)
    for ch in range(6):
        for p in range(128):
            win = POOLW[(ch * 128 + p) // 192]
            for tt in range(16):
                c[p, 256 + ch * 16 + tt] = win / min(tt + 1, win)
    return c


def build(nt=SEQ // T, nlayers=2):
    seq = nt * T
    nc = bass.Bass("TRN2", target_bir_lowering=False)

    def din(name, shape):
        return nc.dram_tensor(name, list(shape), F32, kind="ExternalInput").ap()

    x_d = din("x", [seq, D])
    mem_d = din("mem", [NMEM, D])
    cst_d = din("consts", [128, 352])
    e_pre_g = din("even_pre_g", [1, D]); e_w_in = din("even_w_in", [1, D, EVEN_IN])
    e_ln_g = din("even_a_ln_g", [1, BW]); e_ln_b = din("even_a_ln_b", [1, BW])
    e_ws = din("even_a_ws", [1, 4, 128, 128]); e_bs = din("even_a_bs", [1, 4, 128])
    e_bconv = din("even_b_conv", [1, 3, BW]); e_mem_g = din("even_mem_g", [1, D])
    e_w_kv = din("even_w_kv", [1, D, D]); e_w_out = din("even_w_out", [1, 2 * D, D]); e_post_g = din("even_post_g", [1, D])
    o_pre_g = din("odd_pre_g", [1, D]); o_w_in = din("odd_w_in", [1, D, ODD_IN])
    o_wgrp = din("odd_c_wgrp", [1, 4, 192, 192]); o_cscale = din("odd_c_scale", [1, BW])
    o_dw_w = din("odd_d_dw_w", [1, 31, BW]); o_dw_b = din("odd_d_dw_b", [1, BW])
    o_ln_g = din("odd_d_ln_g", [1, BW]); o_ln_b = din("odd_d_ln_b", [1, BW])
    o_pw_w = din("odd_d_pw_w", [1, BW, BW]); o_pw_b = din("odd_d_pw_b", [1, BW])
    o_mem_g = din("odd_mem_g", [1, D]); o_w_kv = din("odd_w_kv", [1, D, D])
    o_w_out = din("odd_w_out", [1, 2 * D, D]); o_post_g = din("odd_post_g", [1, D])
    out_d = nc.dram_tensor("out", [seq, D], F32, kind="ExternalOutput").ap()

    blocks = []

    def add_block(spec):
        blocks.append(spec)
        return len(blocks) - 1

    def wblk(w2d, ld, r0_list_stride, A, B, c0):
        return dict(kind="w", w=w2d, A=A, B=B, c0=c0, rs=r0_list_stride)

    w_in0 = e_w_in[0]; w_in1 = o_w_in[0]
    L0 = {}
    L0["VA"] = [add_block(dict(kind="w", w=w_in0, A=2, B=512, c0=768, r0=2 * j * 128)) for j in range(4)]
    L0["VB"] = [add_block(dict(kind="w", w=w_in0, A=4, B=256, c0=1280, r0=4 * j * 128)) for j in range(2)]
    def win_blk(w, col):
        return add_block(dict(kind="w", w=w, A=8, B=128, c0=col, r0=0))
    L0["U"] = [win_blk(w_in0, c * 128) for c in range(6)]
    L0["BG"] = [win_blk(w_in0, 1536 + c * 128) for c in range(6)]
    L0["CG"] = [win_blk(w_in0, 2304 + c * 128) for c in range(6)]
    L0["XI"] = [win_blk(w_in0, 3072 + c * 128) for c in range(6)]
    L0["Q"] = [win_blk(w_in0, 3840 + c * 128) for c in range(4)]
    L0["G"] = [win_blk(w_in0, 4352 + c * 128) for c in range(16)]
    L0["WO"] = [add_block(dict(kind="w", w=e_w_out[0], A=2, B=512, c0=half * 512, r0=2 * ep * 128))
                for half in range(2) for ep in range(8)]
    L0["KV"] = [win_blk(e_w_kv[0], j * 128) for j in range(8)]
    L1 = {}
    L1["ZC"] = [win_blk(w_in1, c * 128) for c in range(6)]
    L1["GA"] = [win_blk(w_in1, 768 + c * 128) for c in range(6)]
    L1["GB"] = [win_blk(w_in1, 1536 + c * 128) for c in range(6)]
    L1["Q"] = [win_blk(w_in1, 2304 + c * 128) for c in range(4)]
    L1["G"] = [win_blk(w_in1, 2816 + c * 128) for c in range(16)]
    L1["BD"] = [add_block(dict(kind="bd", cp=c)) for c in range(6)]
    L1["DG"] = [[add_block(dict(kind="dg", c=c, j=j)) for j in range(4)] for c in range(6)]
    L1["PW"] = [add_block(dict(kind="w", w=o_pw_w[0], A=6, B=128, c0=c * 128, r0=0)) for c in range(6)]
    L1["WO"] = [add_block(dict(kind="w", w=o_w_out[0], A=2, B=512, c0=half * 512, r0=2 * ep * 128))
                for half in range(2) for ep in range(8)]
    L1["KV"] = [win_blk(o_w_kv[0], j * 128) for j in range(8)]
    NBLK = len(blocks)
    wsc = nc.dram_tensor("wsc", [NBLK, 128, 1024], BF16).ap()

    def bd_kcs(cp):
        return sorted({kc for kc in range(6)
                       if any(max(kc * 128, g * 192) < min(kc * 128 + 128, g * 192 + 192) and
                              max(cp * 128, g * 192) < min(cp * 128 + 128, g * 192 + 192) for g in range(4))})

    order = []
    order += L0["KV"]
    for ti_ in range(nt):
        order += L0["VA"] + L0["VB"]
        for c in range(6):
            order += [L0["U"][c], L0["G"][c]]
        for c in range(6):
            order += [L0["CG"][c], L0["XI"][c], L0["BG"][c], L0["G"][6 + c]]
        order += L0["Q"] + L0["G"][12:16] + L0["WO"]
        if nlayers > 1:
            if ti_ == 0:
                order += L1["KV"]
            order += L1["ZC"]
            for c in range(6):
                order += [L1["BD"][c], L1["G"][c]]
            for c in range(6):
                order += [L1["GA"][c], L1["GB"][c]] + L1["DG"][c]
            for c in range(6):
                order += [L1["PW"][c], L1["G"][6 + c]]
            order += L1["Q"] + L1["G"][12:16] + L1["WO"]

    with contextlib.ExitStack() as es:
        S = Sched(nc, es)

        def sb(name, shape, dt=F32):
            return es.enter_context(nc.sbuf_tensor(name, list(shape), dt))

        ring = sb("ring", [128, RING, 1024], BF16); ring_res = [Res("ring%d" % i) for i in range(RING)]
        xts = [sb("xt%d" % j, [128, 4, D]) for j in range(2)]
        xtrs = [[Res("xt%d_%d" % (j, i)) for i in range(4)] for j in range(2)]
        xt = xts[0]; xt_res = xtrs[0]
        hT = sb("hT", [128, 8, T], BF16); hT_res = [Res("hT%d" % i) for i in range(4)]
        H = dict(ap=hT, res=[[r_] for r_ in hT_res])
        hb = [sb("hb%d" % i, [128, D], BF16) for i in range(2)]; hb_res = [Res("hb0"), Res("hb1")]
        yg = sb("yg", [128, 16, T], BF16); yg_res = [Res("yg%d" % i) for i in range(16)]
        NTMP = 6
        tmp = [sb("tmp%d" % i, [128, 528]) for i in range(NTMP)]; tmp_res = [Res("tmp%d" % i) for i in range(NTMP)]
        tmp_i = [0]

        def gettmp():
            i = tmp_i[0]; tmp_i[0] = (i + 1) % NTMP
            return tmp[i], tmp_res[i]

        cst = sb("cst", [128, 352]); cst_res = Res("cst")
        ident = sb("ident", [128, 128], BF16); ident_res = Res("ident")
        onesf = sb("onesf", [128, 128]); onesf_res = Res("onesf")
        ones1 = sb("ones1", [1, 128], BF16); ones1_res = Res("ones1")
        gb = [sb("gb%d" % l, [128, 8, 128]) for l in range(2)]; gb_res = [Res("gb0"), Res("gb1")]
        gcol = sb("gcol", [128, 4, 8]); gcol_res = Res("gcol")
        pgb = [sb("pgb%d" % l, [128, D]) for l in range(2)]; pgb_res = [Res("pgb0"), Res("pgb1")]
        lngb = sb("lngb", [128, BW]); lnbb = sb("lnbb", [128, BW]); lnp_res = Res("lnp")
        wmT = sb("wmT", [128, 4, 128], BF16); wmT_res = Res("wmT")
        bsrow = sb("bsrow", [1, 512], BF16); bsrow_res = Res("bsrow")
        chp = sb("chp", [128, 40, 6]); chp_res = Res("chp")
        kT = [sb("kT%d" % l, [128, 4, NMEM], BF16) for l in range(2)]; kT_res = [Res("kT0"), Res("kT1")]
        vv = [sb("vv%d" % l, [128, 2, 512], BF16) for l in range(2)]; vv_res = [Res("vv0"), Res("vv1")]
        ssq = sb("ssq", [128, 8]); ssq_res = Res("ssq")
        ms4 = sb("ms4", [128, 4]); ms4_res = Res("ms4")
        rstd4 = sb("rstd4", [128, 4]); rstd4_res = Res("rstd4")
        vln = sb("vln", [128, 4, BW], BF16); vln_res = [Res("vln%d" % i) for i in range(4)]
        vn = [sb("vn%d" % i, [128, BW]) for i in range(2)]; vn_res = [Res("vn0"), Res("vn1")]
        st6 = [sb("st6_%d" % i, [128, 3, 6]) for i in range(4)]; st6_res = [Res("st6%d" % i) for i in range(4)]
        mv = [sb("mv%d" % i, [128, 4]) for i in range(4)]; mv_res = [Res("mv%d" % i) for i in range(4)]
        pbh = sb("pbh", [128, 6, 2]); pbh_res = [Res("pbh%d" % i) for i in range(6)]
        qT = sb("qT", [128, 4, T], BF16); qT_res = [Res("qT%d" % i) for i in range(4)]
        sgx = sb("sgx", [128, 4, T]); sgx_res = [Res("sgx%d" % i) for i in range(4)]
        ebuf = sb("ebuf", [128, 4, NMEM]); ebuf_res = Res("ebuf")
        pbuf = sb("pbuf", [128, 4, NMEM], BF16); pbuf_res = Res("pbuf")
        pts = sb("pts", [128, 1024], BF16); pts_res = Res("pts")
        at_s = [sb("at_s%d" % i, [128, 16]) for i in range(2)]; at_res = [Res("at0"), Res("at1")]
        zbh = sb("zbh", [128, 6, 16]); zbh_res = [Res("zbh%d" % i) for i in range(6)]
        pooled = sb("pooled", [128, 6, T], BF16); pooled_res = [Res("pooled%d" % i) for i in range(6)]
        zg = sb("zg", [128, 6, 30 + T], BF16); zg_res = [Res("zg%d" % i) for i in range(6)]
        cvs = sb("cvs", [128, 6, T]); cvs_res = [Res("cvs%d" % i) for i in range(6)]
        zs = sb("zs", [128, 6, T], BF16); zs_res = [Res("zs%d" % i) for i in range(6)]
        stA = sb("stA", [128, T]); stA_res = Res("stA")
        stB = sb("stB", [128, T]); stB_res = Res("stB")
        junk = sb("junk", [128, D], BF16); junk_res = Res("junk")
        ps = es.enter_context(nc.psum_tensor("ps", [128, 8, 512], F32))
        psb = ps.bitcast(BF16)
        ps_res = [Res("ps%d" % i) for i in range(8)]
        bank_ptr = [0]
        reserved = set()

        def alloc(n=1):
            while True:
                b = bank_ptr[0]
                if n == 2 and b % 2:
                    b += 1
                if b + n > 8:
                    b = 0
                bank_ptr[0] = (b + n) % 8
                if all((b + i) not in reserved for i in range(n)):
                    return b

        scr_res = Res("scr")
        out_res = Res("out")

        rstate = dict(loaded=0, taken=0, released=0)
        defer = dict(on=False)

        def blk_cols(bid):
            sp_ = blocks[bid]
            if sp_["kind"] == "w":
                return sp_["A"] * sp_["B"]
            return 768 if sp_["kind"] == "bd" else 1024

        prepped = set()
        stage_pool = dict(slots=[], i=0)
        pend_store = []

        def flush_stores(keep):
            while len(pend_store) > keep:
                bid, slot, nb_ = pend_store.pop(0)
                S.dma(wsc[bid][:, 0:nb_], ring[:, slot, 0:nb_], reads=[ring_res[slot]], writes=[scr_res_b[bid]])

        def ring_fill():
            while rstate["loaded"] < len(order) and rstate["loaded"] < rstate["released"] + RING:
                n = rstate["loaded"]
                slot = n % RING
                bid = order[n]
                nb_ = blk_cols(bid)
                if bid not in prepped:
                    sl = stage_pool["slots"]
                    st32, r32 = sl[stage_pool["i"] % len(sl)]
                    stage_pool["i"] += 1
                    prep_a(bid, st32, r32, ring[:, slot, :], [ring_res[slot]])
                    prepped.add(bid)
                    pend_store.append((bid, slot, nb_))
                    flush_stores(2)
                else:
                    flush_stores(0)
                    S.dma(ring[:, slot, 0:nb_], wsc[bid][:, 0:nb_], reads=[scr_res_b[bid]], writes=[ring_res[slot]])
                rstate["loaded"] += 1

        mode = dict(dry=False)

        def take(expect):
            if mode["dry"]:
                order.append(expect)
                return ring[:, 0, :], ring_res[0]
            n = rstate["taken"]
            assert order[n] == expect, (n, order[n], expect)
            assert n < rstate["loaded"], "ring too shallow"
            rstate["taken"] += 1
            slot = n % RING
            return ring[:, slot, :], ring_res[slot]

        def done():
            if mode["dry"]:
                return
            rstate["released"] = rstate["taken"]
            ring_fill()

        def bcast_rows(dram_row, n):
            return bass.AP(dram_row.tensor, dram_row.offset, [[0, 128], [1, n]])

        S.dma(cst[:], cst_d, writes=[cst_res])
        S.op("dve", lambda e: e.tensor_copy(out=ident[:], in_=cst[:, 0:128]), reads=[cst_res], writes=[ident_res])
        nh1 = sb("nh1", [128, 4]); nh1_res = Res("nh1")
        S.op("pool", lambda e: e.memset(nh1[:], -0.5), writes=[nh1_res])
        S.op("pool", lambda e: e.memset(onesf[:], 1.0), writes=[onesf_res])
        S.op("pool", lambda e: e.memset(ones1[:], 1.0), writes=[ones1_res])
        S.op("pool", lambda e: e.memset(zg[:], 0.0), writes=zg_res)
        S.op("pool", lambda e: e.memset(pbh[:], 0.0), writes=pbh_res)
        S.op("pool", lambda e: e.memset(zbh[:], 0.0), writes=zbh_res)
        jdummy = sb("jdummy", [128, 4])
        _g_rows = []
        for i, g in enumerate((e_pre_g, o_pre_g, e_mem_g, o_mem_g)):
            rr = Res("gcolrow%d" % i); _g_rows.append(rr)
            S.dma(gcol[:, i, :], g[0].rearrange("(c p) -> p c", p=128), writes=[rr],
                  allow_slow_non_contiguous=True)
        S.op("pool", lambda e: e.memset(jdummy[:, 0:1], 0.0), reads=_g_rows, writes=[gcol_res])
        for l in range(2):
            src = bass.AP(hnd(gcol), l * 8, [[32, 128], [1, 8], [0, 128]])
            S.op("dve", (lambda src, l: lambda e: e.tensor_copy(out=gb[l][:], in_=src))(src, l),
                 reads=[gcol_res], writes=[gb_res[l]])
        for l, g in enumerate((e_post_g, o_post_g)):
            S.dma(pgb[l][:], bcast_rows(g, D), writes=[pgb_res[l]])
        _lr = [Res("lnrow0"), Res("lnrow1")]
        S.dma(lngb[:], bcast_rows(e_ln_g, BW), writes=[_lr[0]])
        S.dma(lnbb[:], bcast_rows(e_ln_b, BW), writes=[_lr[1]])
        S.op("pool", lambda e: e.memset(jdummy[:, 1:2], 0.0), reads=_lr, writes=[lnp_res])
        prow = [e_bconv[0, k] for k in range(3)] + [o_cscale[0]] + [o_dw_w[0, k] for k in range(31)] + \
               [o_dw_b[0], o_ln_g[0], o_ln_b[0], o_pw_b[0]]
        _c_rows = []
        for r, src in enumerate(prow):
            rr = Res("chprow%d" % r); _c_rows.append(rr)
            S.dma(chp[:, r, :], src.rearrange("(c p) -> p c", p=128), writes=[rr],
                  allow_slow_non_contiguous=True)
        S.op("pool", lambda e: e.memset(jdummy[:, 2:3], 0.0), reads=_c_rows, writes=[chp_res])
        R_BC, R_CS, R_DW, R_DWB, R_LNG, R_LNB, R_PWB = 0, 3, 4, 35, 36, 37, 38
        wnat, wnat_res = tmp[0], tmp_res[0]
        S.dma(wnat[:, 0:512].rearrange("p (h s) -> p h s", h=4), e_ws[0].rearrange("h t s -> t h s"), writes=[wnat_res])
        for h in range(4):
            S.op("dve", (lambda h: lambda e: e.tensor_tensor(out=hb[0][:, h * 128:(h + 1) * 128], in0=wnat[:, h * 128:(h + 1) * 128],
                                                            in1=cst[:, 128:256], op=ALU.mult))(h),
                 reads=[wnat_res, cst_res], writes=[hb_res[0]])
        b = alloc()
        def _wtr(e, b=b):
            for h in range(4):
                ins = e.transpose(out=psb[:, b, h * 128:(h + 1) * 128], in_=hb[0][:, h * 128:(h + 1) * 128], identity=ident[:])
            return ins
        S.op("pe", _wtr, reads=[hb_res[0], ident_res], writes=[ps_res[b]])
        S.op("act", (lambda b: lambda e: e.activation(out=wmT[:].rearrange("p h t -> p (h t)"), in_=psb[:, b, 0:512], func=AF.Copy))(b),
             reads=[ps_res[b]], writes=[wmT_res])
        S.dma(tmp[1][0:1, 0:512], e_bs[0].rearrange("h t -> (h t)").rearrange("(o n) -> o n", o=1), writes=[tmp_res[1]])
        S.op("dve", lambda e: e.tensor_copy(out=bsrow[:], in_=tmp[1][0:1, 0:512]), reads=[tmp_res[1]], writes=[bsrow_res])

        cast_ctr = [0]

        def prep_a(bi, st32, st32_res, st16, st16_res):
            spec = blocks[bi]
            ce = ("act", "dve")[cast_ctr[0] % 2]
            cast_ctr[0] += 1
            if spec["kind"] == "w":
                w = spec["w"]; A = spec["A"]; B = spec["B"]; c0 = spec["c0"]; r0 = spec["r0"]
                src = w[r0:r0 + A * 128, c0:c0 + B].rearrange("(a p) n -> p a n", p=128)
                S.dma(st32[:, 0:A * B].rearrange("p (a n) -> p a n", a=A), src, writes=st32_res)
                n = A * B
                if ce == "act":
                    S.op("act", lambda e: e.activation(out=st16[:, 0:n], in_=st32[:, 0:n], func=AF.Copy), reads=st32_res, writes=st16_res)
                else:
                    S.op("dve", lambda e: e.tensor_copy(out=st16[:, 0:n], in_=st32[:, 0:n]), reads=st32_res, writes=st16_res)
                return n
            elif spec["kind"] == "bd":
                cp = spec["cp"]
                S.op("pool", lambda e: e.memset(st32, 0.0), writes=st32_res)
                for kc in range(6):
                    for g in range(4):
                        ra, rb = max(kc * 128, g * 192), min(kc * 128 + 128, g * 192 + 192)
                        ca, cb = max(cp * 128, g * 192), min(cp * 128 + 128, g * 192 + 192)
                        if ra < rb and ca < cb:
                            dst = st32[ra - kc * 128:rb - kc * 128, kc * 128 + (ca - cp * 128):kc * 128 + (cb - cp * 128)]
                            S.dma(dst, o_wgrp[0, g, ra - g * 192:rb - g * 192, ca - g * 192:cb - g * 192],
                                  reads=st32_res, writes=st32_res)
                S.op("dve", lambda e: e.tensor_copy(out=st16[:, 0:768], in_=st32[:, 0:768]), reads=st32_res, writes=st16_res)
                return 768
            else:
                c = spec["c"]; jj = spec["j"]
                def _dg(e):
                    ins = None
                    for kk in range(8):
                        k_ = jj * 8 + kk
                        if k_ < 31:
                            ins = e.tensor_scalar(out=st16[:, kk * 128:(kk + 1) * 128], in0=cst[:, 0:128],
                                                  scalar1=chp[:, R_DW + k_, c:c + 1], scalar2=0.5, op0=ALU.mult, op1=ALU.mult)
                        else:
                            ins = e.memset(st16[:, kk * 128:(kk + 1) * 128], 0.0)
                    return ins
                S.op("dve", _dg, reads=[cst_res, chp_res], writes=st16_res)
                return 1024

        scr_res_b = [Res("scr%d" % i) for i in range(NBLK)]

        class Prepper:
            def __init__(self, ids, s32, s16, la):
                self.ids = list(ids); self.s32 = s32; self.s16 = s16; self.la = la
                self.na = 0; self.nb = 0; self.n = {}; self.base = 0
                self.stored = set()

            def _a(self):
                i = self.na
                bi = self.ids[i]
                st32, r32 = self.s32[(i - self.base) % len(self.s32)]
                st16, r16 = self.s16[(i - self.base) % len(self.s16)]
                self.n[i] = prep_a(bi, st32, r32, st16, r16)
                self.na += 1

            def step(self):
                if self.nb >= len(self.ids):
                    return False
                while self.na < len(self.ids) and self.na < self.nb + max(self.la, 1):
                    self._a()
                i = self.nb
                bi = self.ids[i]
                st16, r16 = self.s16[(i - self.base) % len(self.s16)]
                n = self.n[i]
                S.dma(wsc[bi][:, 0:n], st16[:, 0:n], reads=r16, writes=[scr_res_b[bi]])
                self.stored.add(bi)
                self.nb += 1
                return True

            def ensure(self, bi):
                while bi not in self.stored:
                    assert self.step()

            def switch(self, s32, s16, la):
                old_la = self.la
                self.la = 0
                while self.nb < self.na:
                    self.step()
                self.s32 = s32; self.s16 = s16; self.la = la
                self.base = self.nb

        def pair16(buf, res, j):
            return (buf[:, 2 * j:2 * j + 2, :].rearrange("p a n -> p (a n)"), [res[2 * j], res[2 * j + 1]])

        stsm = [sb("stsm%d" % i, [128, 4]) for i in range(4)]; stsm_res = [Res("stsm%d" % i) for i in range(4)]

        def rms_pre(l, nsub, g_tile, g_res, xt, xt_res, as_waves=False, dst=None, hidden=False):
            dst_ap, dst_res = (hT, [[r_] for r_ in hT_res]) if dst is None else dst
            def w_sq(sub):
                m = stsm[sub]
                S.op("act", lambda e: e.activation(out=junk[:], in_=xt[:, sub, :], func=AF.Square, accum_out=m[:, 0:1]),
                     reads=[xt_res[sub]], writes=[junk_res, stsm_res[sub]])

            def w_stat(sub):
                m = stsm[sub]; mr = stsm_res[sub]
                if hidden:
                    S.op("dve", lambda e: e.tensor_scalar(out=m[:, 1:2], in0=m[:, 0:1], scalar1=1.0 / D, scalar2=EPS, op0=ALU.mult, op1=ALU.add),
                         reads=[mr], writes=[mr])
                    S.op("pool", lambda e: e.tensor_tensor(out=m[:, 2:3], in0=m[:, 1:2], in1=nh1[:, 0:1], op=ALU.pow),
                         reads=[mr, nh1_res], writes=[mr])
                else:
                    S.op("act", lambda e: e.activation(out=m[:, 1:2], in_=m[:, 0:1], func=AF.Sqrt, bias=EPS, scale=1.0 / D), reads=[mr], writes=[mr])
                    S.op("dve", lambda e: e.reciprocal(out=m[:, 2:3], in_=m[:, 1:2]), reads=[mr], writes=[mr])

            def w_scale(sub):
                m = stsm[sub]; mr = stsm_res[sub]
                hbt, hbr = hb[sub % 2], hb_res[sub % 2]
                if hidden:
                    S.op("dve", lambda e: e.tensor_scalar(out=hbt[:], in0=xt[:, sub, :], scalar1=m[:, 2:3], scalar2=None, op0=ALU.mult),
                         reads=[xt_res[sub], mr], writes=[hbr])
                else:
                    S.op("act", lambda e: e.activation(out=hbt[:], in_=xt[:, sub, :], func=AF.Copy, scale=m[:, 2:3]),
                         reads=[xt_res[sub], mr], writes=[hbr])

            def w_tr(sub):
                hbt, hbr = hb[sub % 2], hb_res[sub % 2]
                b = alloc()
                def _tr(e):
                    for kc in range(8):
                        ins = e.transpose(out=psb[:, b, kc * 128:(kc + 1) * 128], in_=hbt[:, kc * 128:(kc + 1) * 128], identity=ident[:])
                    return ins
                S.op("pe", _tr, reads=[hbr, ident_res], writes=[ps_res[b]])
                S.op("dve", lambda e: e.tensor_tensor(
                    out=dst_ap[:, :, sub * 128:(sub + 1) * 128], in0=psb[:, b, :].rearrange("p (k t) -> p k t", k=8),
                    in1=g_tile, op=ALU.mult), reads=[ps_res[b]] + g_res, writes=list(dst_res[sub]))

            def wave(k):
                if 0 <= k - 3 < nsub:
                    w_tr(k - 3)
                if 0 <= k - 2 < nsub:
                    w_scale(k - 2)
                if 0 <= k - 1 < nsub:
                    w_stat(k - 1)
                if k < nsub:
                    w_sq(k)
            waves = [(lambda k=k: wave(k)) for k in range(nsub + 3)]
            if as_waves:
                return waves
            for w in waves:
                w()

        def mm_fm(blk, blk_res, b, nk=8):
            hcur = H["ap"]
            hres = []
            for rl in H["res"]:
                for r_ in rl:
                    if r_ not in hres:
                        hres.append(r_)
            def f(e):
                for kc in range(nk):
                    ins = e.matmul(out=ps[:, b, :], lhsT=blk[:, kc * 128:(kc + 1) * 128], rhs=hcur[:, kc, :],
                                   start=(kc == 0), stop=(kc == nk - 1))
                return ins
            S.op("pe", f, reads=[blk_res] + hres, writes=[ps_res[b]])

        def proj(bid):
            blk, br = take(bid)
            b = alloc()
            mm_fm(blk, br, b)
            return b

        def silu_to_tmp(b):
            t, tr = gettmp()
            S.op("act", lambda e: e.activation(out=t[:, 0:T], in_=ps[:, b, :], func=AF.Silu), reads=[ps_res[b]], writes=[tr])
            return t, tr

        mg = sgx[:, 0:2, :].rearrange("p a (k t) -> p (a k) t", k=4); mg_res = [sgx_res[0], sgx_res[1]]

        def kv_prologue(l, LB, memg_idx, xt, xt_res):
            S.dma(xt[:, 0:2, :], mem_d.rearrange("(s p) d -> p s d", p=128), writes=[xt_res[0], xt_res[1]])
            src = bass.AP(hnd(gcol), memg_idx * 8, [[32, 128], [1, 8], [0, 128]])
            S.op("dve", lambda e: e.tensor_copy(out=mg, in_=src), reads=[gcol_res], writes=mg_res)
            rms_pre(l, 2, mg, mg_res, xt, xt_res)
            kvb = [take(bid) for bid in LB["KV"]]
            for h in range(4):
                b = alloc()
                def f(e, h=h, b=b):
                    for kc in range(8):
                        ins = e.matmul(out=ps[:, b, 0:NMEM], lhsT=kvb[h][0][:, kc * 128:(kc + 1) * 128], rhs=hT[:, kc, 0:NMEM],
                                       start=(kc == 0), stop=(kc == 7))
                    return ins
                S.op("pe", f, reads=[kvb[h][1], hT_res[0], hT_res[1]], writes=[ps_res[b]])
                S.op("act", (lambda h, b: lambda e: e.activation(out=kT[l][:, h, :], in_=ps[:, b, 0:NMEM], func=AF.Copy))(h, b),
                     reads=[ps_res[b]], writes=[kT_res[l]])
            for mc in range(2):
                b = alloc()
                def f(e, mc=mc, b=b):
                    for j in range(4):
                        for kc in range(8):
                            ins = e.matmul(out=ps[:, b, j * 128:(j + 1) * 128], lhsT=hT[:, kc, mc * 128:(mc + 1) * 128],
                                           rhs=kvb[4 + j][0][:, kc * 128:(kc + 1) * 128], start=(kc == 0), stop=(kc == 7))
                    return ins
                S.op("pe", f, reads=[kvb[4 + j][1] for j in range(4)] + [hT_res[mc]], writes=[ps_res[b]])
                S.op("act", (lambda mc, b: lambda e: e.activation(out=vv[l][:, mc, :], in_=ps[:, b, :], func=AF.Copy))(mc, b),
                     reads=[ps_res[b]], writes=[vv_res[l]])
            done()


        pbufs = [pbuf, sb("pbufB", [128, 4, NMEM], BF16)]; pbufs_res = [pbuf_res, Res("pbufB")]

        def att_proj(l, LB):
            for h in range(4):
                b = proj(LB["Q"][h])
                S.op("act", (lambda h, b: lambda e: e.activation(out=qT[:, h, :], in_=ps[:, b, :], func=AF.Copy))(h, b),
                     reads=[ps_res[b]], writes=[qT_res[h]])
            done()
            for h in range(4):
                b = proj(LB["G"][12 + h])
                S.op("act", (lambda h, b: lambda e: e.activation(out=sgx[:, h, :], in_=ps[:, b, :], func=AF.Silu))(h, b),
                     reads=[ps_res[b]], writes=[sgx_res[h]])
            done()

        def att_stages(l):
            def s1(sub):
                ts = slice(sub * 128, (sub + 1) * 128)
                b2 = alloc(2)
                def fq(e):
                    for h in range(4):
                        ins = e.matmul(out=ps[:, b2 + h // 2, (h % 2) * 256:(h % 2) * 256 + 256], lhsT=qT[:, h, ts], rhs=kT[l][:, h, :],
                                       start=True, stop=True)
                    return ins
                S.op("pe", fq, reads=qT_res + [kT_res[l]], writes=[ps_res[b2], ps_res[b2 + 1]])
                a, ar = at_s[sub % 2], at_res[sub % 2]
                pb, pbr = pbufs[sub % 2], pbufs_res[sub % 2]
                sc4 = ps[:, b2:b2 + 2, :].rearrange("p b (h m) -> p (b h) m", h=2)
                S.op("dve", lambda e: e.tensor_reduce(out=a[:, 0:4], in_=sc4, axis=AX.X, op=ALU.max),
                     reads=[ps_res[b2], ps_res[b2 + 1]], writes=[ar])
                S.op("dve", lambda e: e.tensor_scalar(out=a[:, 4:8], in0=a[:, 0:4], scalar1=-SCALE, scalar2=None, op0=ALU.mult),
                     reads=[ar], writes=[ar])
                def fe(e):
                    for h in range(4):
                        ins = e.activation(out=ebuf[:, h, :], in_=ps[:, b2 + h // 2, (h % 2) * 256:(h % 2) * 256 + 256], func=AF.Exp,
                                           bias=a[:, 4 + h:5 + h], scale=SCALE, accum_out=a[:, 8 + h:9 + h])
                    return ins
                S.op("act", fe, reads=[ps_res[b2], ps_res[b2 + 1], ar], writes=[ebuf_res, ar])
                S.op("dve", lambda e: e.reciprocal(out=a[:, 12:16], in_=a[:, 8:12]), reads=[ar], writes=[ar])
                def fn_(e):
                    for h in range(4):
                        ins = e.tensor_scalar(out=pb[:, h, :], in0=ebuf[:, h, :], scalar1=a[:, 12 + h:13 + h], scalar2=None, op0=ALU.mult)
                    return ins
                S.op("dve", fn_, reads=[ebuf_res, ar], writes=[pbr])

            def s2a(sub):
                pb, pbr = pbufs[sub % 2], pbufs_res[sub % 2]
                bT = alloc()
                def ft(e):
                    for mc in range(2):
                        for h in range(4):
                            ins = e.transpose(out=psb[:, bT, (mc * 4 + h) * 128:(mc * 4 + h + 1) * 128],
                                              in_=pb[:, h, mc * 128:(mc + 1) * 128], identity=ident[:])
                    return ins
                S.op("pe", ft, reads=[pbr, ident_res], writes=[ps_res[bT]])
                S.op("dve", lambda e: e.tensor_copy(out=pts[:], in_=psb[:, bT, :]), reads=[ps_res[bT]], writes=[pts_res])

            def s2b(sub):
                ts = slice(sub * 128, (sub + 1) * 128)
                bO = alloc()
                def fo(e):
                    for h in range(4):
                        for mc in range(2):
                            ins = e.matmul(out=ps[:, bO, h * 128:(h + 1) * 128], lhsT=vv[l][:, mc, h * 128:(h + 1) * 128],
                                           rhs=pts[:, (mc * 4 + h) * 128:(mc * 4 + h + 1) * 128], start=(mc == 0), stop=(mc == 1))
                    return ins
                S.op("pe", fo, reads=[pts_res, vv_res[l]], writes=[ps_res[bO]])
                S.op("dve", lambda e: e.tensor_tensor(out=yg[:, 12:16, ts], in0=ps[:, bO, :].rearrange("p (h t) -> p h t", h=4),
                                                      in1=sgx[:, :, ts], op=ALU.mult),
                     reads=[ps_res[bO]] + sgx_res, writes=yg_res[12:16])

            fmap = {"s1": s1, "s2a": s2a, "s2b": s2b}
            seq = [("s1", 0), ("s1", 1), ("s2a", 0), ("s1", 2), ("s2b", 0), ("s2a", 1), ("s1", 3), ("s2b", 1),
                   ("s2a", 2), ("s2b", 2), ("s2a", 3), ("s2b", 3)]
            return [(lambda k=k, sub=sub: fmap[k](sub)) for k, sub in seq]

        def interleave(mains, fillers, slots):
            fi = 0
            for i, m in enumerate(mains):
                m()
                for _ in range(slots.count(i)):
                    if fi < len(fillers):
                        fillers[fi](); fi += 1
            while fi < len(fillers):
                fillers[fi](); fi += 1

        warm = sb("warm", [128, 4]); warm_res = Res("warm")
        S.op("pool", lambda e: e.memset(warm[:], 1.0), writes=[warm_res])

        def w_out_phase(l, LB, last_layer, tile, xt, xt_res, extra=(), after_sub=()):
            bank_ptr[0] = 0
            after_sub = list(after_sub)
            for half in range(2):
                for ep in range(8):
                    blk, br = take(LB["WO"][half * 8 + ep])
                    def f(e, blk=blk, half=half, ep=ep):
                        for el in range(2):
                            ee = 2 * ep + el
                            for sub in range(4):
                                ins = e.matmul(out=ps[:, 2 * sub + half, :], lhsT=yg[:, ee, sub * 128:(sub + 1) * 128],
                                               rhs=blk[:, el * 512:(el + 1) * 512], start=(ee == 0), stop=(ee == 15))
                        return ins
                    S.op("pe", f, reads=[br, yg_res[2 * ep], yg_res[2 * ep + 1]], writes=[ps_res[2 * s_ + half] for s_ in range(4)])
                    done()
                def fs(e, half=half):
                    for sub in range(4):
                        dmy = (junk[:, 0:512], junk[:, 512:1024], hb[0][:, 0:512], hb[1][:, 0:512])[sub]
                        ins = e.activation(out=dmy, in_=ps[:, 2 * sub + half, :], func=AF.Square,
                                           accum_out=ssq[:, half * 4 + sub:half * 4 + sub + 1])
                    return ins
                S.op("act", fs, reads=[ps_res[2 * s_ + half] for s_ in range(4)], writes=[junk_res, ssq_res, hb_res[0], hb_res[1]])
                if half == 0:
                    S.op("act", lambda e: e.activation(out=warm[:, 0:1], in_=warm[:, 1:2], func=AF.Sqrt), reads=[warm_res], writes=[warm_res])
            S.op("dve", lambda e: e.tensor_tensor(out=ms4[:], in0=ssq[:, 0:4], in1=ssq[:, 4:8], op=ALU.add), reads=[ssq_res], writes=[ms4_res])
            S.op("act", lambda e: e.activation(out=ms4[:], in_=ms4[:], func=AF.Sqrt, bias=EPS, scale=1.0 / D), reads=[ms4_res], writes=[ms4_res])
            S.op("dve", lambda e: e.reciprocal(out=rstd4[:], in_=ms4[:]), reads=[ms4_res], writes=[rstd4_res])
            for sub in range(4):
                for half in range(2):
                    t, tr = gettmp()
                    hs = slice(half * 512, (half + 1) * 512)
                    S.op("dve", (lambda t, sub, half, hs: lambda e: e.scalar_tensor_tensor(
                        out=t[:, 0:512], in0=ps[:, 2 * sub + half, :], scalar=rstd4[:, sub:sub + 1], in1=pgb[l][:, hs],
                        op0=ALU.mult, op1=ALU.mult))(t, sub, half, hs),
                        reads=[ps_res[2 * sub + half], rstd4_res, pgb_res[l]], writes=[tr])
                    S.op("pool", (lambda t, sub, hs: lambda e: e.tensor_tensor(out=xt[:, sub, hs], in0=xt[:, sub, hs], in1=t[:, 0:512], op=ALU.add))(t, sub, hs),
                         reads=[tr, xt_res[sub]], writes=[xt_res[sub]])
                if after_sub:
                    after_sub.pop(0)()
            while after_sub:
                after_sub.pop(0)()
            if last_layer:
                S.dma(out_d[tile * T:(tile + 1) * T, :].rearrange("(s p) d -> p s d", p=128), xt[:, :, :], reads=xt_res, writes=[out_res])

        hTa = cvs[:, 0:4, :].rearrange("p a n -> p (a n)").bitcast(BF16).rearrange("p (k t) -> p k t", k=8)
        pre_done = {}

        def load_x(tile):
            S.dma(xts[tile % 2][:, :, :], x_d[tile * T:(tile + 1) * T, :].rearrange("(s p) d -> p s d", p=128), writes=xtrs[tile % 2])

        def do_tile(tile):
            xt = xts[tile % 2]; xt_res = xtrs[tile % 2]
            real = not mode["dry"]
            if tile == 0:
                load_x(0)
            if pre_done.get(tile):
                H["ap"] = hTa; H["res"] = [list(cvs_res[0:4]) for _ in range(4)]
            else:
                H["ap"] = hT; H["res"] = [[r_] for r_ in hT_res]
                rms_pre(0, 4, gb[0][:], [gb_res[0]], xt, xt_res)
            vab = [take(bid) for bid in L0["VA"]]
            vbb = [take(bid) for bid in L0["VB"]]
            for sub in range(4):
                ts = slice(sub * 128, (sub + 1) * 128)
                b2 = alloc(2)
                def fv(e, ts=ts, b2=b2, hcur=H["ap"]):
                    for kc in range(8):
                        e.matmul(out=ps[:, b2, :], lhsT=hcur[:, kc, ts], rhs=vab[kc // 2][0][:, (kc % 2) * 512:(kc % 2) * 512 + 512],
                                 start=(kc == 0), stop=(kc == 7))
                        ins = e.matmul(out=ps[:, b2 + 1, 0:256], lhsT=hcur[:, kc, ts], rhs=vbb[kc // 4][0][:, (kc % 4) * 256:(kc % 4) * 256 + 256],
                                       start=(kc == 0), stop=(kc == 7))
                    return ins
                S.op("pe", fv, reads=[x_[1] for x_ in vab + vbb] + list(H["res"][sub]), writes=[ps_res[b2], ps_res[b2 + 1]])
                def fbn(e, sub=sub, b2=b2):
                    e.bn_stats(out=st6[sub][:, 0, :], in_=ps[:, b2, 0:256])
                    e.bn_stats(out=st6[sub][:, 1, :], in_=ps[:, b2, 256:512])
                    return e.bn_stats(out=st6[sub][:, 2, :], in_=ps[:, b2 + 1, 0:256])
                S.op("dve", fbn, reads=[ps_res[b2], ps_res[b2 + 1]], writes=[st6_res[sub]])
                m = mv[sub]; mr = mv_res[sub]
                S.op("dve", (lambda sub, m: lambda e: e.bn_aggr(out=m[:, 0:2], in_=st6[sub][:]))(sub, m), reads=[st6_res[sub]], writes=[mr])
                S.op("act", (lambda m: lambda e: e.activation(out=m[:, 2:3], in_=m[:, 1:2], func=AF.Sqrt, bias=EPS))(m), reads=[mr], writes=[mr])
                S.op("dve", (lambda m: lambda e: e.reciprocal(out=m[:, 3:4], in_=m[:, 2:3]))(m), reads=[mr], writes=[mr])
                S.op("dve", (lambda m: lambda e: e.tensor_scalar(out=m[:, 2:3], in0=m[:, 0:1], scalar1=m[:, 3:4], scalar2=-1.0,
                                                                 op0=ALU.mult, op1=ALU.mult))(m), reads=[mr], writes=[mr])
                vt, vr = vn[sub % 2], vn_res[sub % 2]
                def fva(e, m=m, vt=vt, b2=b2):
                    e.activation(out=vt[:, 0:512], in_=ps[:, b2, :], func=AF.Identity, bias=m[:, 2:3], scale=m[:, 3:4])
                    return e.activation(out=vt[:, 512:768], in_=ps[:, b2 + 1, 0:256], func=AF.Identity, bias=m[:, 2:3], scale=m[:, 3:4])
                S.op("act", fva, reads=[ps_res[b2], ps_res[b2 + 1], mr], writes=[vr])
                S.op("pool", (lambda vt: lambda e: e.tensor_tensor(out=vt[:], in0=vt[:], in1=lngb[:], op=ALU.mult))(vt),
                     reads=[vr, lnp_res], writes=[vr])
                S.op("pool", (lambda vt, sub: lambda e: e.tensor_tensor(out=vln[:, sub, :], in0=vt[:], in1=lnbb[:], op=ALU.add))(vt, sub),
                     reads=[vr, lnp_res], writes=[vln_res[sub]])
            done()
            att_proj(0, L0)

            def a_chunk(c):
                bu = proj(L0["U"][c])
                bg = proj(L0["G"][c])
                done()
                bs = alloc()
                segs = _head_segs(c)
                def fsg(e):
                    for sub in range(4):
                        ts = slice(sub * 128, (sub + 1) * 128)
                        for (p0, p1, h) in segs:
                            e.matmul(out=ps[p0:p1, bs, ts], lhsT=vln[:, sub, c * 128 + p0:c * 128 + p1], rhs=wmT[:, h, :],
                                     start=True, stop=False)
                            ins = e.matmul(out=ps[p0:p1, bs, ts], lhsT=ones1[0:1, 0:p1 - p0], rhs=bsrow[0:1, h * 128:(h + 1) * 128],
                                           start=False, stop=True)
                    return ins
                S.op("pe", fsg, reads=vln_res + [wmT_res, ones1_res, bsrow_res], writes=[ps_res[bs]])
                sgt, sgr = silu_to_tmp(bg)
                t, tr = gettmp()
                S.op("dve", lambda e: e.tensor_tensor(out=t[:, 0:T], in0=ps[:, bu, :], in1=sgt[:, 0:T], op=ALU.mult),
                     reads=[ps_res[bu], sgr], writes=[tr])
                S.op("dve", lambda e: e.tensor_tensor(out=yg[:, c, :], in0=ps[:, bs, :], in1=t[:, 0:T], op=ALU.mult),
                     reads=[ps_res[bs], tr], writes=[yg_res[c]])

            def b_chunk(c):
                bcg = proj(L0["CG"][c]); bxi = proj(L0["XI"][c]); bbg = proj(L0["BG"][c]); bg = proj(L0["G"][6 + c])
                done()
                xi, xir = gettmp()
                S.op("act", lambda e: e.activation(out=xi[:, 0:T], in_=ps[:, bxi, :], func=AF.Copy), reads=[ps_res[bxi]], writes=[xir])
                sgt, sgr = silu_to_tmp(bg)
                pr, prr = gettmp()
                S.op("pool", lambda e: e.tensor_copy(out=pr[:, 0:2], in_=pbh[:, c, :]), reads=[pbh_res[c]], writes=[prr])
                S.op("dve", lambda e: e.tensor_tensor(out=pr[:, 2:2 + T], in0=ps[:, bcg, :], in1=xi[:, 0:T], op=ALU.mult),
                     reads=[ps_res[bcg], xir, prr], writes=[prr])
                S.op("pool", lambda e: e.tensor_copy(out=pbh[:, c, :], in_=pr[:, T:T + 2]), reads=[prr], writes=[pbh_res[c]])
                acc, accr = gettmp()
                S.op("dve", lambda e: e.tensor_scalar(out=acc[:, 0:T], in0=pr[:, 0:T], scalar1=chp[:, R_BC + 0, c:c + 1],
                                                      scalar2=None, op0=ALU.mult), reads=[prr, chp_res], writes=[accr])
                for k in (1, 2):
                    S.op("dve", (lambda k: lambda e: e.scalar_tensor_tensor(
                        out=acc[:, 0:T], in0=pr[:, k:k + T], scalar=chp[:, R_BC + k, c:c + 1], in1=acc[:, 0:T],
                        op0=ALU.mult, op1=ALU.add))(k), reads=[prr, chp_res, accr], writes=[accr])
                S.op("dve", lambda e: e.tensor_tensor(out=acc[:, 0:T], in0=ps[:, bbg, :], in1=acc[:, 0:T], op=ALU.mult),
                     reads=[ps_res[bbg], accr], writes=[accr])
                S.op("pool", lambda e: e.tensor_tensor(out=yg[:, 6 + c, :], in0=acc[:, 0:T], in1=sgt[:, 0:T], op=ALU.mult),
                     reads=[accr, sgr], writes=[yg_res[6 + c]])

            mains = [(lambda c=c: a_chunk(c)) for c in range(6)] + [(lambda c=c: b_chunk(c)) for c in range(6)]
            interleave(mains, att_stages(0), list(range(12)))
            pre1 = ()
            if tile > 0 and nlayers > 1:
                pre1 = rms_pre(1, 4, gb[1][:], [gb_res[1]], xt, xt_res, as_waves=True)
            w_out_phase(0, L0, nlayers == 1, tile, xt, xt_res, after_sub=pre1)
            if tile == 0 and nlayers > 1:
                if real:
                    stage_pool["slots"] = stage_pool["l1"]
                kv_prologue(1, L1, 3, xts[1], xtrs[1])
            if tile + 1 < nt and (tile > 0 or nlayers == 1):
                if real:
                    flush_stores(0)
                load_x(tile + 1)
            if nlayers == 1:
                return
            H["ap"] = hT; H["res"] = [[r_] for r_ in hT_res]
            if tile == 0:
                rms_pre(1, 4, gb[1][:], [gb_res[1]], xt, xt_res)
            for c in range(6):
                bz = proj(L1["ZC"][c])
                z, zr = gettmp()
                S.op("pool", (lambda z, c: lambda e: e.tensor_copy(out=z[:, 0:16], in_=zbh[:, c, :]))(z, c), reads=[zbh_res[c]], writes=[zr])
                S.op("act", (lambda z, bz: lambda e: e.activation(out=z[:, 16:528], in_=ps[:, bz, :], func=AF.Copy))(z, bz),
                     reads=[ps_res[bz], zr], writes=[zr])
                S.op("pool", (lambda z, c: lambda e: e.tensor_copy(out=zbh[:, c, :], in_=z[:, 512:528]))(z, c), reads=[zr], writes=[zbh_res[c]])
                wins = sorted({POOLW[(c * 128 + p) // 192] for p in (0, 127)})
                sA, sAr = gettmp(); sB, sBr = gettmp()
                have = {}
                S.op("dve", (lambda sA, z: lambda e: e.tensor_tensor(out=sA[:, 1:528], in0=z[:, 1:528], in1=z[:, 0:527], op=ALU.add))(sA, z),
                     reads=[zr], writes=[sAr])
                have[2] = (sA, sAr)
                if max(wins) >= 4:
                    S.op("dve", (lambda sA, sB: lambda e: e.tensor_tensor(out=sB[:, 3:528], in0=sA[:, 3:528], in1=sA[:, 1:526], op=ALU.add))(sA, sB),
                         reads=[sAr], writes=[sBr])
                    have[4] = (sB, sBr)
                if max(wins) >= 8:
                    S.op("dve", (lambda sA, sB: lambda e: e.tensor_tensor(out=sA[:, 7:528], in0=sB[:, 7:528], in1=sB[:, 3:524], op=ALU.add))(sA, sB),
                         reads=[sBr, sAr], writes=[sAr])
                    have[8] = (sA, sAr)
                if max(wins) >= 16:
                    S.op("dve", (lambda sA, sB: lambda e: e.tensor_tensor(out=sB[:, 15:528], in0=sA[:, 15:528], in1=sA[:, 7:520], op=ALU.add))(sA, sB),
                         reads=[sAr, sBr], writes=[sBr])
                    have[16] = (sB, sBr)
                for (p0, p1) in ((0, 64), (64, 128)):
                    win = POOLW[(c * 128 + p0) // 192]
                    sw, swr = have[win]
                    if tile == 0:
                        S.op("pool", (lambda sw, p0, p1, c: lambda e: e.tensor_tensor(
                            out=sw[p0:p1, 16:32], in0=sw[p0:p1, 16:32], in1=cst[p0:p1, 256 + c * 16:256 + c * 16 + 16], op=ALU.mult))(sw, p0, p1, c),
                            reads=[swr, cst_res], writes=[swr])
                    S.op("dve", (lambda sw, z, p0, p1, c, win: lambda e: e.scalar_tensor_tensor(
                        out=pooled[p0:p1, c, :], in0=sw[p0:p1, 16:528], scalar=1.0 / win, in1=z[p0:p1, 16:528],
                        op0=ALU.mult, op1=ALU.subtract))(sw, z, p0, p1, c, win), reads=[swr, zr], writes=[pooled_res[c]])
            done()
            att_proj(1, L1)

            def bd_chunk(cp):
                blk, br = take(L1["BD"][cp])
                b = alloc()
                kcs = bd_kcs(cp)
                def fbd(e):
                    for i, kc in enumerate(kcs):
                        ins = e.matmul(out=ps[:, b, :], lhsT=blk[:, kc * 128:(kc + 1) * 128], rhs=pooled[:, kc, :],
                                       start=(i == 0), stop=(i == len(kcs) - 1))
                    return ins
                S.op("pe", fbd, reads=[br] + [pooled_res[kc] for kc in kcs], writes=[ps_res[b]])
                bg = proj(L1["G"][cp])
                done()
                sgt, sgr = silu_to_tmp(bg)
                S.op("dve", lambda e: e.scalar_tensor_tensor(
                    out=yg[:, cp, :], in0=ps[:, b, :], scalar=chp[:, R_CS, cp:cp + 1], in1=sgt[:, 0:T], op0=ALU.mult, op1=ALU.mult),
                    reads=[ps_res[b], chp_res, sgr], writes=[yg_res[cp]])

            dst = {}

            def d_proj(c):
                bga = proj(L1["GA"][c]); bgb = proj(L1["GB"][c])
                done()
                th, thr = gettmp()
                S.op("act", lambda e: e.activation(out=th[:, 0:T], in_=ps[:, bgb, :], func=AF.Tanh, scale=0.5), reads=[ps_res[bgb]], writes=[thr])
                S.op("pool", lambda e: e.tensor_copy(out=zg[:, c, 0:30], in_=zg[:, c, T:T + 30]), reads=[zg_res[c]], writes=[zg_res[c]])
                S.op("dve", lambda e: e.scalar_tensor_tensor(
                    out=zg[:, c, 30:30 + T], in0=th[:, 0:T], scalar=1.0, in1=ps[:, bga, :], op0=ALU.add, op1=ALU.mult),
                    reads=[thr, ps_res[bga], zg_res[c]], writes=[zg_res[c]])

            def d_conv(c):
                dgb = [take(bid) for bid in L1["DG"][c]]
                bc = alloc()
                def fcv(e):
                    for k in range(31):
                        ins = e.matmul(out=ps[:, bc, :], lhsT=dgb[k // 8][0][:, (k % 8) * 128:(k % 8 + 1) * 128], rhs=zg[:, c, k:k + T],
                                       start=(k == 0), stop=(k == 30))
                    return ins
                S.op("pe", fcv, reads=[x_[1] for x_ in dgb] + [zg_res[c]], writes=[ps_res[bc]])
                done()
                S.op("act", lambda e: e.activation(out=cvs[:, c, :], in_=ps[:, bc, :], func=AF.Identity, bias=chp[:, R_DWB, c:c + 1]),
                     reads=[ps_res[bc], chp_res], writes=[cvs_res[c]])
                sq, sqr = gettmp()
                S.op("pool", lambda e: e.tensor_tensor(out=sq[:, 0:T], in0=cvs[:, c, :], in1=cvs[:, c, :], op=ALU.mult),
                     reads=[cvs_res[c]], writes=[sqr])
                dst[c] = (sq, sqr)

            def d_stat(c):
                if c == 0:
                    dst["bsum"] = alloc(); reserved.add(dst["bsum"])
                    dst["bsq"] = alloc(); reserved.add(dst["bsq"])
                bsum, bsq = dst["bsum"], dst["bsq"]
                sq, sqr = dst[c]
                def fst(e):
                    e.matmul(out=ps[:, bsum, :], lhsT=onesf[:], rhs=cvs[:, c, :], start=(c == 0), stop=(c == 5))
                    return e.matmul(out=ps[:, bsq, :], lhsT=onesf[:], rhs=sq[:, 0:T], start=(c == 0), stop=(c == 5))
                S.op("pe", fst, reads=[onesf_res, cvs_res[c], sqr], writes=[ps_res[bsum], ps_res[bsq]])
                if c == 5:
                    d_f0()

            def d_f0():
                bsum, bsq = dst["bsum"], dst["bsq"]
                mean, meanr = gettmp(); msq, msqr = gettmp()
                dst["mean"] = (mean, meanr); dst["msq"] = (msq, msqr)
                S.op("dve", lambda e: e.tensor_scalar(out=mean[:, 0:T], in0=ps[:, bsum, :], scalar1=1.0 / BW, scalar2=None, op0=ALU.mult),
                     reads=[ps_res[bsum]], writes=[meanr])
                S.op("dve", lambda e: e.tensor_tensor(out=msq[:, 0:T], in0=mean[:, 0:T], in1=mean[:, 0:T], op=ALU.mult), reads=[meanr], writes=[msqr])
                S.op("dve", lambda e: e.scalar_tensor_tensor(out=msq[:, 0:T], in0=ps[:, bsq, :], scalar=1.0 / BW, in1=msq[:, 0:T],
                                                             op0=ALU.mult, op1=ALU.subtract), reads=[ps_res[bsq], msqr], writes=[msqr])
                reserved.discard(bsum); reserved.discard(bsq)

            def d_f1():
                msq, msqr = dst["msq"]
                S.op("act", lambda e: e.activation(out=msq[:, 0:T], in_=msq[:, 0:T], func=AF.Sqrt, bias=EPS), reads=[msqr], writes=[msqr])

            def d_f2():
                msq, msqr = dst["msq"]; mean, meanr = dst["mean"]
                S.op("dve", lambda e: e.reciprocal(out=stA[:], in_=msq[:, 0:T]), reads=[msqr], writes=[stA_res])
                S.op("dve", lambda e: e.scalar_tensor_tensor(out=stB[:], in0=mean[:, 0:T], scalar=-1.0, in1=stA[:], op0=ALU.mult, op1=ALU.mult),
                     reads=[meanr, stA_res], writes=[stB_res])

            def d_fn(c):
                t, tr = gettmp()
                S.op("dve", lambda e: e.tensor_tensor(out=t[:, 0:T], in0=cvs[:, c, :], in1=stA[:], op=ALU.mult),
                     reads=[cvs_res[c], stA_res], writes=[tr])
                S.op("pool" if c % 2 == 0 else "dve", lambda e: e.tensor_tensor(out=t[:, 0:T], in0=t[:, 0:T], in1=stB[:], op=ALU.add),
                     reads=[tr, stB_res], writes=[tr])
                S.op("act", lambda e: e.activation(out=zs[:, c, :], in_=t[:, 0:T], func=AF.Silu,
                                                   bias=chp[:, R_LNB, c:c + 1], scale=chp[:, R_LNG, c:c + 1]),
                     reads=[tr, chp_res], writes=[zs_res[c]])

            def pw_chunk(cp):
                blk, br = take(L1["PW"][cp])
                b = alloc()
                def fpw(e):
                    for kc in range(6):
                        ins = e.matmul(out=ps[:, b, :], lhsT=blk[:, kc * 128:(kc + 1) * 128], rhs=zs[:, kc, :], start=(kc == 0), stop=(kc == 5))
                    return ins
                S.op("pe", fpw, reads=[br] + zs_res, writes=[ps_res[b]])
                bg = proj(L1["G"][6 + cp])
                done()
                sgt, sgr = silu_to_tmp(bg)
                S.op("dve", lambda e: e.scalar_tensor_tensor(
                    out=yg[:, 6 + cp, :], in0=ps[:, b, :], scalar=chp[:, R_PWB, cp:cp + 1], in1=sgt[:, 0:T], op0=ALU.add, op1=ALU.mult),
                    reads=[ps_res[b], chp_res, sgr], writes=[yg_res[6 + cp]])

            P = lambda c: (lambda: d_proj(c))
            Cv = lambda c: (lambda: d_conv(c))
            St = lambda c: (lambda: d_stat(c))
            mains = [P(0), P(1), Cv(0), P(2), Cv(1), St(0), P(3), Cv(2), St(1), P(4), Cv(3), St(2), P(5), Cv(4), St(3), Cv(5), St(4), St(5)]
            bd = [(lambda cp=cp: bd_chunk(cp)) for cp in range(6)]
            fn = [(lambda c=c: d_fn(c)) for c in range(6)]
            mains += [d_f1, bd[0], d_f2, bd[1], fn[0], fn[1], bd[2], fn[2], fn[3], bd[3], fn[4], fn[5], bd[4], bd[5]]
            mains += [(lambda cp=cp: pw_chunk(cp)) for cp in range(6)]
            slots = [1, 3, 5, 7, 9, 11, 13, 15, 17, 18, 20, 22]
            if 0 < tile and tile + 1 < nt:
                nw = rms_pre(0, 4, gb[0][:], [gb_res[0]], xts[(tile + 1) % 2], xtrs[(tile + 1) % 2], as_waves=True, hidden=True,
                             dst=(hTa, [list(cvs_res[0:4]) for _ in range(4)]))
                for i_, anchor in enumerate((bd[1], bd[2], bd[3])):
                    mains.insert(mains.index(anchor) + 1, nw[i_])
                p0 = mains.index(bd[5])
                for i_, w_ in enumerate(nw[3:]):
                    mains.insert(min(p0 + 1 + 2 * i_, len(mains)), w_)
                pre_done[tile + 1] = True
            interleave(mains, att_stages(1), slots)
            nxt = ()
            if tile + 1 < nt:
                nxt = ()
            w_out_phase(1, L1, True, tile, xt, xt_res, extra=nxt)
            if tile == 0 and nt > 1:
                if real:
                    flush_stores(0)
                load_x(1)

        class _Null:
            def op(self, *a, **k): pass
            def dma(self, *a, **k): pass
        S_real = S
        S = _Null()
        mode["dry"] = True
        del order[:]
        kv_prologue(0, L0, 2, xts[0], xtrs[0])
        n0 = len(order)
        do_tile(0)
        n1 = len(order)
        if nt > 1:
            do_tile(1)
            per = order[n1:]
            for _ in range(nt - 2):
                order.extend(per)
        mode["dry"] = False
        S = S_real
        bank_ptr[0] = 0; tmp_i[0] = 0; reserved.clear(); pre_done.clear()
        stage_pool["l1"] = [(xts[1][:, k, :], [xtrs[1][k]]) for k in (2, 3, 0, 1)]
        stage_pool["slots"] = stage_pool["l1"] + [
            (cvs[:, 2 * j:2 * j + 2, :].rearrange("p a n -> p (a n)"), [cvs_res[2 * j], cvs_res[2 * j + 1]]) for j in range(3)]
        ring_fill()

        kv_prologue(0, L0, 2, xts[0], xtrs[0])
        for tile in range(nt):
            do_tile(tile)
        flush_stores(0)
        assert rstate["taken"] == len(order), (rstate, len(order))
        S.emit()
    print("sbuf bytes remaining", nc.sbuf_bytes_remaining, flush=True)
    return nc


_NC_CACHE = {}


def kernel(**inputs):
    nt = SEQ // T
    if "nc" not in _NC_CACHE:
        _NC_CACHE["nc"] = build(nt, 2)
    nc = _NC_CACHE["nc"]
    consts = make_consts()
    x = np.ascontiguousarray(inputs["x"], dtype=np.float32)
    mem = np.ascontiguousarray(inputs["mem"], dtype=np.float32)
    shared = {k: np.ascontiguousarray(v, dtype=np.float32) for k, v in inputs.items() if k not in ("x", "mem")}
    shared["consts"] = consts
    in_maps = []
    for b in range(8):
        m = dict(shared)
        m["x"] = x[b]
        m["mem"] = mem[b]
        in_maps.append(m)
    res = run_bass_kernel_spmd(nc, in_maps, core_ids=list(range(8)))
    return np.stack([np.asarray(r["out"], dtype=np.float32) for r in res.results], axis=0)
```
